# Optimizing a Trainium2 kernel written in Bass

```python
import functools
import jax, jax.numpy as jnp
from jax import lax
import numpy as np

D_MODEL = 1024
BATCH = 4
SEQ = 4096
DEPTH = 2
DEC_BATCH = 32
DEC_SEQ = 1
PAST_LEN = 8192
PAGE_SIZE = 128

HEAD_DIM = 64
A_W = D_MODEL // 4
B_W = D_MODEL // 4
C_W = D_MODEL // 2
H_A = A_W // HEAD_DIM
H_C = C_W // HEAD_DIM
IN_W = 2 * A_W + 2 * B_W + 3 * C_W + H_C
CHUNK = 128
CONV_W = 31
FFN_CONV_W = 3
D_FF = ((8 * D_MODEL // 3 + 127) // 128) * 128
Q_BLOCK = 128
SCALE = HEAD_DIM ** -0.5
EPS = 1e-6
FORGET_BIAS = 4.0

kernel_name = 'hybrid_gmlp_conformer_fox_decoder_step'


def _rms(x):
    x32 = x.astype(jnp.float32)
    y = x32 * lax.rsqrt(jnp.mean(x32 * x32, axis=-1, keepdims=True) + EPS)
    return y.astype(x.dtype)


def rmsnorm(x, g):
    return _rms(x) * g


def layernorm(x, g, b):
    x32 = x.astype(jnp.float32)
    xc = x32 - jnp.mean(x32, axis=-1, keepdims=True)
    y = xc * lax.rsqrt(jnp.mean(xc * xc, axis=-1, keepdims=True) + EPS)
    return y.astype(x.dtype) * g + b


def causal_dwconv(x, buf, w, b):
    width = w.shape[0]
    xp = jnp.concatenate([buf.astype(x.dtype), x], axis=1)
    y = lax.conv_general_dilated(xp, w[:, None, :].astype(x.dtype), window_strides=(1,), padding='VALID',
                                 dimension_numbers=('NWC', 'WIO', 'NWC'), feature_group_count=x.shape[-1])
    return y + b, xp[:, xp.shape[1] - (width - 1):]


def chunk_spatial_gating(u, v, ln_g, ln_b, w_s, b_s):
    nb, t, _ = v.shape
    v = layernorm(v, ln_g, ln_b)
    n_chunks = -(-t // CHUNK)
    vp = jnp.pad(v, ((0, 0), (0, n_chunks * CHUNK - t), (0, 0))).reshape(nb, n_chunks, CHUNK, H_A, HEAD_DIM)
    causal = jnp.tril(jnp.ones((CHUNK, CHUNK), dtype=bool))
    w_masked = jnp.where(causal[None], w_s, 0.0)
    sv = jnp.einsum('hts,bnshd->bnthd', w_masked, vp) + b_s.T[:, :, None]
    sv = sv.reshape(nb, n_chunks * CHUNK, A_W)[:, :t]
    return u * sv, v


def conformer_conv(a, gate, buf, conv_w, conv_b, ln_g, ln_b):
    glu = a * jax.nn.sigmoid(gate)
    y, new_buf = causal_dwconv(glu, buf, conv_w, conv_b)
    return jax.nn.silu(layernorm(y, ln_g, ln_b)), new_buf


def forget_attention_prompt(q, k, v, logf):
    nb, t, h, dh = q.shape
    n_blocks = t // Q_BLOCK
    cum = jnp.cumsum(logf, axis=1).transpose(0, 2, 1)
    q_blocks = q.reshape(nb, n_blocks, Q_BLOCK, h, dh).transpose(1, 0, 2, 3, 4)
    c_blocks = cum.reshape(nb, h, n_blocks, Q_BLOCK).transpose(2, 0, 1, 3)
    k_pos = jnp.arange(t)

    def block(args):
        i, qi, ci = args
        s = jnp.einsum('bqhd,bkhd->bhqk', qi, k, preferred_element_type=jnp.float32) * SCALE
        s = s + ci[..., None] - cum[:, :, None, :]
        q_pos = i * Q_BLOCK + jnp.arange(Q_BLOCK)
        s = jnp.where(k_pos[None, :] <= q_pos[:, None], s, -jnp.inf)
        p = jax.nn.softmax(s, axis=-1)
        return jnp.einsum('bhqk,bkhd->bqhd', p.astype(v.dtype), v)

    out = lax.map(block, (jnp.arange(n_blocks), q_blocks, c_blocks))
    return out.transpose(1, 0, 2, 3, 4).reshape(nb, t, h, dh)


def forget_attention_sample(q, k, v, logf, past_k, past_v, past_logf):
    p_len = past_k.shape[1]
    t = q.shape[1]
    keys = jnp.concatenate([past_k.astype(k.dtype), k], axis=1)
    vals = jnp.concatenate([past_v.astype(v.dtype), v], axis=1)
    cum = jnp.cumsum(jnp.concatenate([past_logf.astype(jnp.float32), logf], axis=1), axis=1).transpose(0, 2, 1)
    s = jnp.einsum('bqhd,bkhd->bhqk', q, keys, preferred_element_type=jnp.float32) * SCALE
    s = s + cum[:, :, p_len:, None] - cum[:, :, None, :]
    k_pos = jnp.arange(p_len + t)
    q_pos = p_len + jnp.arange(t)
    s = jnp.where(k_pos[None, :] <= q_pos[:, None], s, -jnp.inf)
    p = jax.nn.softmax(s, axis=-1)
    return jnp.einsum('bhqk,bkhd->bqhd', p.astype(vals.dtype), vals)


def trunk_layer(x, c, conv_buf, ffn_buf, attend, norm1_g, norm2_g, w_ada, b_ada, w_in, b_forget,
                a_ln_g, a_ln_b, w_s, b_s, conv_w, conv_b, conv_ln_g, conv_ln_b, mix_g, w_out,
                w_up, ffn_conv_w, ffn_conv_b, w_down):
    nb, t, _ = x.shape
    mod = (jax.nn.silu(c) @ w_ada + b_ada)[:, None, :]
    shift1, scale1, gate1, shift2, scale2, gate2 = jnp.split(mod, 6, axis=-1)
    h = rmsnorm(x, norm1_g) * (1 + scale1) + shift1
    z = h @ w_in
    za = jax.nn.gelu(z[..., :2 * A_W], approximate=False)
    y_a, chunk_v = chunk_spatial_gating(za[..., :A_W], za[..., A_W:], a_ln_g, a_ln_b, w_s, b_s)
    o = 2 * A_W
    y_b, new_conv = conformer_conv(z[..., o:o + B_W], z[..., o + B_W:o + 2 * B_W], conv_buf,
                                   conv_w, conv_b, conv_ln_g, conv_ln_b)
    o += 2 * B_W
    q = z[..., o:o + C_W].reshape(nb, t, H_C, HEAD_DIM)
    k = z[..., o + C_W:o + 2 * C_W].reshape(nb, t, H_C, HEAD_DIM)
    v = z[..., o + 2 * C_W:o + 3 * C_W].reshape(nb, t, H_C, HEAD_DIM)
    logf = jax.nn.log_sigmoid(z[..., o + 3 * C_W:].astype(jnp.float32) + b_forget.astype(jnp.float32))
    y_c = attend(q, k, v, logf).reshape(nb, t, C_W)
    y = jnp.concatenate([_rms(y_a), _rms(y_b), _rms(y_c)], axis=-1) * mix_g
    x = x + gate1 * (y @ w_out)
    h = rmsnorm(x, norm2_g) * (1 + scale2) + shift2
    up, new_ffn = causal_dwconv(h @ w_up, ffn_buf, ffn_conv_w, ffn_conv_b)
    x = x + gate2 * ((jax.nn.silu(up[..., :D_FF]) * up[..., D_FF:]) @ w_down)
    return x, (k, v, logf, new_conv, new_ffn, chunk_v)


def setup_inputs(seed: int = 0) -> dict:
    key = jax.random.key(seed)
    ks = iter(jax.random.split(key, 40))

    def nrm(shape, s):
        return s * jax.random.normal(next(ks), shape, jnp.float32)

    n_pages = PAST_LEN // PAGE_SIZE
    n_phys = (DEC_BATCH * n_pages * 5) // 4
    d_in = D_MODEL ** -0.5
    x_prompt = nrm((BATCH, SEQ, D_MODEL), 1.0)
    x_sample = nrm((DEC_BATCH, DEC_SEQ, D_MODEL), 1.0)
    cache_k = nrm((DEPTH, n_phys, PAGE_SIZE, H_C, HEAD_DIM), 1.0)
    cache_v = nrm((DEPTH, n_phys, PAGE_SIZE, H_C, HEAD_DIM), 1.0)
    cache_logf = jax.nn.log_sigmoid(FORGET_BIAS + nrm((DEPTH, n_phys, PAGE_SIZE, H_C), 1.0))
    state_conv = nrm((DEPTH, DEC_BATCH, CONV_W - 1, B_W), 0.5)
    state_ffn_conv = nrm((DEPTH, DEC_BATCH, FFN_CONV_W - 1, 2 * D_FF), 1.0)
    page_table = jax.random.permutation(next(ks), n_phys)[:DEC_BATCH * n_pages].reshape(DEC_BATCH, n_pages).astype(jnp.int32)
    c_prompt = nrm((BATCH, D_MODEL), 1.0)
    c_sample = nrm((DEC_BATCH, D_MODEL), 1.0)
    return {
        'x_prompt': x_prompt, 'x_sample': x_sample,
        'cache_k': cache_k, 'cache_v': cache_v, 'cache_logf': cache_logf,
        'state_conv': state_conv, 'state_ffn_conv': state_ffn_conv,
        'page_table': page_table, 'c_prompt': c_prompt, 'c_sample': c_sample,
        'norm1_g': 1.0 + nrm((DEPTH, D_MODEL), 0.05),
        'norm2_g': 1.0 + nrm((DEPTH, D_MODEL), 0.05),
        'w_ada': nrm((DEPTH, D_MODEL, 6 * D_MODEL), 0.5 * d_in),
        'b_ada': nrm((DEPTH, 6 * D_MODEL), 0.02),
        'w_in': nrm((DEPTH, D_MODEL, IN_W), d_in),
        'b_forget': FORGET_BIAS + nrm((DEPTH, H_C), 0.5),
        'a_ln_g': 1.0 + nrm((DEPTH, A_W), 0.05),
        'a_ln_b': nrm((DEPTH, A_W), 0.02),
        'w_s': nrm((DEPTH, H_A, CHUNK, CHUNK), CHUNK ** -0.5),
        'b_s': 1.0 + nrm((DEPTH, H_A, CHUNK), 0.1),
        'conv_w': nrm((DEPTH, CONV_W, B_W), CONV_W ** -0.5),
        'conv_b': nrm((DEPTH, B_W), 0.02),
        'conv_ln_g': 1.0 + nrm((DEPTH, B_W), 0.05),
        'conv_ln_b': nrm((DEPTH, B_W), 0.02),
        'mix_g': 1.0 + nrm((DEPTH, D_MODEL), 0.05),
        'w_out': nrm((DEPTH, D_MODEL, D_MODEL), d_in),
        'w_up': nrm((DEPTH, D_MODEL, 2 * D_FF), d_in),
        'ffn_conv_w': nrm((DEPTH, FFN_CONV_W, 2 * D_FF), FFN_CONV_W ** -0.5),
        'ffn_conv_b': nrm((DEPTH, 2 * D_FF), 0.02),
        'w_down': nrm((DEPTH, D_FF, D_MODEL), D_FF ** -0.5),
        'final_g': 1.0 + nrm((D_MODEL,), 0.05),
    }


def reference(x_prompt, x_sample, cache_k, cache_v, cache_logf, state_conv, state_ffn_conv, page_table,
              c_prompt, c_sample, norm1_g, norm2_g, w_ada, b_ada, w_in, b_forget, a_ln_g, a_ln_b, w_s, b_s,
              conv_w, conv_b, conv_ln_g, conv_ln_b, mix_g, w_out, w_up, ffn_conv_w, ffn_conv_b, w_down, final_g):
    layer_params = (norm1_g, norm2_g, w_ada, b_ada, w_in, b_forget, a_ln_g, a_ln_b, w_s, b_s,
                    conv_w, conv_b, conv_ln_g, conv_ln_b, mix_g, w_out, w_up, ffn_conv_w, ffn_conv_b, w_down)
    n_seq, n_pages = page_table.shape
    past_len = n_pages * PAGE_SIZE
    xp, xs = x_prompt, x_sample
    pk, pv, pf, pc, pfc = [], [], [], [], []
    sk, sv, sf, sc, sfc, sa = [], [], [], [], [], []
    for l in range(DEPTH):
        lp = [p[l] for p in layer_params]
        conv0 = jnp.zeros((xp.shape[0], CONV_W - 1, B_W), xp.dtype)
        ffn0 = jnp.zeros((xp.shape[0], FFN_CONV_W - 1, 2 * D_FF), xp.dtype)
        xp, (k, v, f, cb, fb, _) = trunk_layer(xp, c_prompt, conv0, ffn0, forget_attention_prompt, *lp)
        pk.append(k); pv.append(v); pf.append(f); pc.append(cb); pfc.append(fb)
        past_k = jnp.take(cache_k[l], page_table, axis=0).reshape(n_seq, past_len, H_C, HEAD_DIM)
        past_v = jnp.take(cache_v[l], page_table, axis=0).reshape(n_seq, past_len, H_C, HEAD_DIM)
        past_f = jnp.take(cache_logf[l], page_table, axis=0).reshape(n_seq, past_len, H_C)
        attend = functools.partial(forget_attention_sample, past_k=past_k, past_v=past_v, past_logf=past_f)
        xs, (k, v, f, cb, fb, av) = trunk_layer(xs, c_sample, state_conv[l], state_ffn_conv[l], attend, *lp)
        sk.append(k); sv.append(v); sf.append(f); sc.append(cb); sfc.append(fb); sa.append(av)
    y_prompt = rmsnorm(xp, final_g)
    y_sample = rmsnorm(xs, final_g)
    return (y_prompt, y_sample, jnp.stack(pk), jnp.stack(pv), jnp.stack(pf), jnp.stack(pc), jnp.stack(pfc),
            jnp.stack(sk), jnp.stack(sv), jnp.stack(sf), jnp.stack(sc), jnp.stack(sfc), jnp.stack(sa))
```

```python
import numpy as np
import concourse.bass as bass
import concourse.mybir as mybir
from concourse.bass_utils import run_bass_kernel_spmd

F32 = mybir.dt.float32
BF16 = mybir.dt.bfloat16
I32 = mybir.dt.int32
ALU = mybir.AluOpType
AF = mybir.ActivationFunctionType

D = 1024
NL = 2
T = 4096
NS = 4
NPG = 64
DFF = 2816
INW = 2568
EPS = 1e-6
NCORES = 8
NPHYS = 2560
STOP_AT = None


class _Stop(Exception):
    pass


class Buf:
    def __init__(self, name, excl=False):
        self.name = name
        self.w = None
        self.r = {}
        self.excl = excl


class FW:
    def __init__(self, nc):
        self.nc = nc
        self.eng = {'pe': nc.tensor, 'act': nc.scalar, 'dve': nc.vector, 'pool': nc.gpsimd, 'sp': nc.sync}
        self.sems, self.cnt = {}, {}
        self.seen = {k: {} for k in self.eng}
        self._cms = []
        for k in ('pe', 'act', 'dve', 'pool'):
            self._mksem(k)

    def _mksem(self, key):
        cm = self.nc.semaphore("s%d" % len(self.sems))
        self.sems[key] = cm.__enter__()
        self._cms.append(cm)
        self.cnt[key] = 0

    def close(self):
        for cm in reversed(self._cms):
            cm.__exit__(None, None, None)

    def _wait(self, e, key, val):
        if val <= 0 or (e == 'pe' and key == 'pe'):
            return
        if self.seen[e].get(key, 0) >= val:
            return
        self.eng[e].wait_ge(self.sems[key], val)
        self.seen[e][key] = val

    def _deps(self, e, reads, writes, skip=None):
        for b in reads:
            if b.w is not None:
                self._wait(e, *b.w)
            if b.excl:
                for k, v in b.r.items():
                    if k != e:
                        self._wait(e, k, v)
        for b in writes:
            if b.w is not None and b.w[0] != skip:
                self._wait(e, *b.w)
            for k, v in b.r.items():
                self._wait(e, k, v)

    def _mark(self, key, val, reads, writes):
        for b in reads:
            b.r[key] = val
        for b in writes:
            b.w = (key, val)
            b.r = {}

    def op(self, e, fn, reads=(), writes=()):
        self._deps(e, reads, writes)
        ins = fn(self.eng[e])
        self.cnt[e] += 1
        ins.then_inc(self.sems[e], 1)
        self._mark(e, self.cnt[e], reads, writes)

    def dma(self, out, in_, reads=(), writes=(), sem_buf=None, q='sp'):
        b0 = sem_buf if sem_buf is not None else (writes[0] if writes else reads[0])
        key = ('dma', id(b0))
        if key not in self.sems:
            self._mksem(key)
        self._deps(q, reads, writes, skip=key)
        ins = self.eng[q].dma_start(out=out, in_=in_)
        self.cnt[key] += 16
        ins.then_inc(self.sems[key], 16)
        self._mark(key, self.cnt[key], reads, writes)

    def barrier(self):
        for e in self.eng:
            for key in self.sems:
                self._wait(e, key, self.cnt[key])


def build_program():
    nc = bass.Bass("TRN2", target_bir_lowering=False)
    NT = T // 512
    NKB = T // 128

    def din(name, shape, dt=F32):
        return nc.dram_tensor(name, list(shape), dt, kind="ExternalInput").ap()

    def dout(name, shape):
        return nc.dram_tensor(name, list(shape), F32, kind="ExternalOutput").ap()

    xT_d = din("xT", [D, T])
    cT_d = din("cT", [D, 1 + NS])
    w_ada_d = din("w_ada", [NL, D, 6 * D])
    badaT_d = din("badaT", [NL, 128, 48])
    g1r_d = din("g1r", [NL, 128, 8, 1 + NS])
    g2r_d = din("g2r", [NL, 128, 8, 1 + NS])
    mixgT_d = din("mixgT", [NL, 128, 8])
    fgT_d = din("fgT", [128, 8])
    w_in_d = din("w_in", [NL, D, INW])
    w_out_d = din("w_out", [NL, D, D])
    w_up_d = din("w_up", [NL, D, 2 * DFF])
    w_down_d = din("w_down", [NL, DFF, D])
    bfb_d = din("bfb", [NL, 128, 8])
    alng_d = din("alng", [NL, 128, 256])
    alnb_d = din("alnb", [NL, 128, 256])
    wsT_d = din("wsT", [NL, 128, 4, 128])
    bsT_d = din("bsT", [NL, 128, 4])
    cwT_d = din("cwT", [NL, 128, 2, 31])
    cbT_d = din("cbT", [NL, 128, 2])
    clgT_d = din("clgT", [NL, 128, 2])
    clbT_d = din("clbT", [NL, 128, 2])
    fwT_d = din("fwT", [NL, 128, 44, 3])
    fbT_d = din("fbT", [NL, 128, 44])
    tri_d = din("tri", [128, 128])
    ident_d = din("ident", [128, 128])

    xsT0_d = din("xsT0", [128, 8, NS])
    ptb_d = din("ptb", [128, NS * NPG], I32)
    iot_d = din("iot", [128, NS * NPG], I32)
    sel_d = din("sel", [NS, NS, 128])
    bigm_d = din("bigm", [128, 8])
    blkm_d = din("blkm", [8, 512])
    ck_d = din("cache_k", [NL * NPHYS * 128, 512])
    cv_d = din("cache_v", [NL * NPHYS * 128, 512])
    cf_d = din("cache_f", [NL * NPHYS * 128, 8])
    stconv_d = din("stconvT", [NL, 128, 2, NS, 30])
    sffn_d = din("sffnT", [NL, 128, 44, NS, 2])
    alngT_d = din("alngT", [NL, 128, 2])
    alnbT_d = din("alnbT", [NL, 128, 2])
    ws00T_d = din("ws00T", [NL, 128, 2])
    bs0T_d = din("bs0T", [NL, 128, 2])
    ysT_o = dout("ysT", [128, 8, NS])
    ks_o = dout("ks_o", [NL, NS, 512])
    vs_o = dout("vs_o", [NL, NS, 512])
    lfs_o = dout("lfs_o", [NL, NS, 8])
    convs_o = dout("convs", [NL, 128, 2, NS, 30])
    ffns_o = dout("ffns", [NL, 128, 44, NS, 2])
    chv_o = dout("chv", [NL, 128, 2, NS])

    yT_o = dout("yT", [D, T])
    k_o = dout("k_o", [NL, T, 512])
    v_o = dout("v_o", [NL, T, 512])
    lf_o = dout("lf_o", [NL, T, 8])
    convp_o = dout("convp", [NL, 256, 30])
    ffnp_o = dout("ffnp", [NL, 2 * DFF, 2])

    xs_d = nc.dram_tensor("xs_scr", [D, T], F32).ap()

    fw = FW(nc)
    op, dma = fw.op, fw.dma

    def act(out, in_, func, reads, writes, **kw):
        op('act', lambda e: e.activation(out=out, in_=in_, func=func, **kw), reads, writes)

    def mm(out, lhsT, rhs, start, stop, reads, writes):
        op('pe', lambda e: e.matmul(out, lhsT, rhs, start=start, stop=stop), reads, writes)

    def tt(eng, out, a, b, o, reads, writes):
        op(eng, lambda e: e.tensor_tensor(out, a, b, o), reads, writes)

    def ts(eng, out, a, s1, s2, o0, o1, reads, writes):
        if o1 is None:
            op(eng, lambda e: e.tensor_scalar(out, a, s1, None, o0), reads, writes)
        else:
            op(eng, lambda e: e.tensor_scalar(out, a, s1, s2, o0, o1), reads, writes)

    def stt(eng, out, in0, scalar, in1, o0, o1, reads, writes):
        op(eng, lambda e: e.scalar_tensor_tensor(out, in0, scalar, in1, o0, o1), reads, writes)

    def cp(eng, out, in_, reads, writes):
        if eng == 'act':
            act(out, in_, AF.Copy, reads, writes)
        else:
            op(eng, lambda e: e.tensor_copy(out, in_), reads, writes)

    def rsqrt_inplace(tile_ap, buf):
        act(tile_ap, tile_ap, AF.Sqrt, [buf], [buf])
        op('dve', lambda e: e.reciprocal(tile_ap, tile_ap), [buf], [buf])

    from contextlib import ExitStack
    import contextlib
    with ExitStack() as top:
        top.enter_context(contextlib.suppress(_Stop))
        top.enter_context(nc.allow_non_contiguous_dma(reason="small strided parameter loads"))

        uid = [0]

        def sb(stack, name, shape, dt=F32):
            uid[0] += 1
            return stack.enter_context(nc.sbuf_tensor("sb%d_%s" % (uid[0], name), list(shape), dt))

        ps = [top.enter_context(nc.psum_tensor("ps%d" % i, [128, 512], F32)) for i in range(8)]
        bps = [Buf("ps%d" % i, excl=True) for i in range(8)]

        x = sb(top, "x", [128, 8, 512]); bx = Buf("x")
        ones_bf = sb(top, "ones_bf", [128, 128], BF16)
        ones_f = sb(top, "ones_f", [128, 128])
        tri_f = sb(top, "tri_f", [128, 128])
        mask_bf = sb(top, "mask_bf", [128, 128], BF16)
        ident_bf = sb(top, "ident_bf", [128, 128], BF16)
        onecol = sb(top, "onecol", [128, 1])
        bconst = Buf("const")
        silu_c = sb(top, "silu_c", [128, 8, 1 + NS]); bsc = Buf("silu_c")
        mod = sb(top, "mod", [128, 48, 1 + NS]); bmod = Buf("mod")
        a1 = sb(top, "a1", [128, 8, 1 + NS]); a2 = sb(top, "a2", [128, 8, 1 + NS])
        g1r = sb(top, "g1r", [128, 8, 1 + NS]); g2r = sb(top, "g2r", [128, 8, 1 + NS])
        badaT = sb(top, "badaT", [128, 48])
        mixgT = sb(top, "mixgT", [128, 8]); fgT = sb(top, "fgT", [128, 8])
        bfb = sb(top, "bfb", [128, 8])
        alng = sb(top, "alng", [128, 256]); alnb = sb(top, "alnb", [128, 256])
        wsT_bf = sb(top, "wsT_bf", [128, 4, 128], BF16)
        bsT = sb(top, "bsT", [128, 4])
        cwT = sb(top, "cwT", [128, 2, 31]); cbT = sb(top, "cbT", [128, 2])
        clgT = sb(top, "clgT", [128, 2]); clbT = sb(top, "clbT", [128, 2])
        fwT = sb(top, "fwT", [128, 44, 3]); fbT = sb(top, "fbT", [128, 44])
        bpar = Buf("params")
        bstg = [Buf("stg0"), Buf("stg1")]
        bmisc = Buf("misc")
        cum = sb(top, "cum", [128, NKB, 8]); bcum = Buf("cum")
        gtot = sb(top, "gtot", [128, 8]); bgtot = Buf("gtot")
        halo = sb(top, "halo", [128, 44, 2]); bhalo = Buf("halo")

        xs = sb(top, "xs", [128, 8, NS]); bxs = Buf("xs")
        bigm = sb(top, "bigm", [128, 8])
        idx0 = sb(top, "idx0", [128, NS * NPG])
        idxl = sb(top, "idxl", [128, NS * NPG], I32); bidx = Buf("idx")
        alngT = sb(top, "alngT", [128, 2]); alnbT = sb(top, "alnbT", [128, 2])
        ws00T = sb(top, "ws00T", [128, 2]); bs0T = sb(top, "bs0T", [128, 2])
        bpool = [Buf("pq%d" % i) for i in range(5)]
        with ExitStack() as tmpsc:
            ptb = sb(tmpsc, "ptb", [128, NS * NPG], I32)
            iot = sb(tmpsc, "iot", [128, NS * NPG], I32)
            for dst_, src_ in ((xs, xsT0_d), (bigm, bigm_d), (ptb, ptb_d), (iot, iot_d), (tri_f, tri_d),
                               (ones_f, ident_d), (fgT, fgT_d)):
                dma(dst_[:], src_, writes=[bconst], sem_buf=bpar)
            dma(silu_c[:], cT_d.rearrange("(c p) n -> p c n", p=128), writes=[bsc], sem_buf=bpar)
            fw.barrier()
            iof = sb(tmpsc, "iof", [128, NS * NPG])
            cp('dve', idx0[:], ptb[:], [bconst], [bconst])
            cp('dve', iof[:], iot[:], [bconst], [bconst])
            stt('dve', idx0[:], idx0[:], 128.0, iof[:], ALU.mult, ALU.add, [bconst], [bconst])
            op('dve', lambda e: e.tensor_copy(mask_bf[:], tri_f[:]), [bconst], [bconst])
            op('dve', lambda e: e.tensor_copy(ident_bf[:], ones_f[:]), [bconst], [bconst])
            op('pool', lambda e: e.memset(ones_f[:], 1.0), [], [bconst])
            op('pool', lambda e: e.memset(ones_bf[:], 1.0), [], [bconst])
            op('pool', lambda e: e.memset(onecol[:], 1.0), [], [bconst])
            act(silu_c[:], silu_c[:], AF.Silu, [bsc], [bsc])
            fw.barrier()

        def stop(k):
            if STOP_AT == k:
                raise _Stop()

        for l in range(NL):
            last = (l == NL - 1)
            stop(0)
            for dst, src in ((badaT, badaT_d), (g1r, g1r_d), (g2r, g2r_d), (mixgT, mixgT_d), (bfb, bfb_d),
                             (alng, alng_d), (alnb, alnb_d), (bsT, bsT_d), (cwT, cwT_d),
                             (cbT, cbT_d), (clgT, clgT_d), (clbT, clbT_d), (fwT, fwT_d), (fbT, fbT_d),
                             (alngT, alngT_d), (alnbT, alnbT_d), (ws00T, ws00T_d), (bs0T, bs0T_d)):
                dma(dst[:], src[l], writes=[bpar])
            with ExitStack() as tmpsc:
                wsT_f = sb(tmpsc, "wsT_f", [128, 4, 128])
                dma(wsT_f[:], wsT_d[l], writes=[bpar])
                fw.barrier()
                for h in range(4):
                    tt('dve', wsT_bf[:, h, :], wsT_f[:, h, :], tri_f[:], ALU.mult, [bpar, bconst], [bpar])
                fw.barrier()

            with ExitStack() as st:
                stg = [sb(st, "stgA%d" % i, [128, 8, 512]) for i in range(2)]
                wv = w_ada_d[l].rearrange("(c p) n -> p c n", p=128)
                for g in range(12):
                    s_ = g % 2
                    dma(stg[s_][:], wv[:, :, g * 512:(g + 1) * 512], writes=[bstg[s_]])
                    for j in range(4):
                        m = g * 4 + j
                        pb_ = bps[m % 2]
                        pt_ = ps[m % 2]
                        for kc in range(8):
                            mm(pt_[:, 0:1 + NS], stg[s_][:, kc, j * 128:(j + 1) * 128], silu_c[:, kc, :],
                               kc == 0, kc == 7, [bstg[s_], bsc], [pb_])
                        act(mod[:, m, :], pt_[:, 0:1 + NS], AF.Identity, [pb_, bpar], [bmod],
                            bias=badaT[:, m:m + 1], scale=1.0)
                ts('dve', a1[:], mod[:, 8:16, :], 1.0, None, ALU.add, None, [bmod], [bmod])
                tt('dve', a1[:], a1[:], g1r[:], ALU.mult, [bmod, bpar], [bmod])
                ts('dve', a2[:], mod[:, 32:40, :], 1.0, None, ALU.add, None, [bmod], [bmod])
                tt('dve', a2[:], a2[:], g2r[:], ALU.mult, [bmod, bpar], [bmod])
                fw.barrier()
                stop(1)
            shift1 = mod[:, 0:8, :]; gate1 = mod[:, 16:24, :]
            shift2 = mod[:, 24:32, :]; gate2 = mod[:, 40:48, :]

            x_src = xT_d if l == 0 else xs_d
            xsv = xs_d.rearrange("(c p) t -> p c t", p=128)
            x_srcv = x_src.rearrange("(c p) t -> p c t", p=128)

            def rms_rstd(xt, bxt, rstd, brstd, sqs, bsqs, nfeat_chunks, ncols, denom, pst, bpst):
                for c in range(nfeat_chunks):
                    act(sqs[c % 2][:, 0:ncols], xt(c), AF.Square, [bxt], [bsqs[c % 2]])
                    mm(pst[:, 0:ncols], ones_bf[:], sqs[c % 2][:, 0:ncols], c == 0, c == nfeat_chunks - 1,
                       [bconst, bsqs[c % 2]], [bpst])
                ts('dve', rstd, pst[:, 0:ncols], 1.0 / denom, EPS, ALU.mult, ALU.add, [bpst], [brstd])
                rsqrt_inplace(rstd, brstd)

            def idma(out, in_, idx_ap, reads, writes, sem_buf):
                key = ('dma', id(sem_buf))
                if key not in fw.sems:
                    fw._mksem(key)
                fw._deps('pool', reads, writes, skip=key)
                ins = nc.gpsimd.indirect_dma_start(out=out, out_offset=None, in_=in_,
                                                   in_offset=bass.IndirectOffsetOnAxis(ap=idx_ap, axis=0))
                fw.cnt[key] += 16
                ins.then_inc(fw.sems[key], 16)
                fw._mark(key, fw.cnt[key], reads, writes)

            sc1 = slice(1, 1 + NS)

            def chan_stats(chunks, bsrc, w2, bw2, m_t, r_t, bmr, denom, want_mean):
                nch = len(chunks)
                if want_mean:
                    for i_, c_ in enumerate(chunks):
                        mm(ps[6][:, 0:NS], ones_f[:], c_, i_ == 0, i_ == nch - 1, [bconst, bsrc], [bps[6]])
                    ts('dve', m_t, ps[6][:, 0:NS], 1.0 / denom, None, ALU.mult, None, [bps[6]], [bmr])
                for i_, c_ in enumerate(chunks):
                    tt('dve', w2[:, i_, :], c_, c_, ALU.mult, [bsrc], [bw2])
                for i_ in range(nch):
                    mm(ps[6][:, 8:8 + NS], ones_f[:], w2[:, i_, :], i_ == 0, i_ == nch - 1, [bconst, bw2], [bps[6]])
                ts('dve', r_t, ps[6][:, 8:8 + NS], 1.0 / denom, EPS, ALU.mult, ALU.add, [bps[6]], [bmr])
                if want_mean:
                    tt('dve', w2[:, 0, :], m_t, m_t, ALU.mult, [bmr], [bw2])
                    tt('dve', r_t, r_t, w2[:, 0, :], ALU.subtract, [bmr, bw2], [bmr])
                rsqrt_inplace(r_t, bmr)

            def sample_norm(h_bf, bh, a_t, sh_t, ss):
                sqs_ = sb(ss, "s_sq", [128, 8, NS], BF16); bsqs_ = Buf("s_sq")
                rs_ = sb(ss, "s_rs", [128, NS]); brs_ = Buf("s_rs")
                hf_ = sb(ss, "s_hf", [128, 8, NS]); bhf_ = Buf("s_hf")
                for c in range(8):
                    act(sqs_[:, c, :], xs[:, c, :], AF.Square, [bxs], [bsqs_])
                for c in range(8):
                    mm(ps[0][:, 0:NS], ones_bf[:], sqs_[:, c, :], c == 0, c == 7, [bconst, bsqs_], [bps[0]])
                ts('dve', rs_[:], ps[0][:, 0:NS], 1.0 / D, EPS, ALU.mult, ALU.add, [bps[0]], [brs_])
                rsqrt_inplace(rs_[:], brs_)
                for c in range(8):
                    tt('dve', hf_[:, c, :], xs[:, c, :], rs_[:], ALU.mult, [bxs, brs_], [bhf_])
                tt('dve', hf_[:], hf_[:], a_t, ALU.mult, [bhf_, bmod], [bhf_])
                tt('dve', hf_[:], hf_[:], sh_t, ALU.add, [bhf_, bmod], [bhf_])
                cp('dve', h_bf[:], hf_[:], [bhf_], [bh])

            def sample_phase1(win, bwin, wout, bwout):
                with ExitStack() as ss:
                    hs = sb(ss, "s_hs", [128, 8, NS], BF16); bhs = Buf("s_hs")
                    zs = sb(ss, "s_zs", [128, 8, NS]); bzs = Buf("s_zs")
                    w1 = sb(ss, "s_w1", [128, 8, NS]); bw1 = Buf("s_w1")
                    w2 = sb(ss, "s_w2", [128, 8, NS]); bw2 = Buf("s_w2")
                    w3 = sb(ss, "s_w3", [128, 8, NS]); bw3 = Buf("s_w3")
                    mt = sb(ss, "s_mt", [128, NS]); rt = sb(ss, "s_rt", [128, NS]); bmr = Buf("s_mr")
                    tm = sb(ss, "s_tm", [NS, 3, 512]); btm = Buf("s_tm")
                    lftm = sb(ss, "s_lftm", [NS, 8]); blftm = Buf("s_lftm")
                    ysT = sb(ss, "s_ysT", [128, 8, NS], BF16); bys = Buf("s_ysT")
                    ycs = sb(ss, "s_ycs", [128, 4, NS]); bycs = Buf("s_ycs")
                    cst = sb(ss, "s_cst", [128, 2, NS, 31]); bcst = Buf("s_cst")
                    p31 = sb(ss, "s_p31", [128, 2, NS, 31]); bp31 = Buf("s_p31")
                    Kp = [sb(ss, "s_Kp%d" % i, [128, 512]) for i in range(2)]
                    Vp = [sb(ss, "s_Vp%d" % i, [128, 512]) for i in range(2)]
                    prod = sb(ss, "s_prod", [128, 512]); bprod = Buf("s_prod")
                    qrep = sb(ss, "s_qrep", [128, 512]); bqrep = Buf("s_qrep")
                    krep = sb(ss, "s_krep", [128, 512]); bkrep = Buf("s_krep")
                    vrep = sb(ss, "s_vrep", [128, 512]); bvrep = Buf("s_vrep")
                    LF = sb(ss, "s_LF", [128, NPG, 8])
                    cumi = sb(ss, "s_cumi", [128, NPG + 1, 8]); bcumi = Buf("s_cumi")
                    S_ = sb(ss, "s_S", [128, NPG + 1, 8]); bS = Buf("s_S")
                    Pm = sb(ss, "s_P", [128, NPG + 1, 8]); bPm = Buf("s_P")
                    gs = sb(ss, "s_gs", [128, 8]); bgs = Buf("s_gs")
                    lfrep = sb(ss, "s_lfrep", [128, 8]); blfrep = Buf("s_lfrep")
                    Om = sb(ss, "s_Om", [8, 512]); bOm = Buf("s_Om")
                    den = sb(ss, "s_den", [128, 2]); bden = Buf("s_den")
                    bKp, bVp, bLF = bpool[0:2], bpool[2:4], bpool[4]
                    selt = sb(ss, "selt", [NS, NS, 128])
                    blkm = sb(ss, "blkm", [8, 512])
                    bsel = Buf("selblk")
                    dma(selt[:], sel_d, writes=[bsel], sem_buf=bpar)
                    dma(blkm[:], blkm_d, writes=[bsel], sem_buf=bpar)
                    dma(cst[:, :, :, 0:30], stconv_d[l], writes=[bcst], sem_buf=bpar)
                    fw.barrier()

                    ts('dve', idxl[:], idx0[:], float(l * NPHYS * 128), None, ALU.add, None, [bconst], [bidx])
                    sample_norm(hs, bhs, a1[:, :, sc1], shift1[:, :, sc1], ss)
                    for ch in range(8):
                        for kc in range(8):
                            mm(ps[1][:, ch * NS:(ch + 1) * NS], win[:, kc, ch * 128:(ch + 1) * 128], hs[:, kc, :],
                               kc == 0, kc == 7, [bwin, bhs], [bps[1]])
                    cp('dve', zs[:], ps[1][:, 0:8 * NS].rearrange("p (c n) -> p c n", n=NS), [bps[1]], [bzs])
                    for g_, c0 in enumerate((1024, 1536, 2048)):
                        for kc in range(8):
                            mm(ps[2 + g_][0:NS, :], hs[:, kc, :], win[:, kc, c0:c0 + 512], kc == 0, kc == 7,
                               [bwin, bhs], [bps[2 + g_]])
                        cp('act', tm[:, g_, :], ps[2 + g_][0:NS, :], [bps[2 + g_]], [btm])
                    for kc in range(8):
                        mm(ps[5][0:NS, 0:8], hs[:, kc, :], win[:, kc, 2560:2568], kc == 0, kc == 7, [bwin, bhs], [bps[5]])
                    tt('dve', lftm[:], ps[5][0:NS, 0:8], bfb[0:NS, :], ALU.add, [bps[5], bpar], [blftm])
                    act(lftm[:], lftm[:], AF.Exp, [blftm], [blftm], scale=-1.0)
                    act(lftm[:], lftm[:], AF.Ln, [blftm, bconst], [blftm], bias=onecol[0:NS, 0:1], scale=1.0)
                    ts('dve', lftm[:], lftm[:], -1.0, None, ALU.mult, None, [blftm], [blftm])
                    dma(ks_o[l], tm[:, 1, :], reads=[btm], sem_buf=bmisc)
                    dma(vs_o[l], tm[:, 2, :], reads=[btm], sem_buf=bmisc)
                    dma(lfs_o[l], lftm[:], reads=[blftm], sem_buf=bmisc)

                    act(w1[:, 0:4, :], zs[:, 0:4, :], AF.Gelu, [bzs], [bw1])
                    chan_stats([w1[:, 2, :], w1[:, 3, :]], bw1, w2, bw2, mt[:], rt[:], bmr, 256, True)
                    for cc in range(2):
                        tt('dve', w3[:, cc, :], w1[:, 2 + cc, :], mt[:], ALU.subtract, [bw1, bmr], [bw3])
                        tt('dve', w3[:, cc, :], w3[:, cc, :], rt[:], ALU.mult, [bw3, bmr], [bw3])
                        ts('dve', w3[:, cc, :], w3[:, cc, :], alngT[:, cc:cc + 1], alnbT[:, cc:cc + 1], ALU.mult, ALU.add,
                           [bw3, bpar], [bw3])
                    dma(chv_o[l], w3[:, 0:2, :], reads=[bw3], sem_buf=bmisc)
                    for cc in range(2):
                        ts('dve', w3[:, 2 + cc, :], w3[:, cc, :], ws00T[:, cc:cc + 1], bs0T[:, cc:cc + 1], ALU.mult, ALU.add,
                           [bw3, bpar], [bw3])
                        tt('dve', w3[:, 2 + cc, :], w3[:, 2 + cc, :], w1[:, cc, :], ALU.mult, [bw3, bw1], [bw3])
                    chan_stats([w3[:, 2, :], w3[:, 3, :]], bw3, w2, bw2, mt[:], rt[:], bmr, 256, False)
                    for cc in range(2):
                        tt('dve', ysT[:, cc, :], w3[:, 2 + cc, :], rt[:], ALU.mult, [bw3, bmr], [bys])

                    act(w1[:, 4:6, :], zs[:, 6:8, :], AF.Sigmoid, [bzs], [bw1])
                    for cc in range(2):
                        tt('dve', cst[:, cc, :, 30], zs[:, 4 + cc, :], w1[:, 4 + cc, :], ALU.mult, [bzs, bw1], [bcst])
                    dma(convs_o[l], cst[:, :, :, 1:31], reads=[bcst], sem_buf=bmisc)
                    for cc in range(2):
                        for n in range(NS):
                            tt('dve', p31[:, cc, n, :], cst[:, cc, n, :], cwT[:, cc, :], ALU.mult, [bcst, bpar], [bp31])
                        op('dve', lambda e: e.reduce_sum(w3[:, 4 + cc, :], p31[:, cc, :, :], mybir.AxisListType.X),
                           [bp31], [bw3])
                        ts('dve', w3[:, 4 + cc, :], w3[:, 4 + cc, :], cbT[:, cc:cc + 1], None, ALU.add, None, [bw3, bpar], [bw3])
                    chan_stats([w3[:, 4, :], w3[:, 5, :]], bw3, w2, bw2, mt[:], rt[:], bmr, 256, True)
                    for cc in range(2):
                        tt('dve', w3[:, 4 + cc, :], w3[:, 4 + cc, :], mt[:], ALU.subtract, [bw3, bmr], [bw3])
                        tt('dve', w3[:, 4 + cc, :], w3[:, 4 + cc, :], rt[:], ALU.mult, [bw3, bmr], [bw3])
                        act(w3[:, 4 + cc, :], w3[:, 4 + cc, :], AF.Silu, [bw3, bpar], [bw3],
                            scale=clgT[:, cc:cc + 1], bias=clbT[:, cc:cc + 1])
                    chan_stats([w3[:, 4, :], w3[:, 5, :]], bw3, w2, bw2, mt[:], rt[:], bmr, 256, False)
                    for cc in range(2):
                        tt('dve', ysT[:, 2 + cc, :], w3[:, 4 + cc, :], rt[:], ALU.mult, [bw3, bmr], [bys])

                    it = 0
                    for n in range(NS):
                        col = n * NPG
                        mm(ps[7][:, 0:8], selt[:, n, :], lftm[:], True, True, [bsel, blftm], [bps[7]])
                        cp('dve', lfrep[:], ps[7][:, 0:8], [bps[7]], [blfrep])
                        mm(ps[2][:], selt[:, n, :], tm[:, 0, :], True, True, [bsel, btm], [bps[2]])
                        act(qrep[:], ps[2][:], AF.Identity, [bps[2]], [bqrep], scale=0.125, bias=0.0)
                        mm(ps[3][:], selt[:, n, :], tm[:, 1, :], True, True, [bsel, btm], [bps[3]])
                        cp('act', krep[:], ps[3][:], [bps[3]], [bkrep])
                        mm(ps[4][:], selt[:, n, :], tm[:, 2, :], True, True, [bsel, btm], [bps[4]])
                        cp('act', vrep[:], ps[4][:], [bps[4]], [bvrep])
                        op('pool', lambda e: e.memset(gs[:], 0.0), [], [bgs])
                        for pg in range(NPG):
                            idma(LF[:, pg, :], cf_d, idxl[:, col + pg:col + pg + 1], [bidx], [bLF], bLF)
                        for pg in range(NPG):
                            mm(ps[7][:, 16:24], tri_f[:], LF[:, pg, :], True, True, [bconst, bLF], [bps[7]])
                            mm(ps[7][:, 32:40], ones_f[:], LF[:, pg, :], True, True, [bconst, bLF], [bps[7]])
                            tt('dve', cumi[:, pg, :], ps[7][:, 16:24], gs[:], ALU.add, [bps[7], bgs], [bcumi])
                            tt('dve', gs[:], ps[7][:, 32:40], gs[:], ALU.add, [bps[7], bgs], [bgs])
                        tt('dve', gs[:], gs[:], lfrep[:], ALU.add, [bgs, blfrep], [bgs])
                        for pg in range(NPG):
                            tt('dve', cumi[:, pg, :], gs[:], cumi[:, pg, :], ALU.subtract, [bgs, bcumi], [bcumi])
                        cp('dve', cumi[:, NPG, :], bigm[:], [bconst], [bcumi])
                        for pg in range(NPG + 1):
                            sl = it % 2
                            it += 1
                            if pg < NPG:
                                idma(Kp[sl][:], ck_d, idxl[:, col + pg:col + pg + 1], [bidx], [bKp[sl]], bKp[sl])
                                idma(Vp[sl][:], cv_d, idxl[:, col + pg:col + pg + 1], [bidx], [bVp[sl]], bVp[sl])
                                k_ap, bk_, v_ap, bv_ = Kp[sl], bKp[sl], Vp[sl], bVp[sl]
                            else:
                                k_ap, bk_, v_ap, bv_ = krep, bkrep, vrep, bvrep
                            tt('dve', prod[:], k_ap[:], qrep[:], ALU.mult, [bk_, bqrep], [bprod])
                            op('dve', lambda e: e.reduce_sum(S_[:, pg, :], prod[:].rearrange("p (h d) -> p h d", d=64),
                                                             mybir.AxisListType.X), [bprod], [bS])
                            tt('dve', S_[:, pg, :], S_[:, pg, :], cumi[:, pg, :], ALU.add, [bS, bcumi], [bS])
                            act(Pm[:, pg, :], S_[:, pg, :], AF.Exp, [bS], [bPm])
                            mm(ps[5][0:8, :], Pm[:, pg, :], v_ap[:], pg == 0, pg == NPG, [bPm, bv_], [bps[5]])
                        op('dve', lambda e: e.reduce_sum(prod[:, 0:8], Pm[:].rearrange("p g h -> p h g"),
                                                         mybir.AxisListType.X), [bPm], [bprod])
                        mm(ps[7][0:8, 48:49], prod[:, 0:8], ones_f[:, 0:1], True, True, [bprod, bconst], [bps[7]])
                        op('dve', lambda e: e.reciprocal(den[0:8, 0:1], ps[7][0:8, 48:49]), [bps[7]], [bden])
                        stt('dve', Om[:], ps[5][0:8, :], den[0:8, 0:1], blkm[:], ALU.mult, ALU.mult,
                            [bps[5], bden, bsel], [bOm])
                        for c in range(4):
                            mm(ps[6][:, 16 + c:17 + c], Om[:, c * 128:(c + 1) * 128], ones_f[0:8, 0:1], True, True,
                               [bOm, bconst], [bps[6]])
                        cp('dve', ycs[:, :, n], ps[6][:, 16:20], [bps[6]], [bycs])
                    chan_stats([ycs[:, c, :] for c in range(4)], bycs, w2, bw2, mt[:], rt[:], bmr, 512, False)
                    for c in range(4):
                        tt('dve', ysT[:, 4 + c, :], ycs[:, c, :], rt[:], ALU.mult, [bycs, bmr], [bys])

                    for dc in range(8):
                        for kc in range(8):
                            mm(ps[1][:, dc * NS:(dc + 1) * NS], wout[:, kc, dc * 128:(dc + 1) * 128], ysT[:, kc, :],
                               kc == 0, kc == 7, [bwout, bys], [bps[1]])
                    tt('dve', w1[:], ps[1][:, 0:8 * NS].rearrange("p (c n) -> p c n", n=NS), gate1[:, :, sc1], ALU.mult,
                       [bps[1], bmod], [bw1])
                    tt('dve', xs[:], xs[:], w1[:], ALU.add, [bxs, bw1], [bxs])
                    fw.barrier()

            def sample_phase2(wup, bwup, wdn, bwdn):
                with ExitStack() as ss:
                    hs = sb(ss, "t_hs", [128, 8, NS], BF16); bhs = Buf("t_hs")
                    raw_s = sb(ss, "t_raw", [128, 44, NS]); braw_s = Buf("t_raw")
                    sff = sb(ss, "t_sff", [128, 44, NS, 2]); bsff = Buf("t_sff")
                    fo = sb(ss, "t_fo", [128, 44, NS, 2]); bfo = Buf("t_fo")
                    up = sb(ss, "t_up", [128, 44, NS]); bup = Buf("t_up")
                    tmp = sb(ss, "t_tmp", [128, 44, NS]); btmp_ = Buf("t_tmp")
                    hm = sb(ss, "t_hm", [128, 22, NS], BF16); bhm = Buf("t_hm")
                    w1 = sb(ss, "t_w1", [128, 8, NS]); bw1 = Buf("t_w1")
                    dma(sff[:], sffn_d[l], writes=[bsff], sem_buf=bpar)
                    fw.barrier()
                    sample_norm(hs, bhs, a2[:, :, sc1], shift2[:, :, sc1], ss)
                    for ch in range(44):
                        k_ = 1 + (ch // 22)
                        cc_ = ch % 22
                        for kc in range(8):
                            mm(ps[k_][:, cc_ * NS:(cc_ + 1) * NS], wup[:, kc, ch * 128:(ch + 1) * 128], hs[:, kc, :],
                               kc == 0, kc == 7, [bwup, bhs], [bps[k_]])
                    for hf in range(2):
                        cp('dve', raw_s[:, hf * 22:(hf + 1) * 22, :],
                           ps[1 + hf][:, 0:22 * NS].rearrange("p (c n) -> p c n", n=NS), [bps[1 + hf]], [braw_s])
                    for n in range(NS):
                        tt('dve', up[:, :, n], sff[:, :, n, 0], fwT[:, :, 0], ALU.mult, [bsff, bpar], [bup])
                        tt('dve', tmp[:, :, n], sff[:, :, n, 1], fwT[:, :, 1], ALU.mult, [bsff, bpar], [btmp_])
                        tt('dve', up[:, :, n], up[:, :, n], tmp[:, :, n], ALU.add, [bup, btmp_], [bup])
                        tt('dve', tmp[:, :, n], raw_s[:, :, n], fwT[:, :, 2], ALU.mult, [braw_s, bpar], [btmp_])
                        tt('dve', up[:, :, n], up[:, :, n], tmp[:, :, n], ALU.add, [bup, btmp_], [bup])
                        tt('dve', up[:, :, n], up[:, :, n], fbT[:], ALU.add, [bup, bpar], [bup])
                        cp('dve', fo[:, :, n, 0], sff[:, :, n, 1], [bsff], [bfo])
                        cp('dve', fo[:, :, n, 1], raw_s[:, :, n], [braw_s], [bfo])
                    dma(ffns_o[l], fo[:], reads=[bfo], sem_buf=bmisc)
                    act(tmp[:, 0:22, :], up[:, 0:22, :], AF.Silu, [bup], [btmp_])
                    tt('dve', hm[:], tmp[:, 0:22, :], up[:, 22:44, :], ALU.mult, [btmp_, bup], [bhm])
                    for dc in range(8):
                        for j in range(22):
                            mm(ps[3][:, dc * NS:(dc + 1) * NS], wdn[:, j, dc * 128:(dc + 1) * 128], hm[:, j, :],
                               j == 0, j == 21, [bwdn, bhm], [bps[3]])
                    tt('dve', w1[:], ps[3][:, 0:8 * NS].rearrange("p (c n) -> p c n", n=NS), gate2[:, :, sc1], ALU.mult,
                       [bps[3], bmod], [bw1])
                    tt('dve', xs[:], xs[:], w1[:], ALU.add, [bxs, bw1], [bxs])
                    if last:
                        sqs_ = sb(ss, "t_sq", [128, 8, NS], BF16); bsqs_ = Buf("t_sq")
                        rs_ = sb(ss, "t_rs", [128, NS]); brs_ = Buf("t_rs")
                        for c in range(8):
                            act(sqs_[:, c, :], xs[:, c, :], AF.Square, [bxs], [bsqs_])
                        for c in range(8):
                            mm(ps[0][:, 0:NS], ones_bf[:], sqs_[:, c, :], c == 0, c == 7, [bconst, bsqs_], [bps[0]])
                        ts('dve', rs_[:], ps[0][:, 0:NS], 1.0 / D, EPS, ALU.mult, ALU.add, [bps[0]], [brs_])
                        rsqrt_inplace(rs_[:], brs_)
                        for c in range(8):
                            stt('dve', w1[:, c, :], xs[:, c, :], fgT[:, c:c + 1], rs_[:], ALU.mult, ALU.mult,
                                [bxs, bpar, brs_], [bw1])
                        dma(ysT_o, w1[:], reads=[bw1], sem_buf=bmisc)
                    fw.barrier()

            with ExitStack() as big:
                KT = sb(big, "KT", [128, 4, T], BF16); bKT = Buf("KT")
                Vst = sb(big, "Vst", [128, NKB, 512], BF16); bVst = Buf("Vst")
                win = sb(big, "win", [128, 8, INW], BF16); bwin = Buf("win")
                wout = sb(big, "wout", [128, 8, D], BF16); bwout = Buf("wout")
                with ExitStack() as st:
                    stg = [sb(st, "stgB%d" % i, [128, INW]) for i in range(2)]
                    engs = ['dve', 'pool']
                    n = 0
                    for kc in range(8):
                        s_ = n % 2
                        dma(stg[s_][:, 0:INW], w_in_d[l, kc * 128:(kc + 1) * 128, :], writes=[bstg[s_]])
                        cp(engs[n % 2], win[:, kc, :], stg[s_][:, 0:INW], [bstg[s_]], [bwin])
                        n += 1
                    for kc in range(8):
                        s_ = n % 2
                        dma(stg[s_][:, 0:D], w_out_d[l, kc * 128:(kc + 1) * 128, :], writes=[bstg[s_]])
                        ts(engs[n % 2], wout[:, kc, :], stg[s_][:, 0:D], mixgT[:, kc:kc + 1], None, ALU.mult, None,
                           [bstg[s_], bpar], [bwout])
                        n += 1
                    fw.barrier()
                    stop(2)
                sample_phase1(win, bwin, wout, bwout)

                with ExitStack() as sc:
                    hT = sb(sc, "hT", [128, 8, 512], BF16); bhT = Buf("hT")
                    qT = sb(sc, "qT", [128, 4, 512], BF16); bqT = Buf("qT")
                    yT = sb(sc, "yT", [128, 8, 512], BF16)
                    byA, byB, byC = Buf("yA"), Buf("yB"), Buf("yC")
                    glu = sb(sc, "glu", [128, 2, 542]); bglu = Buf("glu")
                    cacc = sb(sc, "cacc", [128, 2, 512]); bcacc = Buf("cacc")
                    za = sb(sc, "za", [128, 512]); bza = Buf("za")
                    tA = sb(sc, "tA", [128, 256]); btA = Buf("tA")
                    tA2 = sb(sc, "tA2", [128, 256]); btA2 = Buf("tA2")
                    vln = sb(sc, "vln", [128, 256], BF16); bvln = Buf("vln")
                    yan = sb(sc, "yan", [128, 256], BF16); byan = Buf("yan")
                    PT = [sb(sc, "PT%d" % i, [128, 512], BF16) for i in range(2)]
                    bPT = [Buf("PT%d" % i) for i in range(2)]
                    sq = [sb(sc, "sq%d" % i, [128, 512], BF16) for i in range(2)]
                    bsq = [Buf("sq%d" % i) for i in range(2)]
                    st0 = sb(sc, "st0", [128, 512]); bst0 = Buf("st0")
                    st1 = sb(sc, "st1", [128, 512]); bst1 = Buf("st1")
                    st2 = sb(sc, "st2", [128, 512]); bst2 = Buf("st2")
                    kst = sb(sc, "kst", [128, 512]); bkst = Buf("kst")
                    vst = sb(sc, "vst", [128, 512]); bvst = Buf("vst")
                    lft = sb(sc, "lft", [128, 8]); blft = Buf("lft")
                    lfs = sb(sc, "lfs", [128, 8]); blfs = Buf("lfs")
                    sml = sb(sc, "sml", [128, 16]); bsml = Buf("sml")
                    biasT = sb(sc, "biasT", [128, NKB, 8]); bbias = Buf("biasT")

                    op('pool', lambda e: e.memset(gtot[:], 0.0), [], [bgtot])
                    op('pool', lambda e: e.memset(glu[:, :, 0:30], 0.0), [], [bglu])

                    for i in range(NT):
                        t0 = i * 512
                        dma(x[:], x_srcv[:, :, t0:t0 + 512], writes=[bx])
                        rms_rstd(lambda c: x[:, c, :], bx, st0[:], bst0, sq, bsq, 8, 512, D, ps[0], bps[0])
                        for c in range(8):
                            tmp, btmp = (st1, bst1) if c % 2 == 0 else (st2, bst2)
                            stt('dve', tmp[:], x[:, c, :], a1[:, c, 0:1], st0[:], ALU.mult, ALU.mult,
                                [bx, bmod, bst0], [btmp])
                            act(hT[:, c, :], tmp[:], AF.Identity, [btmp, bmod], [bhT], bias=shift1[:, c, 0:1], scale=1.0)

                        stop(31)
                        pi = [0]

                        def proj_fm(col0):
                            k_ = 1 + (pi[0] % 2)
                            pi[0] += 1
                            for kc in range(8):
                                mm(ps[k_][:], win[:, kc, col0:col0 + 128], hT[:, kc, :], kc == 0, kc == 7,
                                   [bwin, bhT], [bps[k_]])
                            return ps[k_], bps[k_]

                        for cc in range(2):
                            pg_, bpg_ = proj_fm(512 + 256 + cc * 128)
                            act(st1[:], pg_[:], AF.Sigmoid, [bpg_], [bst1])
                            stop(311)
                            pa_, bpa_ = proj_fm(512 + cc * 128)
                            tt('dve', glu[:, cc, 30:542], pa_[:], st1[:], ALU.mult, [bpa_, bst1], [bglu])
                            stop(312)
                        for j in range(4):
                            pq_, bpq_ = proj_fm(1024 + j * 128)
                            stop(313)
                            cp('act', qT[:, j, :], pq_[:], [bpq_], [bqT])
                            stop(314)
                            pk_, bpk_ = proj_fm(1536 + j * 128)
                            stop(315)
                            cp('dve', KT[:, j, t0:t0 + 512], pk_[:], [bpk_], [bKT])
                            stop(316)
                            stop(320 + j)

                        stop(32)
                        for s in range(4):
                            kb = 4 * i + s
                            tok = slice(s * 128, (s + 1) * 128)

                            def proj_tm(col0, ncol, k_):
                                for kc in range(8):
                                    mm(ps[k_][:, 0:ncol], hT[:, kc, tok], win[:, kc, col0:col0 + ncol], kc == 0, kc == 7,
                                       [bwin, bhT], [bps[k_]])
                                return ps[k_], bps[k_]

                            pk_, bpk_ = proj_tm(1536, 512, 3)
                            stop(3301)
                            cp('act', kst[:], pk_[:], [bpk_], [bkst])
                            stop(3302)
                            dma(k_o[l, t0 + s * 128:t0 + (s + 1) * 128, :], kst[:], reads=[bkst])
                            stop(331)
                            pv_, bpv_ = proj_tm(2048, 512, 4)
                            cp('act', vst[:], pv_[:], [bpv_], [bvst])
                            cp('dve', Vst[:, kb, :], vst[:], [bvst], [bVst])
                            dma(v_o[l, t0 + s * 128:t0 + (s + 1) * 128, :], vst[:], reads=[bvst])
                            stop(332)
                            pf_, bpf_ = proj_tm(2560, 8, 5)
                            tt('dve', lft[:], pf_[:, 0:8], bfb[:], ALU.add, [bpf_, bpar], [blft])
                            act(lft[:], lft[:], AF.Exp, [blft], [blft], scale=-1.0)
                            act(lft[:], lft[:], AF.Ln, [blft, bconst], [blft], bias=onecol[:, 0:1], scale=1.0)
                            ts('dve', lfs[:], lft[:], -1.0, None, ALU.mult, None, [blft], [blfs])
                            dma(lf_o[l, t0 + s * 128:t0 + (s + 1) * 128, :], lfs[:], reads=[blfs])
                            stop(333)
                            mm(ps[5][:, 16:24], tri_f[:], lfs[:], True, True, [bconst, blfs], [bps[5]])
                            mm(ps[5][:, 32:40], ones_f[:], lfs[:], True, True, [bconst, blfs], [bps[5]])
                            tt('dve', cum[:, kb, :], ps[5][:, 16:24], gtot[:], ALU.add, [bps[5], bgtot], [bcum])
                            tt('dve', gtot[:], ps[5][:, 32:40], gtot[:], ALU.add, [bps[5], bgtot], [bgtot])
                            stop(334)

                            pa_, bpa_ = proj_tm(0, 512, 6)
                            act(za[:], pa_[:], AF.Gelu, [bpa_], [bza])
                            stop(335)
                            op('dve', lambda e: e.bn_stats(sml[:, 0:6], za[:, 256:512]), [bza], [bsml])
                            op('dve', lambda e: e.bn_aggr(sml[:, 8:10], sml[:, 0:6]), [bsml], [bsml])
                            ts('dve', sml[:, 9:10], sml[:, 9:10], EPS, None, ALU.add, None, [bsml], [bsml])
                            rsqrt_inplace(sml[:, 9:10], bsml)
                            stop(336)
                            ts('dve', tA[:], za[:, 256:512], sml[:, 8:9], sml[:, 9:10], ALU.subtract, ALU.mult,
                               [bza, bsml], [btA])
                            tt('dve', tA[:], tA[:], alng[:], ALU.mult, [btA, bpar], [btA])
                            tt('dve', vln[:], tA[:], alnb[:], ALU.add, [btA, bpar], [bvln])
                            stop(337)
                            for h in range(4):
                                mm(ps[7][:, h * 64:(h + 1) * 64], wsT_bf[:, h, :], vln[:, h * 64:(h + 1) * 64], True, True,
                                   [bpar, bvln], [bps[7]])
                            for h in range(4):
                                stt('dve', tA2[:, h * 64:(h + 1) * 64], ps[7][:, h * 64:(h + 1) * 64], bsT[:, h:h + 1],
                                    za[:, h * 64:(h + 1) * 64], ALU.add, ALU.mult, [bps[7], bpar, bza], [btA2])
                            tt('dve', tA[:], tA2[:], tA2[:], ALU.mult, [btA2], [btA])
                            op('dve', lambda e: e.reduce_sum(sml[:, 12:13], tA[:], mybir.AxisListType.X), [btA], [bsml])
                            ts('dve', sml[:, 12:13], sml[:, 12:13], 1.0 / 256, EPS, ALU.mult, ALU.add, [bsml], [bsml])
                            rsqrt_inplace(sml[:, 12:13], bsml)
                            stop(338)
                            ts('dve', yan[:], tA2[:], sml[:, 12:13], None, ALU.mult, None, [btA2, bsml], [byan])
                            stop(339)
                            for cA in range(2):
                                mm(ps[7][:, 256 + cA * 128:256 + (cA + 1) * 128], yan[:, cA * 128:(cA + 1) * 128], ident_bf[:],
                                   True, True, [byan, bconst], [bps[7]])
                                cp('act', yT[:, cA, tok], ps[7][:, 256 + cA * 128:256 + (cA + 1) * 128], [bps[7]], [byA])

                        stop(33)
                        for cc in range(2):
                            ts('dve', cacc[:, cc, :], glu[:, cc, 0:512], cwT[:, cc, 0:1], None, ALU.mult, None,
                               [bglu, bpar], [bcacc])
                            for j in range(1, 31):
                                stt('dve', cacc[:, cc, :], glu[:, cc, j:j + 512], cwT[:, cc, j:j + 1], cacc[:, cc, :],
                                    ALU.mult, ALU.add, [bglu, bpar, bcacc], [bcacc])
                            act(cacc[:, cc, :], cacc[:, cc, :], AF.Identity, [bcacc, bpar], [bcacc],
                                bias=cbT[:, cc:cc + 1], scale=1.0)
                        if i == NT - 1:
                            dma(convp_o[l].rearrange("(c p) j -> p c j", p=128), glu[:, :, 512:542], reads=[bglu], sem_buf=bmisc)
                        for cc in range(2):
                            mm(ps[1][:], ones_f[:], cacc[:, cc, :], cc == 0, cc == 1, [bconst, bcacc], [bps[1]])
                        for cc in range(2):
                            act(st1[:], cacc[:, cc, :], AF.Square, [bcacc], [bst1])
                            mm(ps[2][:], ones_f[:], st1[:], cc == 0, cc == 1, [bconst, bst1], [bps[2]])
                        ts('dve', st0[:], ps[1][:], 1.0 / 256, None, ALU.mult, None, [bps[1]], [bst0])
                        tt('dve', st2[:], st0[:], st0[:], ALU.mult, [bst0], [bst2])
                        stt('dve', st2[:], ps[2][:], 1.0 / 256, st2[:], ALU.mult, ALU.subtract, [bps[2], bst2], [bst2])
                        ts('dve', st2[:], st2[:], EPS, None, ALU.add, None, [bst2], [bst2])
                        rsqrt_inplace(st2[:], bst2)
                        for cc in range(2):
                            tt('dve', cacc[:, cc, :], cacc[:, cc, :], st0[:], ALU.subtract, [bcacc, bst0], [bcacc])
                            tt('dve', cacc[:, cc, :], cacc[:, cc, :], st2[:], ALU.mult, [bcacc, bst2], [bcacc])
                            act(cacc[:, cc, :], cacc[:, cc, :], AF.Silu, [bcacc, bpar], [bcacc],
                                scale=clgT[:, cc:cc + 1], bias=clbT[:, cc:cc + 1])
                        rms_rstd(lambda c: cacc[:, c, :], bcacc, st1[:], bst1, sq, bsq, 2, 512, 256, ps[1], bps[1])
                        for cc in range(2):
                            tt('dve', yT[:, 2 + cc, :], cacc[:, cc, :], st1[:], ALU.mult, [bcacc, bst1], [byB])
                        for cc in range(2):
                            cp('pool', glu[:, cc, 0:30], glu[:, cc, 512:542], [bglu], [bglu])

                        stop(34)
                        nkb = 4 * i + 4
                        for kb in range(nkb):
                            tt('dve', biasT[:, kb, :], gtot[:], cum[:, kb, :], ALU.subtract, [bgtot, bcum], [bbias])
                        it = 0
                        for h in range(8):
                            c, pb = h // 2, 64 * (h % 2)
                            for kb in range(nkb):
                                j = kb - 4 * i
                                col0 = 0 if j <= 0 else 128 * j
                                ncols = 512 - col0
                                pS, bpS = ps[3 + (it % 2)], bps[3 + (it % 2)]
                                P_, bP_ = PT[it % 2], bPT[it % 2]
                                it += 1
                                mm(pS[:, 0:ncols], KT[pb:pb + 64, c, kb * 128:(kb + 1) * 128], qT[pb:pb + 64, c, col0:512],
                                   True, True, [bKT, bqT], [bpS])
                                act(P_[:, 0:ncols], pS[:, 0:ncols], AF.Exp, [bpS, bbias], [bP_],
                                    scale=0.125, bias=biasT[:, kb, h:h + 1])
                                if j >= 0:
                                    tt('pool', P_[:, 0:128], P_[:, 0:128], mask_bf[:], ALU.mult, [bP_, bconst], [bP_])
                                mm(ps[5][:, col0:512], Vst[:, kb, c * 128:(c + 1) * 128], P_[:, 0:ncols], kb == 0, kb == nkb - 1,
                                   [bVst, bP_], [bps[5]])
                                mm(ps[6][:, col0:512], ones_bf[:], P_[:, 0:ncols], kb == 0, kb == nkb - 1,
                                   [bconst, bP_], [bps[6]])
                            op('dve', lambda e: e.reciprocal(st0[pb:pb + 64, :], ps[6][pb:pb + 64, :]), [bps[6]], [bst0])
                            tt('dve', yT[pb:pb + 64, 4 + c, :], ps[5][pb:pb + 64, :], st0[pb:pb + 64, :], ALU.mult,
                               [bps[5], bst0], [byC])
                        rms_rstd(lambda c: yT[:, 4 + c, :], byC, st1[:], bst1, sq, bsq, 4, 512, 512, ps[0], bps[0])
                        for c in range(4):
                            tt('dve', yT[:, 4 + c, :], yT[:, 4 + c, :], st1[:], ALU.mult, [byC, bst1], [byC])

                        stop(35)
                        for dc in range(8):
                            k_ = 1 + (dc % 2)
                            for kc in range(8):
                                mm(ps[k_][:], wout[:, kc, dc * 128:(dc + 1) * 128], yT[:, kc, :], kc == 0, kc == 7,
                                   [bwout, byA, byB, byC], [bps[k_]])
                            stt('dve', x[:, dc, :], ps[k_][:], gate1[:, dc, 0:1], x[:, dc, :], ALU.mult, ALU.add,
                                [bps[k_], bmod, bx], [bx])
                        dma(xsv[:, :, t0:t0 + 512], x[:], reads=[bx])
                        stop(3)
                    fw.barrier()

            with ExitStack() as big:
                wup = sb(big, "wup", [128, 8, 2 * DFF], BF16); bwup = Buf("wup")
                wdn = sb(big, "wdn", [128, 22, D], BF16); bwdn = Buf("wdn")
                with ExitStack() as st:
                    stg = [sb(st, "stgC%d" % i, [128, DFF]) for i in range(2)]
                    engs = ['dve', 'pool']
                    n = 0
                    for kc in range(8):
                        for hf in range(2):
                            s_ = n % 2
                            dma(stg[s_][:, 0:DFF], w_up_d[l, kc * 128:(kc + 1) * 128, hf * DFF:(hf + 1) * DFF],
                                writes=[bstg[s_]])
                            cp(engs[n % 2], wup[:, kc, hf * DFF:(hf + 1) * DFF], stg[s_][:, 0:DFF], [bstg[s_]], [bwup])
                            n += 1
                    for j in range(22):
                        s_ = n % 2
                        dma(stg[s_][:, 0:D], w_down_d[l, j * 128:(j + 1) * 128, :], writes=[bstg[s_]])
                        cp(engs[n % 2], wdn[:, j, :], stg[s_][:, 0:D], [bstg[s_]], [bwdn])
                        n += 1
                    fw.barrier()
                    stop(4)
                sample_phase2(wup, bwup, wdn, bwdn)

                with ExitStack() as sc:
                    hT = sb(sc, "hT2", [128, 8, 512], BF16); bhT = Buf("hT2")
                    hmid = sb(sc, "hmid", [128, 22, 512], BF16); bhmid = Buf("hmid")
                    raw = [sb(sc, "raw%d" % i, [128, 514]) for i in range(2)]
                    braw = [Buf("raw%d" % i) for i in range(2)]
                    acc = [sb(sc, "acc%d" % i, [128, 512]) for i in range(2)]
                    bacc = [Buf("acc%d" % i) for i in range(2)]
                    sq = [sb(sc, "sq2%d" % i, [128, 512], BF16) for i in range(2)]
                    bsq = [Buf("sq2%d" % i) for i in range(2)]
                    st0 = sb(sc, "st02", [128, 512]); bst0 = Buf("st02")
                    st1, bst1, st2, bst2 = acc[0], bacc[0], acc[1], bacc[1]

                    op('pool', lambda e: e.memset(halo[:], 0.0), [], [bhalo])
                    for i in range(NT):
                        t0 = i * 512
                        dma(x[:], xsv[:, :, t0:t0 + 512], writes=[bx])
                        rms_rstd(lambda c: x[:, c, :], bx, st0[:], bst0, sq, bsq, 8, 512, D, ps[0], bps[0])
                        for c in range(8):
                            tmp, btmp = (st1, bst1) if c % 2 == 0 else (st2, bst2)
                            stt('dve', tmp[:], x[:, c, :], a2[:, c, 0:1], st0[:], ALU.mult, ALU.mult,
                                [bx, bmod, bst0], [btmp])
                            act(hT[:, c, :], tmp[:], AF.Identity, [btmp, bmod], [bhT], bias=shift2[:, c, 0:1], scale=1.0)
                        for j in range(22):
                            for hf in range(2):
                                ch = hf * 22 + j
                                k_ = 1 + hf + 2 * (j % 2)
                                for kc in range(8):
                                    mm(ps[k_][:], wup[:, kc, ch * 128:(ch + 1) * 128], hT[:, kc, :], kc == 0, kc == 7,
                                       [bwup, bhT], [bps[k_]])
                                r_, br_ = raw[hf], braw[hf]
                                a_, ba_ = acc[hf], bacc[hf]
                                cp('act', r_[:, 2:514], ps[k_][:], [bps[k_]], [br_])
                                cp('pool', r_[:, 0:2], halo[:, ch, :], [bhalo], [br_])
                                act(a_[:], ps[k_][:], AF.Identity, [bps[k_], bpar], [ba_],
                                    scale=fwT[:, ch, 2:3], bias=fbT[:, ch:ch + 1])
                                stt('dve', a_[:], r_[:, 0:512], fwT[:, ch, 0:1], a_[:], ALU.mult, ALU.add,
                                    [br_, bpar, ba_], [ba_])
                                stt('dve', a_[:], r_[:, 1:513], fwT[:, ch, 1:2], a_[:], ALU.mult, ALU.add,
                                    [br_, bpar, ba_], [ba_])
                                cp('pool', halo[:, ch, :], r_[:, 512:514], [br_], [bhalo])
                            act(acc[0][:], acc[0][:], AF.Silu, [bacc[0]], [bacc[0]])
                            tt('dve', hmid[:, j, :], acc[0][:], acc[1][:], ALU.mult, [bacc[0], bacc[1]], [bhmid])
                        if i == NT - 1:
                            dma(ffnp_o[l].rearrange("(c p) k -> p c k", p=128), halo[:], reads=[bhalo], sem_buf=bmisc)
                        for dc in range(8):
                            k_ = 5 + (dc % 2)
                            for j in range(22):
                                mm(ps[k_][:], wdn[:, j, dc * 128:(dc + 1) * 128], hmid[:, j, :], j == 0, j == 21,
                                   [bwdn, bhmid], [bps[k_]])
                            stt('dve', x[:, dc, :], ps[k_][:], gate2[:, dc, 0:1], x[:, dc, :], ALU.mult, ALU.add,
                                [bps[k_], bmod, bx], [bx])
                        if not last:
                            dma(xsv[:, :, t0:t0 + 512], x[:], reads=[bx])
                        else:
                            rms_rstd(lambda c: x[:, c, :], bx, st0[:], bst0, sq, bsq, 8, 512, D, ps[0], bps[0])
                            for c in range(8):
                                stt('dve', x[:, c, :], x[:, c, :], fgT[:, c:c + 1], st0[:], ALU.mult, ALU.mult,
                                    [bx, bpar, bst0], [bx])
                            dma(yT_o.rearrange("(c p) t -> p c t", p=128)[:, :, t0:t0 + 512], x[:], reads=[bx])
                    fw.barrier()
        fw.barrier()
    fw.barrier()
    fw.close()
    return nc


_NC_CACHE = {}


def _host_inputs(inp):
    f = np.float32
    A = lambda a: np.ascontiguousarray(np.asarray(a), dtype=f)
    shared = {}
    shared["w_ada"] = A(inp["w_ada"])
    shared["badaT"] = A(np.asarray(inp["b_ada"]).reshape(NL, 48, 128).transpose(0, 2, 1))
    rep = lambda g: A(np.broadcast_to(np.asarray(g).reshape(NL, 8, 128).transpose(0, 2, 1)[..., None], (NL, 128, 8, 1 + NS)))
    shared["g1r"] = rep(inp["norm1_g"])
    shared["g2r"] = rep(inp["norm2_g"])
    shared["mixgT"] = A(np.asarray(inp["mix_g"]).reshape(NL, 8, 128).transpose(0, 2, 1))
    shared["fgT"] = A(np.asarray(inp["final_g"]).reshape(8, 128).T)
    for k in ("w_in", "w_out", "w_up", "w_down"):
        shared[k] = A(inp[k])
    shared["bfb"] = A(np.broadcast_to(np.asarray(inp["b_forget"])[:, None, :], (NL, 128, 8)))
    shared["alng"] = A(np.broadcast_to(np.asarray(inp["a_ln_g"])[:, None, :], (NL, 128, 256)))
    shared["alnb"] = A(np.broadcast_to(np.asarray(inp["a_ln_b"])[:, None, :], (NL, 128, 256)))
    shared["wsT"] = A(np.asarray(inp["w_s"]).transpose(0, 3, 1, 2))
    shared["bsT"] = A(np.asarray(inp["b_s"]).transpose(0, 2, 1))
    shared["cwT"] = A(np.asarray(inp["conv_w"]).transpose(0, 2, 1).reshape(NL, 2, 128, 31).transpose(0, 2, 1, 3))
    cm = lambda a: A(np.asarray(a).reshape(NL, 2, 128).transpose(0, 2, 1))
    shared["cbT"] = cm(inp["conv_b"]); shared["clgT"] = cm(inp["conv_ln_g"]); shared["clbT"] = cm(inp["conv_ln_b"])
    shared["fwT"] = A(np.asarray(inp["ffn_conv_w"]).transpose(0, 2, 1).reshape(NL, 44, 128, 3).transpose(0, 2, 1, 3))
    shared["fbT"] = A(np.asarray(inp["ffn_conv_b"]).reshape(NL, 44, 128).transpose(0, 2, 1))
    shared["tri"] = np.triu(np.ones((128, 128), f))
    shared["ident"] = np.eye(128, dtype=f)
    shared["alngT"] = cm(inp["a_ln_g"]); shared["alnbT"] = cm(inp["a_ln_b"])
    rep64 = lambda a: A(np.repeat(np.asarray(a), 64, axis=1).reshape(NL, 2, 128).transpose(0, 2, 1))
    shared["ws00T"] = rep64(np.asarray(inp["w_s"])[:, :, 0, 0])
    shared["bs0T"] = rep64(np.asarray(inp["b_s"])[:, :, 0])
    sel = np.zeros((NS, NS, 128), f)
    for n in range(NS):
        sel[n, n, :] = 1.0
    shared["sel"] = sel
    bigm = np.full((128, 8), -30000.0, f); bigm[0, :] = 0.0
    shared["bigm"] = bigm
    blk = np.zeros((8, 512), f)
    for h in range(8):
        blk[h, h * 64:(h + 1) * 64] = 1.0
    shared["blkm"] = blk
    shared["iot"] = np.ascontiguousarray(np.broadcast_to(np.arange(128, dtype=np.int32)[:, None], (128, NS * NPG)))
    shared["cache_k"] = A(inp["cache_k"]).reshape(NL * NPHYS * 128, 512)
    shared["cache_v"] = A(inp["cache_v"]).reshape(NL * NPHYS * 128, 512)
    shared["cache_f"] = A(inp["cache_logf"]).reshape(NL * NPHYS * 128, 8)
    xsm = np.asarray(inp["x_sample"]); stc = np.asarray(inp["state_conv"]); stf = np.asarray(inp["state_ffn_conv"])
    ptab = np.asarray(inp["page_table"]).astype(np.int32)
    xp = np.asarray(inp["x_prompt"]); cp_ = np.asarray(inp["c_prompt"]); cs = np.asarray(inp["c_sample"])
    maps = []
    for c in range(NCORES):
        b = c // 2
        m = dict(shared)
        m["xT"] = A(xp[b].T)
        m["cT"] = A(np.concatenate([cp_[b:b + 1], cs[NS * c:NS * (c + 1)]], 0).T)
        sl = slice(NS * c, NS * (c + 1))
        m["xsT0"] = A(xsm[sl, 0, :].T.reshape(8, 128, NS).transpose(1, 0, 2))
        m["ptb"] = np.ascontiguousarray(np.broadcast_to(ptab[sl].reshape(1, NS * NPG), (128, NS * NPG)))
        m["stconvT"] = A(stc[:, sl].transpose(0, 3, 1, 2).reshape(NL, 2, 128, NS, 30).transpose(0, 2, 1, 3, 4))
        m["sffnT"] = A(stf[:, sl].transpose(0, 3, 1, 2).reshape(NL, 44, 128, NS, 2).transpose(0, 2, 1, 3, 4))
        maps.append(m)
    return maps


def kernel(**inp):
    if "nc" not in _NC_CACHE:
        _NC_CACHE["nc"] = build_program()
    nc = _NC_CACHE["nc"]
    maps = _host_inputs(inp)
    res = run_bass_kernel_spmd(nc, maps, core_ids=list(range(NCORES))).results
    f = np.float32
    B = 4
    y_prompt = np.stack([res[2 * b]["yT"].T for b in range(B)]).astype(f)
    k_p = np.stack([res[2 * b]["k_o"] for b in range(B)], 1).reshape(NL, B, T, 8, 64).astype(f)
    v_p = np.stack([res[2 * b]["v_o"] for b in range(B)], 1).reshape(NL, B, T, 8, 64).astype(f)
    lf_p = np.stack([res[2 * b]["lf_o"] for b in range(B)], 1).astype(f)
    conv_p = np.stack([res[2 * b]["convp"].transpose(0, 2, 1) for b in range(B)], 1).astype(f)
    ffn_p = np.stack([res[2 * b]["ffnp"].transpose(0, 2, 1) for b in range(B)], 1).astype(f)
    cat = lambda fn, ax: np.concatenate([fn(res[c]) for c in range(NCORES)], ax).astype(f)
    y_s = cat(lambda r: r["ysT"].transpose(2, 1, 0).reshape(NS, 1, D), 0)
    k_s = cat(lambda r: r["ks_o"].reshape(NL, NS, 1, 8, 64), 1)
    v_s = cat(lambda r: r["vs_o"].reshape(NL, NS, 1, 8, 64), 1)
    lf_s = cat(lambda r: r["lfs_o"].reshape(NL, NS, 1, 8), 1)
    conv_s = cat(lambda r: r["convs"].transpose(0, 3, 4, 2, 1).reshape(NL, NS, 30, 256), 1)
    ffn_s = cat(lambda r: r["ffns"].transpose(0, 3, 4, 2, 1).reshape(NL, NS, 2, 2 * DFF), 1)
    chv_s = cat(lambda r: r["chv"].transpose(0, 3, 2, 1).reshape(NL, NS, 1, 256), 1)
    return (y_prompt, y_s, k_p, v_p, lf_p, conv_p, ffn_p, k_s, v_s, lf_s, conv_s, ffn_s, chv_s)
```

```python
import numpy as np
import concourse.bass as bass
import concourse.mybir as mybir
from concourse.bass_utils import run_bass_kernel_spmd

F32 = mybir.dt.float32
BF16 = mybir.dt.bfloat16
I32 = mybir.dt.int32
ALU = mybir.AluOpType
AF = mybir.ActivationFunctionType

D = 1024
NL = 2
T = 4096
NS = 4
NPG = 64
DFF = 2816
INW = 2568
EPS = 1e-6
NCORES = 8
NPHYS = 2560
STOP_AT = None


class _Stop(Exception):
    pass


class Buf:
    def __init__(self, name, excl=False):
        self.name = name
        self.w = None
        self.r = {}
        self.excl = excl


class FW:
    def __init__(self, nc):
        self.nc = nc
        self.eng = {'pe': nc.tensor, 'act': nc.scalar, 'dve': nc.vector, 'pool': nc.gpsimd, 'sp': nc.sync}
        self.sems, self.cnt = {}, {}
        self.seen = {k: {} for k in self.eng}
        self._cms = []
        for k in ('pe', 'act', 'dve', 'pool'):
            self._mksem(k)

    def _mksem(self, key):
        cm = self.nc.semaphore("s%d" % len(self.sems))
        self.sems[key] = cm.__enter__()
        self._cms.append(cm)
        self.cnt[key] = 0

    def close(self):
        for cm in reversed(self._cms):
            cm.__exit__(None, None, None)

    def _wait(self, e, key, val):
        if val <= 0 or (e == 'pe' and key == 'pe'):
            return
        if self.seen[e].get(key, 0) >= val:
            return
        self.eng[e].wait_ge(self.sems[key], val)
        self.seen[e][key] = val

    def _deps(self, e, reads, writes, skip=None):
        for b in reads:
            if b.w is not None:
                self._wait(e, *b.w)
            if b.excl:
                for k, v in b.r.items():
                    if k != e:
                        self._wait(e, k, v)
        for b in writes:
            if b.w is not None and b.w[0] != skip:
                self._wait(e, *b.w)
            for k, v in b.r.items():
                self._wait(e, k, v)

    def _mark(self, key, val, reads, writes):
        for b in reads:
            b.r[key] = val
        for b in writes:
            b.w = (key, val)
            b.r = {}

    def op(self, e, fn, reads=(), writes=(), inc=True):
        self._deps(e, reads, writes)
        ins = fn(self.eng[e])
        if inc:
            self.cnt[e] += 1
            ins.then_inc(self.sems[e], 1)
            self._mark(e, self.cnt[e], reads, writes)
        else:
            self._mark(e, self.cnt[e] + 1, reads, writes)

    def dma(self, out, in_, reads=(), writes=(), sem_buf=None, q='sp'):
        b0 = sem_buf if sem_buf is not None else (writes[0] if writes else reads[0])
        key = ('dma', id(b0))
        if key not in self.sems:
            self._mksem(key)
        self._deps(q, reads, writes, skip=key)
        ins = self.eng[q].dma_start(out=out, in_=in_)
        self.cnt[key] += 16
        ins.then_inc(self.sems[key], 16)
        self._mark(key, self.cnt[key], reads, writes)

    def barrier(self):
        for e in self.eng:
            for key in self.sems:
                self._wait(e, key, self.cnt[key])


def build_program():
    nc = bass.Bass("TRN2", target_bir_lowering=False)
    NT = T // 512
    NKB = T // 128

    def din(name, shape, dt=F32):
        return nc.dram_tensor(name, list(shape), dt, kind="ExternalInput").ap()

    def dout(name, shape):
        return nc.dram_tensor(name, list(shape), F32, kind="ExternalOutput").ap()

    xT_d = din("xT", [D, T])
    cT_d = din("cT", [D, 1 + NS])
    w_ada_d = din("w_ada", [NL, D, 6 * D])
    badaT_d = din("badaT", [NL, 128, 48])
    g1r_d = din("g1r", [NL, 128, 8, 1 + NS])
    g2r_d = din("g2r", [NL, 128, 8, 1 + NS])
    mixgT_d = din("mixgT", [NL, 128, 8])
    fgT_d = din("fgT", [128, 8])
    w_in_d = din("w_in", [NL, D, INW])
    w_out_d = din("w_out", [NL, D, D])
    w_up_d = din("w_up", [NL, D, 2 * DFF])
    w_down_d = din("w_down", [NL, DFF, D])
    bfb_d = din("bfb", [NL, 128, 8])
    alng_d = din("alng", [NL, 128, 256])
    alnb_d = din("alnb", [NL, 128, 256])
    wsT_d = din("wsT", [NL, 128, 4, 128])
    bsT_d = din("bsT", [NL, 128, 4])
    cwT_d = din("cwT", [NL, 128, 2, 31])
    cbT_d = din("cbT", [NL, 128, 2])
    clgT_d = din("clgT", [NL, 128, 2])
    clbT_d = din("clbT", [NL, 128, 2])
    fwT_d = din("fwT", [NL, 128, 44, 3])
    fbT_d = din("fbT", [NL, 128, 44])
    tri_d = din("tri", [128, 128])
    ident_d = din("ident", [128, 128])

    xsT0_d = din("xsT0", [128, 8, NS])
    ptb_d = din("ptb", [128, NS * NPG], I32)
    iot_d = din("iot", [128, NS * NPG], I32)
    sel_d = din("sel", [NS, NS, 128])
    bigm_d = din("bigm", [128, 8])
    blkm_d = din("blkm", [8, 512])
    ck_d = din("cache_k", [NL * NPHYS * 128, 512])
    cv_d = din("cache_v", [NL * NPHYS * 128, 512])
    cf_d = din("cache_f", [NL * NPHYS * 128, 8])
    stconv_d = din("stconvT", [NL, 128, 2, NS, 30])
    sffn_d = din("sffnT", [NL, 128, 44, NS, 2])
    alngT_d = din("alngT", [NL, 128, 2])
    alnbT_d = din("alnbT", [NL, 128, 2])
    ws00T_d = din("ws00T", [NL, 128, 2])
    bs0T_d = din("bs0T", [NL, 128, 2])
    ysT_o = dout("ysT", [128, 8, NS])
    ks_o = dout("ks_o", [NL, NS, 512])
    vs_o = dout("vs_o", [NL, NS, 512])
    lfs_o = dout("lfs_o", [NL, NS, 8])
    convs_o = dout("convs", [NL, 128, 2, NS, 30])
    ffns_o = dout("ffns", [NL, 128, 44, NS, 2])
    chv_o = dout("chv", [NL, 128, 2, NS])

    yT_o = dout("yT", [D, T])
    k_o = dout("k_o", [NL, T, 512])
    v_o = dout("v_o", [NL, T, 512])
    lf_o = dout("lf_o", [NL, T, 8])
    convp_o = dout("convp", [NL, 256, 30])
    ffnp_o = dout("ffnp", [NL, 2 * DFF, 2])

    xs_d = nc.dram_tensor("xs_scr", [D, T], F32).ap()

    fw = FW(nc)
    op, dma = fw.op, fw.dma

    def act(out, in_, func, reads, writes, **kw):
        op('act', lambda e: e.activation(out=out, in_=in_, func=func, **kw), reads, writes)

    def mm(out, lhsT, rhs, start, stop, reads, writes, lazy=False):
        op('pe', lambda e: e.matmul(out, lhsT, rhs, start=start, stop=stop), reads, writes,
           inc=(bool(stop) or not lazy))

    def tt(eng, out, a, b, o, reads, writes):
        op(eng, lambda e: e.tensor_tensor(out, a, b, o), reads, writes)

    def ts(eng, out, a, s1, s2, o0, o1, reads, writes):
        if o1 is None:
            op(eng, lambda e: e.tensor_scalar(out, a, s1, None, o0), reads, writes)
        else:
            op(eng, lambda e: e.tensor_scalar(out, a, s1, s2, o0, o1), reads, writes)

    def stt(eng, out, in0, scalar, in1, o0, o1, reads, writes):
        op(eng, lambda e: e.scalar_tensor_tensor(out, in0, scalar, in1, o0, o1), reads, writes)

    def cp(eng, out, in_, reads, writes):
        if eng == 'act':
            act(out, in_, AF.Copy, reads, writes)
        else:
            op(eng, lambda e: e.tensor_copy(out, in_), reads, writes)

    def rsqrt_inplace(tile_ap, buf):
        act(tile_ap, tile_ap, AF.Sqrt, [buf], [buf])
        op('dve', lambda e: e.reciprocal(tile_ap, tile_ap), [buf], [buf])

    from contextlib import ExitStack
    import contextlib
    with ExitStack() as top:
        top.enter_context(contextlib.suppress(_Stop))
        top.enter_context(nc.allow_non_contiguous_dma(reason="small strided parameter loads"))

        uid = [0]

        def sb(stack, name, shape, dt=F32):
            uid[0] += 1
            return stack.enter_context(nc.sbuf_tensor("sb%d_%s" % (uid[0], name), list(shape), dt))

        ps = [top.enter_context(nc.psum_tensor("ps%d" % i, [128, 512], F32)) for i in range(8)]
        bps = [Buf("ps%d" % i, excl=True) for i in range(8)]

        x = sb(top, "x", [128, 8, 512]); bx = Buf("x")
        ones_bf = sb(top, "ones_bf", [128, 128], BF16)
        ones_f = sb(top, "ones_f", [128, 128])
        tri_f = sb(top, "tri_f", [128, 128])
        mask_bf = sb(top, "mask_bf", [128, 128], BF16)
        ident_bf = sb(top, "ident_bf", [128, 128], BF16)
        onecol = sb(top, "onecol", [128, 1])
        bconst = Buf("const")
        silu_c = sb(top, "silu_c", [128, 8, 1 + NS]); bsc = Buf("silu_c")
        mod = sb(top, "mod", [128, 48, 1 + NS]); bmod = Buf("mod")
        a1 = sb(top, "a1", [128, 8, 1 + NS]); a2 = sb(top, "a2", [128, 8, 1 + NS])
        g1r = sb(top, "g1r", [128, 8, 1 + NS]); g2r = sb(top, "g2r", [128, 8, 1 + NS])
        badaT = sb(top, "badaT", [128, 48])
        mixgT = sb(top, "mixgT", [128, 8]); fgT = sb(top, "fgT", [128, 8])
        bfb = sb(top, "bfb", [128, 8])
        wsT_bf = sb(top, "wsT_bf", [128, 4, 128], BF16)
        bsT = sb(top, "bsT", [128, 4])
        cwT = sb(top, "cwT", [128, 2, 31]); cbT = sb(top, "cbT", [128, 2])
        clgT = sb(top, "clgT", [128, 2]); clbT = sb(top, "clbT", [128, 2])
        fwT = sb(top, "fwT", [128, 44, 3]); fbT = sb(top, "fbT", [128, 44])
        bpar = Buf("params")
        bstg = [Buf("stg0"), Buf("stg1")]
        bmisc = Buf("misc")
        cum = sb(top, "cum", [128, NKB, 8]); bcum = Buf("cum")
        gtot = sb(top, "gtot", [128, 8]); bgtot = Buf("gtot")
        halo = sb(top, "halo", [128, 44, 2]); bhalo = Buf("halo")

        xs = sb(top, "xs", [128, 8, NS]); bxs = Buf("xs")
        bigm = sb(top, "bigm", [128, 8])
        idx0 = sb(top, "idx0", [128, NS * NPG])
        idxl = sb(top, "idxl", [128, NS * NPG], I32); bidx = Buf("idx")
        alngT = sb(top, "alngT", [128, 2]); alnbT = sb(top, "alnbT", [128, 2])
        ws00T = sb(top, "ws00T", [128, 2]); bs0T = sb(top, "bs0T", [128, 2])
        bpool = [Buf("pq%d" % i) for i in range(5)]
        with ExitStack() as tmpsc:
            ptb = sb(tmpsc, "ptb", [128, NS * NPG], I32)
            iot = sb(tmpsc, "iot", [128, NS * NPG], I32)
            for dst_, src_ in ((xs, xsT0_d), (bigm, bigm_d), (ptb, ptb_d), (iot, iot_d), (tri_f, tri_d),
                               (ones_f, ident_d), (fgT, fgT_d)):
                dma(dst_[:], src_, writes=[bconst], sem_buf=bpar)
            dma(silu_c[:], cT_d.rearrange("(c p) n -> p c n", p=128), writes=[bsc], sem_buf=bpar)
            fw.barrier()
            iof = sb(tmpsc, "iof", [128, NS * NPG])
            cp('dve', idx0[:], ptb[:], [bconst], [bconst])
            cp('dve', iof[:], iot[:], [bconst], [bconst])
            stt('dve', idx0[:], idx0[:], 128.0, iof[:], ALU.mult, ALU.add, [bconst], [bconst])
            op('dve', lambda e: e.tensor_copy(mask_bf[:], tri_f[:]), [bconst], [bconst])
            op('dve', lambda e: e.tensor_copy(ident_bf[:], ones_f[:]), [bconst], [bconst])
            op('pool', lambda e: e.memset(ones_f[:], 1.0), [], [bconst])
            op('pool', lambda e: e.memset(ones_bf[:], 1.0), [], [bconst])
            op('pool', lambda e: e.memset(onecol[:], 1.0), [], [bconst])
            act(silu_c[:], silu_c[:], AF.Silu, [bsc], [bsc])
            fw.barrier()

        def stop(k):
            if STOP_AT == k:
                raise _Stop()

        for l in range(NL):
            last = (l == NL - 1)
            stop(0)
            for dst, src in ((badaT, badaT_d), (g1r, g1r_d), (g2r, g2r_d), (mixgT, mixgT_d), (bfb, bfb_d),
                             (bsT, bsT_d), (cwT, cwT_d),
                             (cbT, cbT_d), (clgT, clgT_d), (clbT, clbT_d), (fwT, fwT_d), (fbT, fbT_d),
                             (alngT, alngT_d), (alnbT, alnbT_d), (ws00T, ws00T_d), (bs0T, bs0T_d)):
                dma(dst[:], src[l], writes=[bpar])
            with ExitStack() as tmpsc:
                wsT_f = sb(tmpsc, "wsT_f", [128, 4, 128])
                dma(wsT_f[:], wsT_d[l], writes=[bpar])
                fw.barrier()
                for h in range(4):
                    tt('dve', wsT_bf[:, h, :], wsT_f[:, h, :], tri_f[:], ALU.mult, [bpar, bconst], [bpar])
                fw.barrier()

            with ExitStack() as st:
                stg = [sb(st, "stgA%d" % i, [128, 8, 512]) for i in range(2)]
                wv = w_ada_d[l].rearrange("(c p) n -> p c n", p=128)
                for g in range(12):
                    s_ = g % 2
                    dma(stg[s_][:], wv[:, :, g * 512:(g + 1) * 512], writes=[bstg[s_]])
                    for j in range(4):
                        m = g * 4 + j
                        pb_ = bps[m % 2]
                        pt_ = ps[m % 2]
                        for kc in range(8):
                            mm(pt_[:, 0:1 + NS], stg[s_][:, kc, j * 128:(j + 1) * 128], silu_c[:, kc, :],
                               kc == 0, kc == 7, [bstg[s_], bsc], [pb_], lazy=True)
                        act(mod[:, m, :], pt_[:, 0:1 + NS], AF.Identity, [pb_, bpar], [bmod],
                            bias=badaT[:, m:m + 1], scale=1.0)
                ts('dve', a1[:], mod[:, 8:16, :], 1.0, None, ALU.add, None, [bmod], [bmod])
                tt('dve', a1[:], a1[:], g1r[:], ALU.mult, [bmod, bpar], [bmod])
                ts('dve', a2[:], mod[:, 32:40, :], 1.0, None, ALU.add, None, [bmod], [bmod])
                tt('dve', a2[:], a2[:], g2r[:], ALU.mult, [bmod, bpar], [bmod])
                fw.barrier()
                stop(1)
            shift1 = mod[:, 0:8, :]; gate1 = mod[:, 16:24, :]
            shift2 = mod[:, 24:32, :]; gate2 = mod[:, 40:48, :]

            x_src = xT_d if l == 0 else xs_d
            xsv = xs_d.rearrange("(c p) t -> p c t", p=128)
            x_srcv = x_src.rearrange("(c p) t -> p c t", p=128)

            def rms_rstd(xt, bxt, rstd, brstd, sqs, bsqs, nfeat_chunks, ncols, denom, pst, bpst):
                for c in range(nfeat_chunks):
                    act(sqs[c % 2][:, 0:ncols], xt(c), AF.Square, [bxt], [bsqs[c % 2]])
                    mm(pst[:, 0:ncols], ones_bf[:], sqs[c % 2][:, 0:ncols], c == 0, c == nfeat_chunks - 1,
                       [bconst, bsqs[c % 2]], [bpst])
                ts('dve', rstd, pst[:, 0:ncols], 1.0 / denom, EPS, ALU.mult, ALU.add, [bpst], [brstd])
                rsqrt_inplace(rstd, brstd)

            def idma(out, in_, idx_ap, reads, writes, sem_buf):
                key = ('dma', id(sem_buf))
                if key not in fw.sems:
                    fw._mksem(key)
                fw._deps('pool', reads, writes, skip=key)
                ins = nc.gpsimd.indirect_dma_start(out=out, out_offset=None, in_=in_,
                                                   in_offset=bass.IndirectOffsetOnAxis(ap=idx_ap, axis=0))
                fw.cnt[key] += 16
                ins.then_inc(fw.sems[key], 16)
                fw._mark(key, fw.cnt[key], reads, writes)

            sc1 = slice(1, 1 + NS)

            def chan_stats(chunks, bsrc, w2, bw2, m_t, r_t, bmr, denom, want_mean):
                nch = len(chunks)
                if want_mean:
                    for i_, c_ in enumerate(chunks):
                        mm(ps[6][:, 0:NS], ones_f[:], c_, i_ == 0, i_ == nch - 1, [bconst, bsrc], [bps[6]])
                    ts('dve', m_t, ps[6][:, 0:NS], 1.0 / denom, None, ALU.mult, None, [bps[6]], [bmr])
                for i_, c_ in enumerate(chunks):
                    tt('dve', w2[:, i_, :], c_, c_, ALU.mult, [bsrc], [bw2])
                for i_ in range(nch):
                    mm(ps[6][:, 8:8 + NS], ones_f[:], w2[:, i_, :], i_ == 0, i_ == nch - 1, [bconst, bw2], [bps[6]])
                ts('dve', r_t, ps[6][:, 8:8 + NS], 1.0 / denom, EPS, ALU.mult, ALU.add, [bps[6]], [bmr])
                if want_mean:
                    tt('dve', w2[:, 0, :], m_t, m_t, ALU.mult, [bmr], [bw2])
                    tt('dve', r_t, r_t, w2[:, 0, :], ALU.subtract, [bmr, bw2], [bmr])
                rsqrt_inplace(r_t, bmr)

            def sample_norm(h_bf, bh, a_t, sh_t, ss):
                sqs_ = sb(ss, "s_sq", [128, 8, NS], BF16); bsqs_ = Buf("s_sq")
                rs_ = sb(ss, "s_rs", [128, NS]); brs_ = Buf("s_rs")
                hf_ = sb(ss, "s_hf", [128, 8, NS]); bhf_ = Buf("s_hf")
                for c in range(8):
                    act(sqs_[:, c, :], xs[:, c, :], AF.Square, [bxs], [bsqs_])
                for c in range(8):
                    mm(ps[0][:, 0:NS], ones_bf[:], sqs_[:, c, :], c == 0, c == 7, [bconst, bsqs_], [bps[0]])
                ts('dve', rs_[:], ps[0][:, 0:NS], 1.0 / D, EPS, ALU.mult, ALU.add, [bps[0]], [brs_])
                rsqrt_inplace(rs_[:], brs_)
                for c in range(8):
                    tt('dve', hf_[:, c, :], xs[:, c, :], rs_[:], ALU.mult, [bxs, brs_], [bhf_])
                tt('dve', hf_[:], hf_[:], a_t, ALU.mult, [bhf_, bmod], [bhf_])
                tt('dve', hf_[:], hf_[:], sh_t, ALU.add, [bhf_, bmod], [bhf_])
                cp('dve', h_bf[:], hf_[:], [bhf_], [bh])

            def sample_phase1(win, bwin, wout, bwout):
                with ExitStack() as ss:
                    hs = sb(ss, "s_hs", [128, 8, NS], BF16); bhs = Buf("s_hs")
                    zs = sb(ss, "s_zs", [128, 8, NS]); bzs = Buf("s_zs")
                    w1 = sb(ss, "s_w1", [128, 8, NS]); bw1 = Buf("s_w1")
                    w2 = sb(ss, "s_w2", [128, 8, NS]); bw2 = Buf("s_w2")
                    w3 = sb(ss, "s_w3", [128, 8, NS]); bw3 = Buf("s_w3")
                    mt = sb(ss, "s_mt", [128, NS]); rt = sb(ss, "s_rt", [128, NS]); bmr = Buf("s_mr")
                    tm = sb(ss, "s_tm", [NS, 3, 512]); btm = Buf("s_tm")
                    lftm = sb(ss, "s_lftm", [NS, 8]); blftm = Buf("s_lftm")
                    ysT = sb(ss, "s_ysT", [128, 8, NS], BF16); bys = Buf("s_ysT")
                    ycs = sb(ss, "s_ycs", [128, 4, NS]); bycs = Buf("s_ycs")
                    cst = sb(ss, "s_cst", [128, 2, NS, 31]); bcst = Buf("s_cst")
                    p31 = sb(ss, "s_p31", [128, 2, NS, 31]); bp31 = Buf("s_p31")
                    Kp = [sb(ss, "s_Kp%d" % i, [128, 512]) for i in range(4)]
                    Vp = [sb(ss, "s_Vp%d" % i, [128, 512]) for i in range(4)]
                    prod = sb(ss, "s_prod", [128, 512]); bprod = Buf("s_prod")
                    qrep = sb(ss, "s_qrep", [128, 512]); bqrep = Buf("s_qrep")
                    krep = sb(ss, "s_krep", [128, 512]); bkrep = Buf("s_krep")
                    vrep = sb(ss, "s_vrep", [128, 512]); bvrep = Buf("s_vrep")
                    LF = sb(ss, "s_LF", [128, NPG, 8])
                    cumi = sb(ss, "s_cumi", [128, NPG + 1, 8]); bcumi = Buf("s_cumi")
                    S_ = sb(ss, "s_S", [128, NPG + 1, 8]); bS = Buf("s_S")
                    Pm = sb(ss, "s_P", [128, NPG + 1, 8]); bPm = Buf("s_P")
                    gs = sb(ss, "s_gs", [128, 8]); bgs = Buf("s_gs")
                    lfrep = sb(ss, "s_lfrep", [128, 8]); blfrep = Buf("s_lfrep")
                    Om = sb(ss, "s_Om", [8, 512]); bOm = Buf("s_Om")
                    den = sb(ss, "s_den", [128, 2]); bden = Buf("s_den")
                    bKV, bLF = bpool[0:4], bpool[4]
                    selt = sb(ss, "selt", [NS, NS, 128])
                    blkm = sb(ss, "blkm", [8, 512])
                    bsel = Buf("selblk")
                    dma(selt[:], sel_d, writes=[bsel], sem_buf=bpar)
                    dma(blkm[:], blkm_d, writes=[bsel], sem_buf=bpar)
                    dma(cst[:, :, :, 0:30], stconv_d[l], writes=[bcst], sem_buf=bpar)
                    fw.barrier()

                    ts('dve', idxl[:], idx0[:], float(l * NPHYS * 128), None, ALU.add, None, [bconst], [bidx])
                    sample_norm(hs, bhs, a1[:, :, sc1], shift1[:, :, sc1], ss)
                    for ch in range(8):
                        for kc in range(8):
                            mm(ps[1][:, ch * NS:(ch + 1) * NS], win[:, kc, ch * 128:(ch + 1) * 128], hs[:, kc, :],
                               kc == 0, kc == 7, [bwin, bhs], [bps[1]])
                    cp('dve', zs[:], ps[1][:, 0:8 * NS].rearrange("p (c n) -> p c n", n=NS), [bps[1]], [bzs])
                    for g_, c0 in enumerate((1024, 1536, 2048)):
                        for kc in range(8):
                            mm(ps[2 + g_][0:NS, :], hs[:, kc, :], win[:, kc, c0:c0 + 512], kc == 0, kc == 7,
                               [bwin, bhs], [bps[2 + g_]])
                        cp('act', tm[:, g_, :], ps[2 + g_][0:NS, :], [bps[2 + g_]], [btm])
                    for kc in range(8):
                        mm(ps[5][0:NS, 0:8], hs[:, kc, :], win[:, kc, 2560:2568], kc == 0, kc == 7, [bwin, bhs], [bps[5]])
                    tt('dve', lftm[:], ps[5][0:NS, 0:8], bfb[0:NS, :], ALU.add, [bps[5], bpar], [blftm])
                    act(lftm[:], lftm[:], AF.Exp, [blftm], [blftm], scale=-1.0)
                    act(lftm[:], lftm[:], AF.Ln, [blftm, bconst], [blftm], bias=onecol[0:NS, 0:1], scale=1.0)
                    ts('dve', lftm[:], lftm[:], -1.0, None, ALU.mult, None, [blftm], [blftm])
                    dma(ks_o[l], tm[:, 1, :], reads=[btm], sem_buf=bmisc)
                    dma(vs_o[l], tm[:, 2, :], reads=[btm], sem_buf=bmisc)
                    dma(lfs_o[l], lftm[:], reads=[blftm], sem_buf=bmisc)

                    act(w1[:, 0:4, :], zs[:, 0:4, :], AF.Gelu, [bzs], [bw1])
                    chan_stats([w1[:, 2, :], w1[:, 3, :]], bw1, w2, bw2, mt[:], rt[:], bmr, 256, True)
                    for cc in range(2):
                        tt('dve', w3[:, cc, :], w1[:, 2 + cc, :], mt[:], ALU.subtract, [bw1, bmr], [bw3])
                        tt('dve', w3[:, cc, :], w3[:, cc, :], rt[:], ALU.mult, [bw3, bmr], [bw3])
                        ts('dve', w3[:, cc, :], w3[:, cc, :], alngT[:, cc:cc + 1], alnbT[:, cc:cc + 1], ALU.mult, ALU.add,
                           [bw3, bpar], [bw3])
                    dma(chv_o[l], w3[:, 0:2, :], reads=[bw3], sem_buf=bmisc)
                    for cc in range(2):
                        ts('dve', w3[:, 2 + cc, :], w3[:, cc, :], ws00T[:, cc:cc + 1], bs0T[:, cc:cc + 1], ALU.mult, ALU.add,
                           [bw3, bpar], [bw3])
                        tt('dve', w3[:, 2 + cc, :], w3[:, 2 + cc, :], w1[:, cc, :], ALU.mult, [bw3, bw1], [bw3])
                    chan_stats([w3[:, 2, :], w3[:, 3, :]], bw3, w2, bw2, mt[:], rt[:], bmr, 256, False)
                    for cc in range(2):
                        tt('dve', ysT[:, cc, :], w3[:, 2 + cc, :], rt[:], ALU.mult, [bw3, bmr], [bys])

                    act(w1[:, 4:6, :], zs[:, 6:8, :], AF.Sigmoid, [bzs], [bw1])
                    for cc in range(2):
                        tt('dve', cst[:, cc, :, 30], zs[:, 4 + cc, :], w1[:, 4 + cc, :], ALU.mult, [bzs, bw1], [bcst])
                    dma(convs_o[l], cst[:, :, :, 1:31], reads=[bcst], sem_buf=bmisc)
                    for cc in range(2):
                        for n in range(NS):
                            tt('dve', p31[:, cc, n, :], cst[:, cc, n, :], cwT[:, cc, :], ALU.mult, [bcst, bpar], [bp31])
                        op('dve', lambda e: e.reduce_sum(w3[:, 4 + cc, :], p31[:, cc, :, :], mybir.AxisListType.X),
                           [bp31], [bw3])
                        ts('dve', w3[:, 4 + cc, :], w3[:, 4 + cc, :], cbT[:, cc:cc + 1], None, ALU.add, None, [bw3, bpar], [bw3])
                    chan_stats([w3[:, 4, :], w3[:, 5, :]], bw3, w2, bw2, mt[:], rt[:], bmr, 256, True)
                    for cc in range(2):
                        tt('dve', w3[:, 4 + cc, :], w3[:, 4 + cc, :], mt[:], ALU.subtract, [bw3, bmr], [bw3])
                        tt('dve', w3[:, 4 + cc, :], w3[:, 4 + cc, :], rt[:], ALU.mult, [bw3, bmr], [bw3])
                        act(w3[:, 4 + cc, :], w3[:, 4 + cc, :], AF.Silu, [bw3, bpar], [bw3],
                            scale=clgT[:, cc:cc + 1], bias=clbT[:, cc:cc + 1])
                    chan_stats([w3[:, 4, :], w3[:, 5, :]], bw3, w2, bw2, mt[:], rt[:], bmr, 256, False)
                    for cc in range(2):
                        tt('dve', ysT[:, 2 + cc, :], w3[:, 4 + cc, :], rt[:], ALU.mult, [bw3, bmr], [bys])

                    it = 0
                    for n in range(NS):
                        col = n * NPG
                        mm(ps[7][:, 0:8], selt[:, n, :], lftm[:], True, True, [bsel, blftm], [bps[7]])
                        cp('dve', lfrep[:], ps[7][:, 0:8], [bps[7]], [blfrep])
                        mm(ps[2][:], selt[:, n, :], tm[:, 0, :], True, True, [bsel, btm], [bps[2]])
                        act(qrep[:], ps[2][:], AF.Identity, [bps[2]], [bqrep], scale=0.125, bias=0.0)
                        mm(ps[3][:], selt[:, n, :], tm[:, 1, :], True, True, [bsel, btm], [bps[3]])
                        cp('act', krep[:], ps[3][:], [bps[3]], [bkrep])
                        mm(ps[4][:], selt[:, n, :], tm[:, 2, :], True, True, [bsel, btm], [bps[4]])
                        cp('act', vrep[:], ps[4][:], [bps[4]], [bvrep])
                        op('pool', lambda e: e.memset(gs[:], 0.0), [], [bgs])
                        for pg in range(NPG):
                            idma(LF[:, pg, :], cf_d, idxl[:, col + pg:col + pg + 1], [bidx], [bLF], bLF)
                        for pg in range(NPG):
                            mm(ps[7][:, 16:24], tri_f[:], LF[:, pg, :], True, True, [bconst, bLF], [bps[7]])
                            mm(ps[7][:, 32:40], ones_f[:], LF[:, pg, :], True, True, [bconst, bLF], [bps[7]])
                            tt('dve', cumi[:, pg, :], ps[7][:, 16:24], gs[:], ALU.add, [bps[7], bgs], [bcumi])
                            tt('dve', gs[:], ps[7][:, 32:40], gs[:], ALU.add, [bps[7], bgs], [bgs])
                        tt('dve', gs[:], gs[:], lfrep[:], ALU.add, [bgs, blfrep], [bgs])
                        for pg in range(NPG):
                            tt('dve', cumi[:, pg, :], gs[:], cumi[:, pg, :], ALU.subtract, [bgs, bcumi], [bcumi])
                        cp('dve', cumi[:, NPG, :], bigm[:], [bconst], [bcumi])
                        for pg in range(NPG + 1):
                            sl = it % 4
                            it += 1
                            if pg < NPG:
                                idma(Kp[sl][:], ck_d, idxl[:, col + pg:col + pg + 1], [bidx], [bKV[sl]], bKV[sl])
                                idma(Vp[sl][:], cv_d, idxl[:, col + pg:col + pg + 1], [bidx], [bKV[sl]], bKV[sl])
                                k_ap, bk_, v_ap, bv_ = Kp[sl], bKV[sl], Vp[sl], bKV[sl]
                            else:
                                k_ap, bk_, v_ap, bv_ = krep, bkrep, vrep, bvrep
                            tt('dve', prod[:], k_ap[:], qrep[:], ALU.mult, [bk_, bqrep], [bprod])
                            op('dve', lambda e: e.reduce_sum(S_[:, pg, :], prod[:].rearrange("p (h d) -> p h d", d=64),
                                                             mybir.AxisListType.X), [bprod], [bS])
                            tt('dve', S_[:, pg, :], S_[:, pg, :], cumi[:, pg, :], ALU.add, [bS, bcumi], [bS])
                            act(Pm[:, pg, :], S_[:, pg, :], AF.Exp, [bS], [bPm])
                            mm(ps[5][0:8, :], Pm[:, pg, :], v_ap[:], pg == 0, pg == NPG, [bPm, bv_], [bps[5]])
                        op('dve', lambda e: e.reduce_sum(prod[:, 0:8], Pm[:].rearrange("p g h -> p h g"),
                                                         mybir.AxisListType.X), [bPm], [bprod])
                        mm(ps[7][0:8, 48:49], prod[:, 0:8], ones_f[:, 0:1], True, True, [bprod, bconst], [bps[7]])
                        op('dve', lambda e: e.reciprocal(den[0:8, 0:1], ps[7][0:8, 48:49]), [bps[7]], [bden])
                        stt('dve', Om[:], ps[5][0:8, :], den[0:8, 0:1], blkm[:], ALU.mult, ALU.mult,
                            [bps[5], bden, bsel], [bOm])
                        for c in range(4):
                            mm(ps[6][:, 16 + c:17 + c], Om[:, c * 128:(c + 1) * 128], ones_f[0:8, 0:1], True, True,
                               [bOm, bconst], [bps[6]])
                        cp('dve', ycs[:, :, n], ps[6][:, 16:20], [bps[6]], [bycs])
                    chan_stats([ycs[:, c, :] for c in range(4)], bycs, w2, bw2, mt[:], rt[:], bmr, 512, False)
                    for c in range(4):
                        tt('dve', ysT[:, 4 + c, :], ycs[:, c, :], rt[:], ALU.mult, [bycs, bmr], [bys])

                    for dc in range(8):
                        for kc in range(8):
                            mm(ps[1][:, dc * NS:(dc + 1) * NS], wout[:, kc, dc * 128:(dc + 1) * 128], ysT[:, kc, :],
                               kc == 0, kc == 7, [bwout, bys], [bps[1]])
                    tt('dve', w1[:], ps[1][:, 0:8 * NS].rearrange("p (c n) -> p c n", n=NS), gate1[:, :, sc1], ALU.mult,
                       [bps[1], bmod], [bw1])
                    tt('dve', xs[:], xs[:], w1[:], ALU.add, [bxs, bw1], [bxs])
                    fw.barrier()

            def sample_phase2(wup, bwup, wdn, bwdn):
                with ExitStack() as ss:
                    hs = sb(ss, "t_hs", [128, 8, NS], BF16); bhs = Buf("t_hs")
                    raw_s = sb(ss, "t_raw", [128, 44, NS]); braw_s = Buf("t_raw")
                    sff = sb(ss, "t_sff", [128, 44, NS, 2]); bsff = Buf("t_sff")
                    fo = sb(ss, "t_fo", [128, 44, NS, 2]); bfo = Buf("t_fo")
                    up = sb(ss, "t_up", [128, 44, NS]); bup = Buf("t_up")
                    tmp = sb(ss, "t_tmp", [128, 44, NS]); btmp_ = Buf("t_tmp")
                    hm = sb(ss, "t_hm", [128, 22, NS], BF16); bhm = Buf("t_hm")
                    w1 = sb(ss, "t_w1", [128, 8, NS]); bw1 = Buf("t_w1")
                    dma(sff[:], sffn_d[l], writes=[bsff], sem_buf=bpar)
                    fw.barrier()
                    sample_norm(hs, bhs, a2[:, :, sc1], shift2[:, :, sc1], ss)
                    for ch in range(44):
                        k_ = 1 + (ch // 22)
                        cc_ = ch % 22
                        for kc in range(8):
                            mm(ps[k_][:, cc_ * NS:(cc_ + 1) * NS], wup[:, kc, ch * 128:(ch + 1) * 128], hs[:, kc, :],
                               kc == 0, kc == 7, [bwup, bhs], [bps[k_]])
                    for hf in range(2):
                        cp('dve', raw_s[:, hf * 22:(hf + 1) * 22, :],
                           ps[1 + hf][:, 0:22 * NS].rearrange("p (c n) -> p c n", n=NS), [bps[1 + hf]], [braw_s])
                    for n in range(NS):
                        tt('dve', up[:, :, n], sff[:, :, n, 0], fwT[:, :, 0], ALU.mult, [bsff, bpar], [bup])
                        tt('dve', tmp[:, :, n], sff[:, :, n, 1], fwT[:, :, 1], ALU.mult, [bsff, bpar], [btmp_])
                        tt('dve', up[:, :, n], up[:, :, n], tmp[:, :, n], ALU.add, [bup, btmp_], [bup])
                        tt('dve', tmp[:, :, n], raw_s[:, :, n], fwT[:, :, 2], ALU.mult, [braw_s, bpar], [btmp_])
                        tt('dve', up[:, :, n], up[:, :, n], tmp[:, :, n], ALU.add, [bup, btmp_], [bup])
                        tt('dve', up[:, :, n], up[:, :, n], fbT[:], ALU.add, [bup, bpar], [bup])
                        cp('dve', fo[:, :, n, 0], sff[:, :, n, 1], [bsff], [bfo])
                        cp('dve', fo[:, :, n, 1], raw_s[:, :, n], [braw_s], [bfo])
                    dma(ffns_o[l], fo[:], reads=[bfo], sem_buf=bmisc)
                    act(tmp[:, 0:22, :], up[:, 0:22, :], AF.Silu, [bup], [btmp_])
                    tt('dve', hm[:], tmp[:, 0:22, :], up[:, 22:44, :], ALU.mult, [btmp_, bup], [bhm])
                    for dc in range(8):
                        for j in range(22):
                            mm(ps[3][:, dc * NS:(dc + 1) * NS], wdn[:, j, dc * 128:(dc + 1) * 128], hm[:, j, :],
                               j == 0, j == 21, [bwdn, bhm], [bps[3]])
                    tt('dve', w1[:], ps[3][:, 0:8 * NS].rearrange("p (c n) -> p c n", n=NS), gate2[:, :, sc1], ALU.mult,
                       [bps[3], bmod], [bw1])
                    tt('dve', xs[:], xs[:], w1[:], ALU.add, [bxs, bw1], [bxs])
                    if last:
                        sqs_ = sb(ss, "t_sq", [128, 8, NS], BF16); bsqs_ = Buf("t_sq")
                        rs_ = sb(ss, "t_rs", [128, NS]); brs_ = Buf("t_rs")
                        for c in range(8):
                            act(sqs_[:, c, :], xs[:, c, :], AF.Square, [bxs], [bsqs_])
                        for c in range(8):
                            mm(ps[0][:, 0:NS], ones_bf[:], sqs_[:, c, :], c == 0, c == 7, [bconst, bsqs_], [bps[0]])
                        ts('dve', rs_[:], ps[0][:, 0:NS], 1.0 / D, EPS, ALU.mult, ALU.add, [bps[0]], [brs_])
                        rsqrt_inplace(rs_[:], brs_)
                        for c in range(8):
                            stt('dve', w1[:, c, :], xs[:, c, :], fgT[:, c:c + 1], rs_[:], ALU.mult, ALU.mult,
                                [bxs, bpar, brs_], [bw1])
                        dma(ysT_o, w1[:], reads=[bw1], sem_buf=bmisc)
                    fw.barrier()

            with ExitStack() as big:
                KT = sb(big, "KT", [128, 4, T], BF16); bKT = Buf("KT")
                Vst = sb(big, "Vst", [128, NKB, 512], BF16); bVst = Buf("Vst")
                win = sb(big, "win", [128, 8, INW], BF16); bwin = Buf("win")
                wout = sb(big, "wout", [128, 8, D], BF16); bwout = Buf("wout")
                with ExitStack() as st:
                    stg = [sb(st, "stgB%d" % i, [128, INW]) for i in range(2)]
                    engs = ['dve', 'pool']
                    n = 0
                    for kc in range(8):
                        s_ = n % 2
                        dma(stg[s_][:, 0:INW], w_in_d[l, kc * 128:(kc + 1) * 128, :], writes=[bstg[s_]])
                        cp(engs[n % 2], win[:, kc, :], stg[s_][:, 0:INW], [bstg[s_]], [bwin])
                        n += 1
                    for kc in range(8):
                        s_ = n % 2
                        dma(stg[s_][:, 0:D], w_out_d[l, kc * 128:(kc + 1) * 128, :], writes=[bstg[s_]])
                        ts(engs[n % 2], wout[:, kc, :], stg[s_][:, 0:D], mixgT[:, kc:kc + 1], None, ALU.mult, None,
                           [bstg[s_], bpar], [bwout])
                        n += 1
                    fw.barrier()
                    stop(2)
                sample_phase1(win, bwin, wout, bwout)

                with ExitStack() as sc:
                    hT = sb(sc, "hT", [128, 8, 512], BF16); bhT = Buf("hT")
                    qT = sb(sc, "qT", [128, 4, 512], BF16); bqT = Buf("qT")
                    yT = sb(sc, "yT", [128, 8, 512], BF16)
                    byA, byB, byC = Buf("yA"), Buf("yB"), Buf("yC")
                    glu = sb(sc, "glu", [128, 2, 542]); bglu = Buf("glu")
                    cacc = sb(sc, "cacc", [128, 2, 512]); bcacc = Buf("cacc")
                    za = sb(sc, "za", [128, 512]); bza = Buf("za")
                    tA = sb(sc, "tA", [128, 256]); btA = Buf("tA")
                    tA2 = sb(sc, "tA2", [128, 256]); btA2 = Buf("tA2")
                    vln = sb(sc, "vln", [128, 256], BF16); bvln = Buf("vln")
                    yan = sb(sc, "yan", [128, 256], BF16); byan = Buf("yan")
                    PT = [sb(sc, "PT%d" % i, [128, 512], BF16) for i in range(2)]
                    bPT = [Buf("PT%d" % i) for i in range(2)]
                    sq = [sb(sc, "sq%d" % i, [128, 512], BF16) for i in range(2)]
                    bsq = [Buf("sq%d" % i) for i in range(2)]
                    st0 = sb(sc, "st0", [128, 512]); bst0 = Buf("st0")
                    st1 = sb(sc, "st1", [128, 512]); bst1 = Buf("st1")
                    st2 = sb(sc, "st2", [128, 512]); bst2 = Buf("st2")
                    kst = sb(sc, "kst", [128, 512]); bkst = Buf("kst")
                    vst = sb(sc, "vst", [128, 512]); bvst = Buf("vst")
                    lft = sb(sc, "lft", [128, 8]); blft = Buf("lft")
                    lfs = sb(sc, "lfs", [128, 8]); blfs = Buf("lfs")
                    sml = sb(sc, "sml", [128, 16]); bsml = Buf("sml")
                    biasT = sb(sc, "biasT", [128, NKB, 8]); bbias = Buf("biasT")
                    alng = sb(sc, "alng", [128, 256]); alnb = sb(sc, "alnb", [128, 256])
                    dma(alng[:], alng_d[l], writes=[bpar])
                    dma(alnb[:], alnb_d[l], writes=[bpar])
                    fw.barrier()

                    op('pool', lambda e: e.memset(gtot[:], 0.0), [], [bgtot])
                    op('pool', lambda e: e.memset(glu[:, :, 0:30], 0.0), [], [bglu])

                    for i in range(NT):
                        t0 = i * 512
                        dma(x[:], x_srcv[:, :, t0:t0 + 512], writes=[bx])
                        rms_rstd(lambda c: x[:, c, :], bx, st0[:], bst0, sq, bsq, 8, 512, D, ps[0], bps[0])
                        for c in range(8):
                            tmp, btmp = (st1, bst1) if c % 2 == 0 else (st2, bst2)
                            stt('dve', tmp[:], x[:, c, :], a1[:, c, 0:1], st0[:], ALU.mult, ALU.mult,
                                [bx, bmod, bst0], [btmp])
                            act(hT[:, c, :], tmp[:], AF.Identity, [btmp, bmod], [bhT], bias=shift1[:, c, 0:1], scale=1.0)

                        stop(31)
                        pi = [0]

                        def proj_fm(col0):
                            k_ = 1 + (pi[0] % 2)
                            pi[0] += 1
                            for kc in range(8):
                                mm(ps[k_][:], win[:, kc, col0:col0 + 128], hT[:, kc, :], kc == 0, kc == 7,
                                   [bwin, bhT], [bps[k_]], lazy=True)
                            return ps[k_], bps[k_]

                        for cc in range(2):
                            pg_, bpg_ = proj_fm(512 + 256 + cc * 128)
                            act(st1[:], pg_[:], AF.Sigmoid, [bpg_], [bst1])
                            stop(311)
                            pa_, bpa_ = proj_fm(512 + cc * 128)
                            tt('dve', glu[:, cc, 30:542], pa_[:], st1[:], ALU.mult, [bpa_, bst1], [bglu])
                            stop(312)
                        for j in range(4):
                            pq_, bpq_ = proj_fm(1024 + j * 128)
                            stop(313)
                            cp('act', qT[:, j, :], pq_[:], [bpq_], [bqT])
                            stop(314)
                            pk_, bpk_ = proj_fm(1536 + j * 128)
                            stop(315)
                            cp('dve', KT[:, j, t0:t0 + 512], pk_[:], [bpk_], [bKT])
                            stop(316)
                            stop(320 + j)

                        stop(32)
                        for s in range(4):
                            kb = 4 * i + s
                            tok = slice(s * 128, (s + 1) * 128)

                            def proj_tm(col0, ncol, k_):
                                for kc in range(8):
                                    mm(ps[k_][:, 0:ncol], hT[:, kc, tok], win[:, kc, col0:col0 + ncol], kc == 0, kc == 7,
                                       [bwin, bhT], [bps[k_]], lazy=True)
                                return ps[k_], bps[k_]

                            pk_, bpk_ = proj_tm(1536, 512, 3)
                            stop(3301)
                            cp('act', kst[:], pk_[:], [bpk_], [bkst])
                            stop(3302)
                            dma(k_o[l, t0 + s * 128:t0 + (s + 1) * 128, :], kst[:], reads=[bkst])
                            stop(331)
                            pv_, bpv_ = proj_tm(2048, 512, 4)
                            cp('act', vst[:], pv_[:], [bpv_], [bvst])
                            cp('dve', Vst[:, kb, :], vst[:], [bvst], [bVst])
                            dma(v_o[l, t0 + s * 128:t0 + (s + 1) * 128, :], vst[:], reads=[bvst])
                            stop(332)
                            pf_, bpf_ = proj_tm(2560, 8, 5)
                            tt('dve', lft[:], pf_[:, 0:8], bfb[:], ALU.add, [bpf_, bpar], [blft])
                            act(lft[:], lft[:], AF.Exp, [blft], [blft], scale=-1.0)
                            act(lft[:], lft[:], AF.Ln, [blft, bconst], [blft], bias=onecol[:, 0:1], scale=1.0)
                            ts('dve', lfs[:], lft[:], -1.0, None, ALU.mult, None, [blft], [blfs])
                            dma(lf_o[l, t0 + s * 128:t0 + (s + 1) * 128, :], lfs[:], reads=[blfs])
                            stop(333)
                            mm(ps[5][:, 16:24], tri_f[:], lfs[:], True, True, [bconst, blfs], [bps[5]])
                            mm(ps[5][:, 32:40], ones_f[:], lfs[:], True, True, [bconst, blfs], [bps[5]])
                            tt('dve', cum[:, kb, :], ps[5][:, 16:24], gtot[:], ALU.add, [bps[5], bgtot], [bcum])
                            tt('dve', gtot[:], ps[5][:, 32:40], gtot[:], ALU.add, [bps[5], bgtot], [bgtot])
                            stop(334)

                            pa_, bpa_ = proj_tm(0, 512, 6)
                            act(za[:], pa_[:], AF.Gelu, [bpa_], [bza])
                            stop(335)
                            op('dve', lambda e: e.bn_stats(sml[:, 0:6], za[:, 256:512]), [bza], [bsml])
                            op('dve', lambda e: e.bn_aggr(sml[:, 8:10], sml[:, 0:6]), [bsml], [bsml])
                            ts('dve', sml[:, 9:10], sml[:, 9:10], EPS, None, ALU.add, None, [bsml], [bsml])
                            rsqrt_inplace(sml[:, 9:10], bsml)
                            stop(336)
                            ts('dve', tA[:], za[:, 256:512], sml[:, 8:9], sml[:, 9:10], ALU.subtract, ALU.mult,
                               [bza, bsml], [btA])
                            tt('dve', tA[:], tA[:], alng[:], ALU.mult, [btA, bpar], [btA])
                            tt('dve', vln[:], tA[:], alnb[:], ALU.add, [btA, bpar], [bvln])
                            stop(337)
                            for h in range(4):
                                mm(ps[7][:, h * 64:(h + 1) * 64], wsT_bf[:, h, :], vln[:, h * 64:(h + 1) * 64], True, True,
                                   [bpar, bvln], [bps[7]])
                            for h in range(4):
                                stt('dve', tA2[:, h * 64:(h + 1) * 64], ps[7][:, h * 64:(h + 1) * 64], bsT[:, h:h + 1],
                                    za[:, h * 64:(h + 1) * 64], ALU.add, ALU.mult, [bps[7], bpar, bza], [btA2])
                            tt('dve', tA[:], tA2[:], tA2[:], ALU.mult, [btA2], [btA])
                            op('dve', lambda e: e.reduce_sum(sml[:, 12:13], tA[:], mybir.AxisListType.X), [btA], [bsml])
                            ts('dve', sml[:, 12:13], sml[:, 12:13], 1.0 / 256, EPS, ALU.mult, ALU.add, [bsml], [bsml])
                            rsqrt_inplace(sml[:, 12:13], bsml)
                            stop(338)
                            ts('dve', yan[:], tA2[:], sml[:, 12:13], None, ALU.mult, None, [btA2, bsml], [byan])
                            stop(339)
                            for cA in range(2):
                                mm(ps[7][:, 256 + cA * 128:256 + (cA + 1) * 128], yan[:, cA * 128:(cA + 1) * 128], ident_bf[:],
                                   True, True, [byan, bconst], [bps[7]])
                                cp('act', yT[:, cA, tok], ps[7][:, 256 + cA * 128:256 + (cA + 1) * 128], [bps[7]], [byA])

                        stop(33)
                        for cc in range(2):
                            ts('dve', cacc[:, cc, :], glu[:, cc, 0:512], cwT[:, cc, 0:1], None, ALU.mult, None,
                               [bglu, bpar], [bcacc])
                            for j in range(1, 31):
                                stt('dve', cacc[:, cc, :], glu[:, cc, j:j + 512], cwT[:, cc, j:j + 1], cacc[:, cc, :],
                                    ALU.mult, ALU.add, [bglu, bpar, bcacc], [bcacc])
                            act(cacc[:, cc, :], cacc[:, cc, :], AF.Identity, [bcacc, bpar], [bcacc],
                                bias=cbT[:, cc:cc + 1], scale=1.0)
                        if i == NT - 1:
                            dma(convp_o[l].rearrange("(c p) j -> p c j", p=128), glu[:, :, 512:542], reads=[bglu], sem_buf=bmisc)
                        for cc in range(2):
                            mm(ps[1][:], ones_f[:], cacc[:, cc, :], cc == 0, cc == 1, [bconst, bcacc], [bps[1]])
                        for cc in range(2):
                            act(st1[:], cacc[:, cc, :], AF.Square, [bcacc], [bst1])
                            mm(ps[2][:], ones_f[:], st1[:], cc == 0, cc == 1, [bconst, bst1], [bps[2]])
                        ts('dve', st0[:], ps[1][:], 1.0 / 256, None, ALU.mult, None, [bps[1]], [bst0])
                        tt('dve', st2[:], st0[:], st0[:], ALU.mult, [bst0], [bst2])
                        stt('dve', st2[:], ps[2][:], 1.0 / 256, st2[:], ALU.mult, ALU.subtract, [bps[2], bst2], [bst2])
                        ts('dve', st2[:], st2[:], EPS, None, ALU.add, None, [bst2], [bst2])
                        rsqrt_inplace(st2[:], bst2)
                        for cc in range(2):
                            tt('dve', cacc[:, cc, :], cacc[:, cc, :], st0[:], ALU.subtract, [bcacc, bst0], [bcacc])
                            tt('dve', cacc[:, cc, :], cacc[:, cc, :], st2[:], ALU.mult, [bcacc, bst2], [bcacc])
                            act(cacc[:, cc, :], cacc[:, cc, :], AF.Silu, [bcacc, bpar], [bcacc],
                                scale=clgT[:, cc:cc + 1], bias=clbT[:, cc:cc + 1])
                        rms_rstd(lambda c: cacc[:, c, :], bcacc, st1[:], bst1, sq, bsq, 2, 512, 256, ps[1], bps[1])
                        for cc in range(2):
                            tt('dve', yT[:, 2 + cc, :], cacc[:, cc, :], st1[:], ALU.mult, [bcacc, bst1], [byB])
                        for cc in range(2):
                            cp('pool', glu[:, cc, 0:30], glu[:, cc, 512:542], [bglu], [bglu])

                        stop(34)
                        nkb = 4 * i + 4
                        for kb in range(nkb):
                            tt('dve', biasT[:, kb, :], gtot[:], cum[:, kb, :], ALU.subtract, [bgtot, bcum], [bbias])
                        for h in range(8):
                            c, pb = h // 2, 64 * (h % 2)
                            if h % 2 == 0:
                                pO, bpO, pD, bpD = ps[5], bps[5], ps[6], bps[6]
                            else:
                                pO, bpO, pD, bpD = ps[7], bps[7], ps[0], bps[0]

                            def issue_S(kb):
                                j = kb - 4 * i
                                col0 = 0 if j <= 0 else 128 * j
                                ncols = 512 - col0
                                sl_ = kb % 2
                                pS, bpS = ps[3 + sl_], bps[3 + sl_]
                                P_, bP_ = PT[sl_], bPT[sl_]
                                mm(pS[:, 0:ncols], KT[pb:pb + 64, c, kb * 128:(kb + 1) * 128], qT[pb:pb + 64, c, col0:512],
                                   True, True, [bKT, bqT], [bpS])
                                act(P_[:, 0:ncols], pS[:, 0:ncols], AF.Exp, [bpS, bbias], [bP_],
                                    scale=0.125, bias=biasT[:, kb, h:h + 1])
                                if j >= 0:
                                    tt('pool', P_[:, 0:128], P_[:, 0:128], mask_bf[:], ALU.mult, [bP_, bconst], [bP_])
                                return P_, bP_, col0, ncols

                            cur = issue_S(0)
                            for kb in range(nkb):
                                nxt = issue_S(kb + 1) if kb + 1 < nkb else None
                                P_, bP_, col0, ncols = cur
                                mm(pO[:, col0:512], Vst[:, kb, c * 128:(c + 1) * 128], P_[:, 0:ncols], kb == 0, kb == nkb - 1,
                                   [bVst, bP_], [bpO])
                                mm(pD[:, col0:512], ones_bf[:], P_[:, 0:ncols], kb == 0, kb == nkb - 1,
                                   [bconst, bP_], [bpD])
                                cur = nxt
                            op('dve', lambda e: e.reciprocal(st0[pb:pb + 64, :], pD[pb:pb + 64, :]), [bpD], [bst0])
                            tt('dve', yT[pb:pb + 64, 4 + c, :], pO[pb:pb + 64, :], st0[pb:pb + 64, :], ALU.mult,
                               [bpO, bst0], [byC])
                        rms_rstd(lambda c: yT[:, 4 + c, :], byC, st1[:], bst1, sq, bsq, 4, 512, 512, ps[0], bps[0])
                        for c in range(4):
                            tt('dve', yT[:, 4 + c, :], yT[:, 4 + c, :], st1[:], ALU.mult, [byC, bst1], [byC])

                        stop(35)
                        for dc in range(8):
                            k_ = 1 + (dc % 2)
                            for kc in range(8):
                                mm(ps[k_][:], wout[:, kc, dc * 128:(dc + 1) * 128], yT[:, kc, :], kc == 0, kc == 7,
                                   [bwout, byA, byB, byC], [bps[k_]], lazy=True)
                            stt('dve', x[:, dc, :], ps[k_][:], gate1[:, dc, 0:1], x[:, dc, :], ALU.mult, ALU.add,
                                [bps[k_], bmod, bx], [bx])
                        dma(xsv[:, :, t0:t0 + 512], x[:], reads=[bx])
                        stop(3)
                    fw.barrier()

            with ExitStack() as big:
                wup = sb(big, "wup", [128, 8, 2 * DFF], BF16); bwup = Buf("wup")
                wdn = sb(big, "wdn", [128, 22, D], BF16); bwdn = Buf("wdn")
                with ExitStack() as st:
                    stg = [sb(st, "stgC%d" % i, [128, DFF]) for i in range(2)]
                    engs = ['dve', 'pool']
                    n = 0
                    for kc in range(8):
                        for hf in range(2):
                            s_ = n % 2
                            dma(stg[s_][:, 0:DFF], w_up_d[l, kc * 128:(kc + 1) * 128, hf * DFF:(hf + 1) * DFF],
                                writes=[bstg[s_]])
                            cp(engs[n % 2], wup[:, kc, hf * DFF:(hf + 1) * DFF], stg[s_][:, 0:DFF], [bstg[s_]], [bwup])
                            n += 1
                    for j in range(22):
                        s_ = n % 2
                        dma(stg[s_][:, 0:D], w_down_d[l, j * 128:(j + 1) * 128, :], writes=[bstg[s_]])
                        cp(engs[n % 2], wdn[:, j, :], stg[s_][:, 0:D], [bstg[s_]], [bwdn])
                        n += 1
                    fw.barrier()
                    stop(4)
                sample_phase2(wup, bwup, wdn, bwdn)

                with ExitStack() as sc:
                    hT = sb(sc, "hT2", [128, 8, 512], BF16); bhT = Buf("hT2")
                    hmid = sb(sc, "hmid", [128, 22, 512], BF16); bhmid = Buf("hmid")
                    raw = [sb(sc, "raw%d" % i, [128, 514]) for i in range(4)]
                    braw = [Buf("raw%d" % i) for i in range(4)]
                    acc = [sb(sc, "acc%d" % i, [128, 512]) for i in range(4)]
                    bacc = [Buf("acc%d" % i) for i in range(4)]
                    sq = [sb(sc, "sq2%d" % i, [128, 512], BF16) for i in range(2)]
                    bsq = [Buf("sq2%d" % i) for i in range(2)]
                    st0 = sb(sc, "st02", [128, 512]); bst0 = Buf("st02")
                    st1, bst1, st2, bst2 = acc[0], bacc[0], acc[1], bacc[1]

                    op('pool', lambda e: e.memset(halo[:], 0.0), [], [bhalo])
                    for i in range(NT):
                        t0 = i * 512
                        dma(x[:], xsv[:, :, t0:t0 + 512], writes=[bx])
                        rms_rstd(lambda c: x[:, c, :], bx, st0[:], bst0, sq, bsq, 8, 512, D, ps[0], bps[0])
                        for c in range(8):
                            tmp, btmp = (st1, bst1) if c % 2 == 0 else (st2, bst2)
                            stt('dve', tmp[:], x[:, c, :], a2[:, c, 0:1], st0[:], ALU.mult, ALU.mult,
                                [bx, bmod, bst0], [btmp])
                            act(hT[:, c, :], tmp[:], AF.Identity, [btmp, bmod], [bhT], bias=shift2[:, c, 0:1], scale=1.0)
                        for j in range(22):
                            for hf in range(2):
                                ch = hf * 22 + j
                                k_ = 1 + hf + 2 * (j % 2)
                                for kc in range(8):
                                    mm(ps[k_][:], wup[:, kc, ch * 128:(ch + 1) * 128], hT[:, kc, :], kc == 0, kc == 7,
                                       [bwup, bhT], [bps[k_]], lazy=True)
                                r_, br_ = raw[hf + 2 * (j % 2)], braw[hf + 2 * (j % 2)]
                                a_, ba_ = acc[hf + 2 * (j % 2)], bacc[hf + 2 * (j % 2)]
                                cp('act', r_[:, 2:514], ps[k_][:], [bps[k_]], [br_])
                                cp('pool', r_[:, 0:2], halo[:, ch, :], [bhalo], [br_])
                                act(a_[:], ps[k_][:], AF.Identity, [bps[k_], bpar], [ba_],
                                    scale=fwT[:, ch, 2:3], bias=fbT[:, ch:ch + 1])
                                stt('dve', a_[:], r_[:, 0:512], fwT[:, ch, 0:1], a_[:], ALU.mult, ALU.add,
                                    [br_, bpar, ba_], [ba_])
                                stt('dve', a_[:], r_[:, 1:513], fwT[:, ch, 1:2], a_[:], ALU.mult, ALU.add,
                                    [br_, bpar, ba_], [ba_])
                                cp('pool', halo[:, ch, :], r_[:, 512:514], [br_], [bhalo])
                            ag_, bag_ = acc[2 * (j % 2)], bacc[2 * (j % 2)]
                            au_, bau_ = acc[1 + 2 * (j % 2)], bacc[1 + 2 * (j % 2)]
                            act(ag_[:], ag_[:], AF.Silu, [bag_], [bag_])
                            tt('dve', hmid[:, j, :], ag_[:], au_[:], ALU.mult, [bag_, bau_], [bhmid])
                        if i == NT - 1:
                            dma(ffnp_o[l].rearrange("(c p) k -> p c k", p=128), halo[:], reads=[bhalo], sem_buf=bmisc)
                        for dc in range(8):
                            k_ = 5 + (dc % 2)
                            for j in range(22):
                                mm(ps[k_][:], wdn[:, j, dc * 128:(dc + 1) * 128], hmid[:, j, :], j == 0, j == 21,
                                   [bwdn, bhmid], [bps[k_]], lazy=True)
                            stt('dve', x[:, dc, :], ps[k_][:], gate2[:, dc, 0:1], x[:, dc, :], ALU.mult, ALU.add,
                                [bps[k_], bmod, bx], [bx])
                        if not last:
                            dma(xsv[:, :, t0:t0 + 512], x[:], reads=[bx])
                        else:
                            rms_rstd(lambda c: x[:, c, :], bx, st0[:], bst0, sq, bsq, 8, 512, D, ps[0], bps[0])
                            for c in range(8):
                                stt('dve', x[:, c, :], x[:, c, :], fgT[:, c:c + 1], st0[:], ALU.mult, ALU.mult,
                                    [bx, bpar, bst0], [bx])
                            dma(yT_o.rearrange("(c p) t -> p c t", p=128)[:, :, t0:t0 + 512], x[:], reads=[bx])
                    fw.barrier()
        fw.barrier()
    fw.barrier()
    fw.close()
    return nc


_NC_CACHE = {}


def _host_inputs(inp):
    f = np.float32
    A = lambda a: np.ascontiguousarray(np.asarray(a), dtype=f)
    shared = {}
    shared["w_ada"] = A(inp["w_ada"])
    shared["badaT"] = A(np.asarray(inp["b_ada"]).reshape(NL, 48, 128).transpose(0, 2, 1))
    rep = lambda g: A(np.broadcast_to(np.asarray(g).reshape(NL, 8, 128).transpose(0, 2, 1)[..., None], (NL, 128, 8, 1 + NS)))
    shared["g1r"] = rep(inp["norm1_g"])
    shared["g2r"] = rep(inp["norm2_g"])
    shared["mixgT"] = A(np.asarray(inp["mix_g"]).reshape(NL, 8, 128).transpose(0, 2, 1))
    shared["fgT"] = A(np.asarray(inp["final_g"]).reshape(8, 128).T)
    for k in ("w_in", "w_out", "w_up", "w_down"):
        shared[k] = A(inp[k])
    shared["bfb"] = A(np.broadcast_to(np.asarray(inp["b_forget"])[:, None, :], (NL, 128, 8)))
    shared["alng"] = A(np.broadcast_to(np.asarray(inp["a_ln_g"])[:, None, :], (NL, 128, 256)))
    shared["alnb"] = A(np.broadcast_to(np.asarray(inp["a_ln_b"])[:, None, :], (NL, 128, 256)))
    shared["wsT"] = A(np.asarray(inp["w_s"]).transpose(0, 3, 1, 2))
    shared["bsT"] = A(np.asarray(inp["b_s"]).transpose(0, 2, 1))
    shared["cwT"] = A(np.asarray(inp["conv_w"]).transpose(0, 2, 1).reshape(NL, 2, 128, 31).transpose(0, 2, 1, 3))
    cm = lambda a: A(np.asarray(a).reshape(NL, 2, 128).transpose(0, 2, 1))
    shared["cbT"] = cm(inp["conv_b"]); shared["clgT"] = cm(inp["conv_ln_g"]); shared["clbT"] = cm(inp["conv_ln_b"])
    shared["fwT"] = A(np.asarray(inp["ffn_conv_w"]).transpose(0, 2, 1).reshape(NL, 44, 128, 3).transpose(0, 2, 1, 3))
    shared["fbT"] = A(np.asarray(inp["ffn_conv_b"]).reshape(NL, 44, 128).transpose(0, 2, 1))
    shared["tri"] = np.triu(np.ones((128, 128), f))
    shared["ident"] = np.eye(128, dtype=f)
    shared["alngT"] = cm(inp["a_ln_g"]); shared["alnbT"] = cm(inp["a_ln_b"])
    rep64 = lambda a: A(np.repeat(np.asarray(a), 64, axis=1).reshape(NL, 2, 128).transpose(0, 2, 1))
    shared["ws00T"] = rep64(np.asarray(inp["w_s"])[:, :, 0, 0])
    shared["bs0T"] = rep64(np.asarray(inp["b_s"])[:, :, 0])
    sel = np.zeros((NS, NS, 128), f)
    for n in range(NS):
        sel[n, n, :] = 1.0
    shared["sel"] = sel
    bigm = np.full((128, 8), -30000.0, f); bigm[0, :] = 0.0
    shared["bigm"] = bigm
    blk = np.zeros((8, 512), f)
    for h in range(8):
        blk[h, h * 64:(h + 1) * 64] = 1.0
    shared["blkm"] = blk
    shared["iot"] = np.ascontiguousarray(np.broadcast_to(np.arange(128, dtype=np.int32)[:, None], (128, NS * NPG)))
    shared["cache_k"] = A(inp["cache_k"]).reshape(NL * NPHYS * 128, 512)
    shared["cache_v"] = A(inp["cache_v"]).reshape(NL * NPHYS * 128, 512)
    shared["cache_f"] = A(inp["cache_logf"]).reshape(NL * NPHYS * 128, 8)
    xsm = np.asarray(inp["x_sample"]); stc = np.asarray(inp["state_conv"]); stf = np.asarray(inp["state_ffn_conv"])
    ptab = np.asarray(inp["page_table"]).astype(np.int32)
    xp = np.asarray(inp["x_prompt"]); cp_ = np.asarray(inp["c_prompt"]); cs = np.asarray(inp["c_sample"])
    maps = []
    for c in range(NCORES):
        b = c // 2
        m = dict(shared)
        m["xT"] = A(xp[b].T)
        m["cT"] = A(np.concatenate([cp_[b:b + 1], cs[NS * c:NS * (c + 1)]], 0).T)
        sl = slice(NS * c, NS * (c + 1))
        m["xsT0"] = A(xsm[sl, 0, :].T.reshape(8, 128, NS).transpose(1, 0, 2))
        m["ptb"] = np.ascontiguousarray(np.broadcast_to(ptab[sl].reshape(1, NS * NPG), (128, NS * NPG)))
        m["stconvT"] = A(stc[:, sl].transpose(0, 3, 1, 2).reshape(NL, 2, 128, NS, 30).transpose(0, 2, 1, 3, 4))
        m["sffnT"] = A(stf[:, sl].transpose(0, 3, 1, 2).reshape(NL, 44, 128, NS, 2).transpose(0, 2, 1, 3, 4))
        maps.append(m)
    return maps


def kernel(**inp):
    if "nc" not in _NC_CACHE:
        _NC_CACHE["nc"] = build_program()
    nc = _NC_CACHE["nc"]
    maps = _host_inputs(inp)
    res = run_bass_kernel_spmd(nc, maps, core_ids=list(range(NCORES))).results
    f = np.float32
    B = 4
    y_prompt = np.stack([res[2 * b]["yT"].T for b in range(B)]).astype(f)
    k_p = np.stack([res[2 * b]["k_o"] for b in range(B)], 1).reshape(NL, B, T, 8, 64).astype(f)
    v_p = np.stack([res[2 * b]["v_o"] for b in range(B)], 1).reshape(NL, B, T, 8, 64).astype(f)
    lf_p = np.stack([res[2 * b]["lf_o"] for b in range(B)], 1).astype(f)
    conv_p = np.stack([res[2 * b]["convp"].transpose(0, 2, 1) for b in range(B)], 1).astype(f)
    ffn_p = np.stack([res[2 * b]["ffnp"].transpose(0, 2, 1) for b in range(B)], 1).astype(f)
    cat = lambda fn, ax: np.concatenate([fn(res[c]) for c in range(NCORES)], ax).astype(f)
    y_s = cat(lambda r: r["ysT"].transpose(2, 1, 0).reshape(NS, 1, D), 0)
    k_s = cat(lambda r: r["ks_o"].reshape(NL, NS, 1, 8, 64), 1)
    v_s = cat(lambda r: r["vs_o"].reshape(NL, NS, 1, 8, 64), 1)
    lf_s = cat(lambda r: r["lfs_o"].reshape(NL, NS, 1, 8), 1)
    conv_s = cat(lambda r: r["convs"].transpose(0, 3, 4, 2, 1).reshape(NL, NS, 30, 256), 1)
    ffn_s = cat(lambda r: r["ffns"].transpose(0, 3, 4, 2, 1).reshape(NL, NS, 2, 2 * DFF), 1)
    chv_s = cat(lambda r: r["chv"].transpose(0, 3, 2, 1).reshape(NL, NS, 1, 256), 1)
    return (y_prompt, y_s, k_p, v_p, lf_p, conv_p, ffn_p, k_s, v_s, lf_s, conv_s, ffn_s, chv_s)
```

```python
import numpy as np
import concourse.bass as bass
import concourse.mybir as mybir
from concourse.bass_utils import run_bass_kernel_spmd

F32 = mybir.dt.float32
BF16 = mybir.dt.bfloat16
I32 = mybir.dt.int32
ALU = mybir.AluOpType
AF = mybir.ActivationFunctionType

D = 1024
NL = 2
T = 4096
NS = 4
NPG = 64
DFF = 2816
INW = 2568
EPS = 1e-6
NCORES = 8
NPHYS = 2560
STOP_AT = None


class _Stop(Exception):
    pass


class Buf:
    def __init__(self, name, excl=False):
        self.name = name
        self.w = None
        self.r = {}
        self.excl = excl


class FW:
    def __init__(self, nc):
        self.nc = nc
        self.eng = {'pe': nc.tensor, 'act': nc.scalar, 'dve': nc.vector, 'pool': nc.gpsimd, 'sp': nc.sync}
        self.sems, self.cnt = {}, {}
        self.seen = {k: {} for k in self.eng}
        self._cms = []
        for k in ('pe', 'act', 'dve', 'pool'):
            self._mksem(k)

    def _mksem(self, key):
        cm = self.nc.semaphore("s%d" % len(self.sems))
        self.sems[key] = cm.__enter__()
        self._cms.append(cm)
        self.cnt[key] = 0

    def close(self):
        for cm in reversed(self._cms):
            cm.__exit__(None, None, None)

    def _wait(self, e, key, val):
        if val <= 0 or (e == 'pe' and key == 'pe'):
            return
        if self.seen[e].get(key, 0) >= val:
            return
        self.eng[e].wait_ge(self.sems[key], val)
        self.seen[e][key] = val

    def _deps(self, e, reads, writes, skip=None):
        for b in reads:
            if b.w is not None:
                self._wait(e, *b.w)
            if b.excl:
                for k, v in b.r.items():
                    if k != e:
                        self._wait(e, k, v)
        for b in writes:
            if b.w is not None and b.w[0] != skip:
                self._wait(e, *b.w)
            for k, v in b.r.items():
                self._wait(e, k, v)

    def _mark(self, key, val, reads, writes):
        for b in reads:
            b.r[key] = val
        for b in writes:
            b.w = (key, val)
            b.r = {}

    def op(self, e, fn, reads=(), writes=(), inc=True):
        self._deps(e, reads, writes)
        ins = fn(self.eng[e])
        if inc:
            self.cnt[e] += 1
            ins.then_inc(self.sems[e], 1)
            self._mark(e, self.cnt[e], reads, writes)
        else:
            self._mark(e, self.cnt[e] + 1, reads, writes)

    def dma(self, out, in_, reads=(), writes=(), sem_buf=None, q='sp'):
        b0 = sem_buf if sem_buf is not None else (writes[0] if writes else reads[0])
        key = ('dma', id(b0))
        if key not in self.sems:
            self._mksem(key)
        self._deps(q, reads, writes, skip=key)
        ins = self.eng[q].dma_start(out=out, in_=in_)
        self.cnt[key] += 16
        ins.then_inc(self.sems[key], 16)
        self._mark(key, self.cnt[key], reads, writes)

    def barrier(self):
        for e in self.eng:
            for key in self.sems:
                self._wait(e, key, self.cnt[key])


def build_program():
    nc = bass.Bass("TRN2", target_bir_lowering=False)
    NT = T // 512
    NKB = T // 128

    def din(name, shape, dt=F32):
        return nc.dram_tensor(name, list(shape), dt, kind="ExternalInput").ap()

    def dout(name, shape):
        return nc.dram_tensor(name, list(shape), F32, kind="ExternalOutput").ap()

    xT_d = din("xT", [D, T])
    cT_d = din("cT", [D, 1 + NS])
    w_ada_d = din("w_ada", [NL, D, 6 * D])
    badaT_d = din("badaT", [NL, 128, 48])
    g1r_d = din("g1r", [NL, 128, 8, 1 + NS])
    g2r_d = din("g2r", [NL, 128, 8, 1 + NS])
    mixgT_d = din("mixgT", [NL, 128, 8])
    fgT_d = din("fgT", [128, 8])
    w_in_d = din("w_in", [NL, D, INW])
    w_out_d = din("w_out", [NL, D, D])
    w_up_d = din("w_up", [NL, D, 2 * DFF])
    w_down_d = din("w_down", [NL, DFF, D])
    bfb_d = din("bfb", [NL, 128, 8])
    alng_d = din("alng", [NL, 128, 256])
    alnb_d = din("alnb", [NL, 128, 256])
    wsT_d = din("wsT", [NL, 128, 4, 128])
    bsT_d = din("bsT", [NL, 128, 4])
    cwT_d = din("cwT", [NL, 128, 2, 31])
    cbT_d = din("cbT", [NL, 128, 2])
    clgT_d = din("clgT", [NL, 128, 2])
    clbT_d = din("clbT", [NL, 128, 2])
    fwT_d = din("fwT", [NL, 128, 44, 3])
    fbT_d = din("fbT", [NL, 128, 44])
    tri_d = din("tri", [128, 128])
    ident_d = din("ident", [128, 128])

    xsT0_d = din("xsT0", [128, 8, NS])
    ptb_d = din("ptb", [128, NS * NPG], I32)
    iot_d = din("iot", [128, NS * NPG], I32)
    sel_d = din("sel", [NS, NS, 128])
    bigm_d = din("bigm", [128, 8])
    blkm_d = din("blkm", [8, 512])
    ck_d = din("cache_k", [NL * NPHYS * 128, 512])
    cv_d = din("cache_v", [NL * NPHYS * 128, 512])
    cf_d = din("cache_f", [NL * NPHYS * 128, 8])
    stconv_d = din("stconvT", [NL, 128, 2, NS, 30])
    sffn_d = din("sffnT", [NL, 128, 44, NS, 2])
    alngT_d = din("alngT", [NL, 128, 2])
    alnbT_d = din("alnbT", [NL, 128, 2])
    ws00T_d = din("ws00T", [NL, 128, 2])
    bs0T_d = din("bs0T", [NL, 128, 2])
    ysT_o = dout("ysT", [128, 8, NS])
    ks_o = dout("ks_o", [NL, NS, 512])
    vs_o = dout("vs_o", [NL, NS, 512])
    lfs_o = dout("lfs_o", [NL, NS, 8])
    convs_o = dout("convs", [NL, 128, 2, NS, 30])
    ffns_o = dout("ffns", [NL, 128, 44, NS, 2])
    chv_o = dout("chv", [NL, 128, 2, NS])

    yT_o = dout("yT", [D, T])
    k_o = dout("k_o", [NL, T, 512])
    v_o = dout("v_o", [NL, T, 512])
    lf_o = dout("lf_o", [NL, T, 8])
    convp_o = dout("convp", [NL, 256, 30])
    ffnp_o = dout("ffnp", [NL, 2 * DFF, 2])

    xs_d = nc.dram_tensor("xs_scr", [D, T], F32).ap()

    fw = FW(nc)
    op, dma = fw.op, fw.dma

    def act(out, in_, func, reads, writes, **kw):
        op('act', lambda e: e.activation(out=out, in_=in_, func=func, **kw), reads, writes)

    def mm(out, lhsT, rhs, start, stop, reads, writes, lazy=False):
        op('pe', lambda e: e.matmul(out, lhsT, rhs, start=start, stop=stop), reads, writes,
           inc=(bool(stop) or not lazy))

    def tt(eng, out, a, b, o, reads, writes):
        op(eng, lambda e: e.tensor_tensor(out, a, b, o), reads, writes)

    def ts(eng, out, a, s1, s2, o0, o1, reads, writes):
        if o1 is None:
            op(eng, lambda e: e.tensor_scalar(out, a, s1, None, o0), reads, writes)
        else:
            op(eng, lambda e: e.tensor_scalar(out, a, s1, s2, o0, o1), reads, writes)

    def stt(eng, out, in0, scalar, in1, o0, o1, reads, writes):
        op(eng, lambda e: e.scalar_tensor_tensor(out, in0, scalar, in1, o0, o1), reads, writes)

    def cp(eng, out, in_, reads, writes):
        if eng == 'act':
            act(out, in_, AF.Copy, reads, writes)
        else:
            op(eng, lambda e: e.tensor_copy(out, in_), reads, writes)

    def rsqrt_inplace(tile_ap, buf):
        act(tile_ap, tile_ap, AF.Sqrt, [buf], [buf])
        op('dve', lambda e: e.reciprocal(tile_ap, tile_ap), [buf], [buf])

    from contextlib import ExitStack
    import contextlib
    with ExitStack() as top:
        top.enter_context(contextlib.suppress(_Stop))
        top.enter_context(nc.allow_non_contiguous_dma(reason="small strided parameter loads"))

        uid = [0]

        def sb(stack, name, shape, dt=F32):
            uid[0] += 1
            return stack.enter_context(nc.sbuf_tensor("sb%d_%s" % (uid[0], name), list(shape), dt))

        ps = [top.enter_context(nc.psum_tensor("ps%d" % i, [128, 512], F32)) for i in range(8)]
        bps = [Buf("ps%d" % i, excl=True) for i in range(8)]

        x = sb(top, "x", [128, 8, 512]); bx = Buf("x")
        ones_bf = sb(top, "ones_bf", [128, 128], BF16)
        ones_f = sb(top, "ones_f", [128, 128])
        tri_f = sb(top, "tri_f", [128, 128])
        mask_bf = sb(top, "mask_bf", [128, 128], BF16)
        ident_bf = sb(top, "ident_bf", [128, 128], BF16)
        onecol = sb(top, "onecol", [128, 1])
        bconst = Buf("const")
        silu_c = sb(top, "silu_c", [128, 8, 1 + NS]); bsc = Buf("silu_c")
        mod = sb(top, "mod", [128, 48, 1 + NS]); bmod = Buf("mod")
        a1 = sb(top, "a1", [128, 8, 1 + NS]); a2 = sb(top, "a2", [128, 8, 1 + NS])
        g1r = sb(top, "g1r", [128, 8, 1 + NS]); g2r = sb(top, "g2r", [128, 8, 1 + NS])
        badaT = sb(top, "badaT", [128, 48])
        mixgT = sb(top, "mixgT", [128, 8]); fgT = sb(top, "fgT", [128, 8])
        bfb = sb(top, "bfb", [128, 8])
        wsT_bf = sb(top, "wsT_bf", [128, 4, 128], BF16)
        bsT = sb(top, "bsT", [128, 4])
        cwT = sb(top, "cwT", [128, 2, 31]); cbT = sb(top, "cbT", [128, 2])
        clgT = sb(top, "clgT", [128, 2]); clbT = sb(top, "clbT", [128, 2])
        fwT = sb(top, "fwT", [128, 44, 3]); fbT = sb(top, "fbT", [128, 44])
        bpar = Buf("params")
        bstg = [Buf("stg%d" % i) for i in range(4)]
        bmisc = Buf("misc")
        cum = sb(top, "cum", [128, NKB, 8]); bcum = Buf("cum")
        gtot = sb(top, "gtot", [128, 8]); bgtot = Buf("gtot")
        halo = sb(top, "halo", [128, 44, 2]); bhalo = Buf("halo")

        xs = sb(top, "xs", [128, 8, NS]); bxs = Buf("xs")
        bigm = sb(top, "bigm", [128, 8])
        idx0 = sb(top, "idx0", [128, NS * NPG])
        idxl = sb(top, "idxl", [128, NS * NPG], I32); bidx = Buf("idx")
        alngT = sb(top, "alngT", [128, 2]); alnbT = sb(top, "alnbT", [128, 2])
        ws00T = sb(top, "ws00T", [128, 2]); bs0T = sb(top, "bs0T", [128, 2])
        bpool = [Buf("pq%d" % i) for i in range(5)]
        with ExitStack() as tmpsc:
            ptb = sb(tmpsc, "ptb", [128, NS * NPG], I32)
            iot = sb(tmpsc, "iot", [128, NS * NPG], I32)
            for dst_, src_ in ((xs, xsT0_d), (bigm, bigm_d), (ptb, ptb_d), (iot, iot_d), (tri_f, tri_d),
                               (ones_f, ident_d), (fgT, fgT_d)):
                dma(dst_[:], src_, writes=[bconst], sem_buf=bpar)
            dma(silu_c[:], cT_d.rearrange("(c p) n -> p c n", p=128), writes=[bsc], sem_buf=bpar)
            fw.barrier()
            iof = sb(tmpsc, "iof", [128, NS * NPG])
            cp('dve', idx0[:], ptb[:], [bconst], [bconst])
            cp('dve', iof[:], iot[:], [bconst], [bconst])
            stt('dve', idx0[:], idx0[:], 128.0, iof[:], ALU.mult, ALU.add, [bconst], [bconst])
            op('dve', lambda e: e.tensor_copy(mask_bf[:], tri_f[:]), [bconst], [bconst])
            op('dve', lambda e: e.tensor_copy(ident_bf[:], ones_f[:]), [bconst], [bconst])
            op('pool', lambda e: e.memset(ones_f[:], 1.0), [], [bconst])
            op('pool', lambda e: e.memset(ones_bf[:], 1.0), [], [bconst])
            op('pool', lambda e: e.memset(onecol[:], 1.0), [], [bconst])
            act(silu_c[:], silu_c[:], AF.Silu, [bsc], [bsc])
            fw.barrier()

        def stop(k):
            if STOP_AT == k:
                raise _Stop()

        for l in range(NL):
            last = (l == NL - 1)
            stop(0)
            for dst, src in ((badaT, badaT_d), (g1r, g1r_d), (g2r, g2r_d), (mixgT, mixgT_d), (bfb, bfb_d),
                             (bsT, bsT_d), (cwT, cwT_d),
                             (cbT, cbT_d), (clgT, clgT_d), (clbT, clbT_d), (fwT, fwT_d), (fbT, fbT_d),
                             (alngT, alngT_d), (alnbT, alnbT_d), (ws00T, ws00T_d), (bs0T, bs0T_d)):
                dma(dst[:], src[l], writes=[bpar])
            with ExitStack() as tmpsc:
                wsT_f = sb(tmpsc, "wsT_f", [128, 4, 128])
                dma(wsT_f[:], wsT_d[l], writes=[bpar])
                fw.barrier()
                for h in range(4):
                    tt('dve', wsT_bf[:, h, :], wsT_f[:, h, :], tri_f[:], ALU.mult, [bpar, bconst], [bpar])
                fw.barrier()

            with ExitStack() as st:
                stg = [sb(st, "stgA%d" % i, [128, 8, 512]) for i in range(4)]
                wv = w_ada_d[l].rearrange("(c p) n -> p c n", p=128)
                for g in range(12):
                    s_ = g % 4
                    dma(stg[s_][:], wv[:, :, g * 512:(g + 1) * 512], writes=[bstg[s_]])
                    for j in range(4):
                        m = g * 4 + j
                        pb_ = bps[m % 2]
                        pt_ = ps[m % 2]
                        for kc in range(8):
                            mm(pt_[:, 0:1 + NS], stg[s_][:, kc, j * 128:(j + 1) * 128], silu_c[:, kc, :],
                               kc == 0, kc == 7, [bstg[s_], bsc], [pb_], lazy=True)
                        act(mod[:, m, :], pt_[:, 0:1 + NS], AF.Identity, [pb_, bpar], [bmod],
                            bias=badaT[:, m:m + 1], scale=1.0)
                ts('dve', a1[:], mod[:, 8:16, :], 1.0, None, ALU.add, None, [bmod], [bmod])
                tt('dve', a1[:], a1[:], g1r[:], ALU.mult, [bmod, bpar], [bmod])
                ts('dve', a2[:], mod[:, 32:40, :], 1.0, None, ALU.add, None, [bmod], [bmod])
                tt('dve', a2[:], a2[:], g2r[:], ALU.mult, [bmod, bpar], [bmod])
                fw.barrier()
                stop(1)
            shift1 = mod[:, 0:8, :]; gate1 = mod[:, 16:24, :]
            shift2 = mod[:, 24:32, :]; gate2 = mod[:, 40:48, :]

            x_src = xT_d if l == 0 else xs_d
            xsv = xs_d.rearrange("(c p) t -> p c t", p=128)
            x_srcv = x_src.rearrange("(c p) t -> p c t", p=128)

            def rms_rstd(xt, bxt, rstd, brstd, sqs, bsqs, nfeat_chunks, ncols, denom, pst, bpst):
                for c in range(nfeat_chunks):
                    act(sqs[c % 2][:, 0:ncols], xt(c), AF.Square, [bxt], [bsqs[c % 2]])
                    mm(pst[:, 0:ncols], ones_bf[:], sqs[c % 2][:, 0:ncols], c == 0, c == nfeat_chunks - 1,
                       [bconst, bsqs[c % 2]], [bpst])
                ts('dve', rstd, pst[:, 0:ncols], 1.0 / denom, EPS, ALU.mult, ALU.add, [bpst], [brstd])
                rsqrt_inplace(rstd, brstd)

            def idma(out, in_, idx_ap, reads, writes, sem_buf):
                key = ('dma', id(sem_buf))
                if key not in fw.sems:
                    fw._mksem(key)
                fw._deps('pool', reads, writes, skip=key)
                ins = nc.gpsimd.indirect_dma_start(out=out, out_offset=None, in_=in_,
                                                   in_offset=bass.IndirectOffsetOnAxis(ap=idx_ap, axis=0))
                fw.cnt[key] += 16
                ins.then_inc(fw.sems[key], 16)
                fw._mark(key, fw.cnt[key], reads, writes)

            sc1 = slice(1, 1 + NS)

            def chan_stats(chunks, bsrc, w2, bw2, m_t, r_t, bmr, denom, want_mean):
                nch = len(chunks)
                if want_mean:
                    for i_, c_ in enumerate(chunks):
                        mm(ps[6][:, 0:NS], ones_f[:], c_, i_ == 0, i_ == nch - 1, [bconst, bsrc], [bps[6]])
                    ts('dve', m_t, ps[6][:, 0:NS], 1.0 / denom, None, ALU.mult, None, [bps[6]], [bmr])
                for i_, c_ in enumerate(chunks):
                    tt('dve', w2[:, i_, :], c_, c_, ALU.mult, [bsrc], [bw2])
                for i_ in range(nch):
                    mm(ps[6][:, 8:8 + NS], ones_f[:], w2[:, i_, :], i_ == 0, i_ == nch - 1, [bconst, bw2], [bps[6]])
                ts('dve', r_t, ps[6][:, 8:8 + NS], 1.0 / denom, EPS, ALU.mult, ALU.add, [bps[6]], [bmr])
                if want_mean:
                    tt('dve', w2[:, 0, :], m_t, m_t, ALU.mult, [bmr], [bw2])
                    tt('dve', r_t, r_t, w2[:, 0, :], ALU.subtract, [bmr, bw2], [bmr])
                rsqrt_inplace(r_t, bmr)

            def sample_norm(h_bf, bh, a_t, sh_t, ss):
                sqs_ = sb(ss, "s_sq", [128, 8, NS], BF16); bsqs_ = Buf("s_sq")
                rs_ = sb(ss, "s_rs", [128, NS]); brs_ = Buf("s_rs")
                hf_ = sb(ss, "s_hf", [128, 8, NS]); bhf_ = Buf("s_hf")
                for c in range(8):
                    act(sqs_[:, c, :], xs[:, c, :], AF.Square, [bxs], [bsqs_])
                for c in range(8):
                    mm(ps[0][:, 0:NS], ones_bf[:], sqs_[:, c, :], c == 0, c == 7, [bconst, bsqs_], [bps[0]])
                ts('dve', rs_[:], ps[0][:, 0:NS], 1.0 / D, EPS, ALU.mult, ALU.add, [bps[0]], [brs_])
                rsqrt_inplace(rs_[:], brs_)
                for c in range(8):
                    tt('dve', hf_[:, c, :], xs[:, c, :], rs_[:], ALU.mult, [bxs, brs_], [bhf_])
                tt('dve', hf_[:], hf_[:], a_t, ALU.mult, [bhf_, bmod], [bhf_])
                tt('dve', hf_[:], hf_[:], sh_t, ALU.add, [bhf_, bmod], [bhf_])
                cp('dve', h_bf[:], hf_[:], [bhf_], [bh])

            def sample_phase1(win, bwin, wout, bwout):
                with ExitStack() as ss:
                    hs = sb(ss, "s_hs", [128, 8, NS], BF16); bhs = Buf("s_hs")
                    zs = sb(ss, "s_zs", [128, 8, NS]); bzs = Buf("s_zs")
                    w1 = sb(ss, "s_w1", [128, 8, NS]); bw1 = Buf("s_w1")
                    w2 = sb(ss, "s_w2", [128, 8, NS]); bw2 = Buf("s_w2")
                    w3 = sb(ss, "s_w3", [128, 8, NS]); bw3 = Buf("s_w3")
                    mt = sb(ss, "s_mt", [128, NS]); rt = sb(ss, "s_rt", [128, NS]); bmr = Buf("s_mr")
                    tm = sb(ss, "s_tm", [NS, 3, 512]); btm = Buf("s_tm")
                    lftm = sb(ss, "s_lftm", [NS, 8]); blftm = Buf("s_lftm")
                    ysT = sb(ss, "s_ysT", [128, 8, NS], BF16); bys = Buf("s_ysT")
                    ycs = sb(ss, "s_ycs", [128, 4, NS]); bycs = Buf("s_ycs")
                    cst = sb(ss, "s_cst", [128, 2, NS, 31]); bcst = Buf("s_cst")
                    p31 = sb(ss, "s_p31", [128, 2, NS, 31]); bp31 = Buf("s_p31")
                    Kp = [sb(ss, "s_Kp%d" % i, [128, 512]) for i in range(4)]
                    Vp = [sb(ss, "s_Vp%d" % i, [128, 512]) for i in range(4)]
                    prod = sb(ss, "s_prod", [128, 512]); bprod = Buf("s_prod")
                    qrep = sb(ss, "s_qrep", [128, 512]); bqrep = Buf("s_qrep")
                    krep = sb(ss, "s_krep", [128, 512]); bkrep = Buf("s_krep")
                    vrep = sb(ss, "s_vrep", [128, 512]); bvrep = Buf("s_vrep")
                    LF = sb(ss, "s_LF", [128, NPG, 8])
                    cumi = sb(ss, "s_cumi", [128, NPG + 1, 8]); bcumi = Buf("s_cumi")
                    S_ = sb(ss, "s_S", [128, NPG + 1, 8]); bS = Buf("s_S")
                    Pm = sb(ss, "s_P", [128, NPG + 1, 8]); bPm = Buf("s_P")
                    gs = sb(ss, "s_gs", [128, 8]); bgs = Buf("s_gs")
                    lfrep = sb(ss, "s_lfrep", [128, 8]); blfrep = Buf("s_lfrep")
                    Om = sb(ss, "s_Om", [8, 512]); bOm = Buf("s_Om")
                    den = sb(ss, "s_den", [128, 2]); bden = Buf("s_den")
                    bKV, bLF = bpool[0:4], bpool[4]
                    selt = sb(ss, "selt", [NS, NS, 128])
                    blkm = sb(ss, "blkm", [8, 512])
                    bsel = Buf("selblk")
                    dma(selt[:], sel_d, writes=[bsel], sem_buf=bpar)
                    dma(blkm[:], blkm_d, writes=[bsel], sem_buf=bpar)
                    dma(cst[:, :, :, 0:30], stconv_d[l], writes=[bcst], sem_buf=bpar)
                    fw.barrier()

                    ts('dve', idxl[:], idx0[:], float(l * NPHYS * 128), None, ALU.add, None, [bconst], [bidx])
                    sample_norm(hs, bhs, a1[:, :, sc1], shift1[:, :, sc1], ss)
                    for ch in range(8):
                        for kc in range(8):
                            mm(ps[1][:, ch * NS:(ch + 1) * NS], win[:, kc, ch * 128:(ch + 1) * 128], hs[:, kc, :],
                               kc == 0, kc == 7, [bwin, bhs], [bps[1]])
                    cp('dve', zs[:], ps[1][:, 0:8 * NS].rearrange("p (c n) -> p c n", n=NS), [bps[1]], [bzs])
                    for g_, c0 in enumerate((1024, 1536, 2048)):
                        for kc in range(8):
                            mm(ps[2 + g_][0:NS, :], hs[:, kc, :], win[:, kc, c0:c0 + 512], kc == 0, kc == 7,
                               [bwin, bhs], [bps[2 + g_]])
                        cp('act', tm[:, g_, :], ps[2 + g_][0:NS, :], [bps[2 + g_]], [btm])
                    for kc in range(8):
                        mm(ps[5][0:NS, 0:8], hs[:, kc, :], win[:, kc, 2560:2568], kc == 0, kc == 7, [bwin, bhs], [bps[5]])
                    tt('dve', lftm[:], ps[5][0:NS, 0:8], bfb[0:NS, :], ALU.add, [bps[5], bpar], [blftm])
                    act(lftm[:], lftm[:], AF.Exp, [blftm], [blftm], scale=-1.0)
                    act(lftm[:], lftm[:], AF.Ln, [blftm, bconst], [blftm], bias=onecol[0:NS, 0:1], scale=1.0)
                    ts('dve', lftm[:], lftm[:], -1.0, None, ALU.mult, None, [blftm], [blftm])
                    dma(ks_o[l], tm[:, 1, :], reads=[btm], sem_buf=bmisc, q='pool')
                    dma(vs_o[l], tm[:, 2, :], reads=[btm], sem_buf=bmisc, q='pool')
                    dma(lfs_o[l], lftm[:], reads=[blftm], sem_buf=bmisc, q='pool')

                    act(w1[:, 0:4, :], zs[:, 0:4, :], AF.Gelu, [bzs], [bw1])
                    chan_stats([w1[:, 2, :], w1[:, 3, :]], bw1, w2, bw2, mt[:], rt[:], bmr, 256, True)
                    for cc in range(2):
                        tt('dve', w3[:, cc, :], w1[:, 2 + cc, :], mt[:], ALU.subtract, [bw1, bmr], [bw3])
                        tt('dve', w3[:, cc, :], w3[:, cc, :], rt[:], ALU.mult, [bw3, bmr], [bw3])
                        ts('dve', w3[:, cc, :], w3[:, cc, :], alngT[:, cc:cc + 1], alnbT[:, cc:cc + 1], ALU.mult, ALU.add,
                           [bw3, bpar], [bw3])
                    dma(chv_o[l], w3[:, 0:2, :], reads=[bw3], sem_buf=bmisc, q='pool')
                    for cc in range(2):
                        ts('dve', w3[:, 2 + cc, :], w3[:, cc, :], ws00T[:, cc:cc + 1], bs0T[:, cc:cc + 1], ALU.mult, ALU.add,
                           [bw3, bpar], [bw3])
                        tt('dve', w3[:, 2 + cc, :], w3[:, 2 + cc, :], w1[:, cc, :], ALU.mult, [bw3, bw1], [bw3])
                    chan_stats([w3[:, 2, :], w3[:, 3, :]], bw3, w2, bw2, mt[:], rt[:], bmr, 256, False)
                    for cc in range(2):
                        tt('dve', ysT[:, cc, :], w3[:, 2 + cc, :], rt[:], ALU.mult, [bw3, bmr], [bys])

                    act(w1[:, 4:6, :], zs[:, 6:8, :], AF.Sigmoid, [bzs], [bw1])
                    for cc in range(2):
                        tt('dve', cst[:, cc, :, 30], zs[:, 4 + cc, :], w1[:, 4 + cc, :], ALU.mult, [bzs, bw1], [bcst])
                    dma(convs_o[l], cst[:, :, :, 1:31], reads=[bcst], sem_buf=bmisc, q='pool')
                    for cc in range(2):
                        for n in range(NS):
                            tt('dve', p31[:, cc, n, :], cst[:, cc, n, :], cwT[:, cc, :], ALU.mult, [bcst, bpar], [bp31])
                        op('dve', lambda e: e.reduce_sum(w3[:, 4 + cc, :], p31[:, cc, :, :], mybir.AxisListType.X),
                           [bp31], [bw3])
                        ts('dve', w3[:, 4 + cc, :], w3[:, 4 + cc, :], cbT[:, cc:cc + 1], None, ALU.add, None, [bw3, bpar], [bw3])
                    chan_stats([w3[:, 4, :], w3[:, 5, :]], bw3, w2, bw2, mt[:], rt[:], bmr, 256, True)
                    for cc in range(2):
                        tt('dve', w3[:, 4 + cc, :], w3[:, 4 + cc, :], mt[:], ALU.subtract, [bw3, bmr], [bw3])
                        tt('dve', w3[:, 4 + cc, :], w3[:, 4 + cc, :], rt[:], ALU.mult, [bw3, bmr], [bw3])
                        act(w3[:, 4 + cc, :], w3[:, 4 + cc, :], AF.Silu, [bw3, bpar], [bw3],
                            scale=clgT[:, cc:cc + 1], bias=clbT[:, cc:cc + 1])
                    chan_stats([w3[:, 4, :], w3[:, 5, :]], bw3, w2, bw2, mt[:], rt[:], bmr, 256, False)
                    for cc in range(2):
                        tt('dve', ysT[:, 2 + cc, :], w3[:, 4 + cc, :], rt[:], ALU.mult, [bw3, bmr], [bys])

                    it = 0
                    for n in range(NS):
                        col = n * NPG
                        mm(ps[7][:, 0:8], selt[:, n, :], lftm[:], True, True, [bsel, blftm], [bps[7]])
                        cp('dve', lfrep[:], ps[7][:, 0:8], [bps[7]], [blfrep])
                        mm(ps[2][:], selt[:, n, :], tm[:, 0, :], True, True, [bsel, btm], [bps[2]])
                        act(qrep[:], ps[2][:], AF.Identity, [bps[2]], [bqrep], scale=0.125, bias=0.0)
                        mm(ps[3][:], selt[:, n, :], tm[:, 1, :], True, True, [bsel, btm], [bps[3]])
                        cp('act', krep[:], ps[3][:], [bps[3]], [bkrep])
                        mm(ps[4][:], selt[:, n, :], tm[:, 2, :], True, True, [bsel, btm], [bps[4]])
                        cp('act', vrep[:], ps[4][:], [bps[4]], [bvrep])
                        op('pool', lambda e: e.memset(gs[:], 0.0), [], [bgs])
                        for pg in range(NPG):
                            idma(LF[:, pg, :], cf_d, idxl[:, col + pg:col + pg + 1], [bidx], [bLF], bLF)
                        for pg in range(NPG):
                            mm(ps[7][:, 16:24], tri_f[:], LF[:, pg, :], True, True, [bconst, bLF], [bps[7]])
                            mm(ps[7][:, 32:40], ones_f[:], LF[:, pg, :], True, True, [bconst, bLF], [bps[7]])
                            tt('dve', cumi[:, pg, :], ps[7][:, 16:24], gs[:], ALU.add, [bps[7], bgs], [bcumi])
                            tt('dve', gs[:], ps[7][:, 32:40], gs[:], ALU.add, [bps[7], bgs], [bgs])
                        tt('dve', gs[:], gs[:], lfrep[:], ALU.add, [bgs, blfrep], [bgs])
                        for pg in range(NPG):
                            tt('dve', cumi[:, pg, :], gs[:], cumi[:, pg, :], ALU.subtract, [bgs, bcumi], [bcumi])
                        cp('dve', cumi[:, NPG, :], bigm[:], [bconst], [bcumi])
                        for pg in range(NPG + 1):
                            sl = it % 4
                            it += 1
                            if pg < NPG:
                                idma(Kp[sl][:], ck_d, idxl[:, col + pg:col + pg + 1], [bidx], [bKV[sl]], bKV[sl])
                                idma(Vp[sl][:], cv_d, idxl[:, col + pg:col + pg + 1], [bidx], [bKV[sl]], bKV[sl])
                                k_ap, bk_, v_ap, bv_ = Kp[sl], bKV[sl], Vp[sl], bKV[sl]
                            else:
                                k_ap, bk_, v_ap, bv_ = krep, bkrep, vrep, bvrep
                            tt('dve', prod[:], k_ap[:], qrep[:], ALU.mult, [bk_, bqrep], [bprod])
                            op('dve', lambda e: e.reduce_sum(S_[:, pg, :], prod[:].rearrange("p (h d) -> p h d", d=64),
                                                             mybir.AxisListType.X), [bprod], [bS])
                            tt('dve', S_[:, pg, :], S_[:, pg, :], cumi[:, pg, :], ALU.add, [bS, bcumi], [bS])
                            act(Pm[:, pg, :], S_[:, pg, :], AF.Exp, [bS], [bPm])
                            mm(ps[5][0:8, :], Pm[:, pg, :], v_ap[:], pg == 0, pg == NPG, [bPm, bv_], [bps[5]])
                        op('dve', lambda e: e.reduce_sum(prod[:, 0:8], Pm[:].rearrange("p g h -> p h g"),
                                                         mybir.AxisListType.X), [bPm], [bprod])
                        mm(ps[7][0:8, 48:49], prod[:, 0:8], ones_f[:, 0:1], True, True, [bprod, bconst], [bps[7]])
                        op('dve', lambda e: e.reciprocal(den[0:8, 0:1], ps[7][0:8, 48:49]), [bps[7]], [bden])
                        stt('dve', Om[:], ps[5][0:8, :], den[0:8, 0:1], blkm[:], ALU.mult, ALU.mult,
                            [bps[5], bden, bsel], [bOm])
                        for c in range(4):
                            mm(ps[6][:, 16 + c:17 + c], Om[:, c * 128:(c + 1) * 128], ones_f[0:8, 0:1], True, True,
                               [bOm, bconst], [bps[6]])
                        cp('dve', ycs[:, :, n], ps[6][:, 16:20], [bps[6]], [bycs])
                    chan_stats([ycs[:, c, :] for c in range(4)], bycs, w2, bw2, mt[:], rt[:], bmr, 512, False)
                    for c in range(4):
                        tt('dve', ysT[:, 4 + c, :], ycs[:, c, :], rt[:], ALU.mult, [bycs, bmr], [bys])

                    for dc in range(8):
                        for kc in range(8):
                            mm(ps[1][:, dc * NS:(dc + 1) * NS], wout[:, kc, dc * 128:(dc + 1) * 128], ysT[:, kc, :],
                               kc == 0, kc == 7, [bwout, bys], [bps[1]])
                    tt('dve', w1[:], ps[1][:, 0:8 * NS].rearrange("p (c n) -> p c n", n=NS), gate1[:, :, sc1], ALU.mult,
                       [bps[1], bmod], [bw1])
                    tt('dve', xs[:], xs[:], w1[:], ALU.add, [bxs, bw1], [bxs])
                    fw.barrier()

            def sample_phase2(wup, bwup, wdn, bwdn):
                with ExitStack() as ss:
                    hs = sb(ss, "t_hs", [128, 8, NS], BF16); bhs = Buf("t_hs")
                    raw_s = sb(ss, "t_raw", [128, 44, NS]); braw_s = Buf("t_raw")
                    sff = sb(ss, "t_sff", [128, 44, NS, 2]); bsff = Buf("t_sff")
                    fo = sb(ss, "t_fo", [128, 44, NS, 2]); bfo = Buf("t_fo")
                    up = sb(ss, "t_up", [128, 44, NS]); bup = Buf("t_up")
                    tmp = sb(ss, "t_tmp", [128, 44, NS]); btmp_ = Buf("t_tmp")
                    hm = sb(ss, "t_hm", [128, 22, NS], BF16); bhm = Buf("t_hm")
                    w1 = sb(ss, "t_w1", [128, 8, NS]); bw1 = Buf("t_w1")
                    dma(sff[:], sffn_d[l], writes=[bsff], sem_buf=bpar)
                    fw.barrier()
                    sample_norm(hs, bhs, a2[:, :, sc1], shift2[:, :, sc1], ss)
                    for ch in range(44):
                        k_ = 1 + (ch // 22)
                        cc_ = ch % 22
                        for kc in range(8):
                            mm(ps[k_][:, cc_ * NS:(cc_ + 1) * NS], wup[:, kc, ch * 128:(ch + 1) * 128], hs[:, kc, :],
                               kc == 0, kc == 7, [bwup, bhs], [bps[k_]])
                    for hf in range(2):
                        cp('dve', raw_s[:, hf * 22:(hf + 1) * 22, :],
                           ps[1 + hf][:, 0:22 * NS].rearrange("p (c n) -> p c n", n=NS), [bps[1 + hf]], [braw_s])
                    for n in range(NS):
                        tt('dve', up[:, :, n], sff[:, :, n, 0], fwT[:, :, 0], ALU.mult, [bsff, bpar], [bup])
                        tt('dve', tmp[:, :, n], sff[:, :, n, 1], fwT[:, :, 1], ALU.mult, [bsff, bpar], [btmp_])
                        tt('dve', up[:, :, n], up[:, :, n], tmp[:, :, n], ALU.add, [bup, btmp_], [bup])
                        tt('dve', tmp[:, :, n], raw_s[:, :, n], fwT[:, :, 2], ALU.mult, [braw_s, bpar], [btmp_])
                        tt('dve', up[:, :, n], up[:, :, n], tmp[:, :, n], ALU.add, [bup, btmp_], [bup])
                        tt('dve', up[:, :, n], up[:, :, n], fbT[:], ALU.add, [bup, bpar], [bup])
                        cp('dve', fo[:, :, n, 0], sff[:, :, n, 1], [bsff], [bfo])
                        cp('dve', fo[:, :, n, 1], raw_s[:, :, n], [braw_s], [bfo])
                    dma(ffns_o[l], fo[:], reads=[bfo], sem_buf=bmisc, q='pool')
                    act(tmp[:, 0:22, :], up[:, 0:22, :], AF.Silu, [bup], [btmp_])
                    tt('dve', hm[:], tmp[:, 0:22, :], up[:, 22:44, :], ALU.mult, [btmp_, bup], [bhm])
                    for dc in range(8):
                        for j in range(22):
                            mm(ps[3][:, dc * NS:(dc + 1) * NS], wdn[:, j, dc * 128:(dc + 1) * 128], hm[:, j, :],
                               j == 0, j == 21, [bwdn, bhm], [bps[3]])
                    tt('dve', w1[:], ps[3][:, 0:8 * NS].rearrange("p (c n) -> p c n", n=NS), gate2[:, :, sc1], ALU.mult,
                       [bps[3], bmod], [bw1])
                    tt('dve', xs[:], xs[:], w1[:], ALU.add, [bxs, bw1], [bxs])
                    if last:
                        sqs_ = sb(ss, "t_sq", [128, 8, NS], BF16); bsqs_ = Buf("t_sq")
                        rs_ = sb(ss, "t_rs", [128, NS]); brs_ = Buf("t_rs")
                        for c in range(8):
                            act(sqs_[:, c, :], xs[:, c, :], AF.Square, [bxs], [bsqs_])
                        for c in range(8):
                            mm(ps[0][:, 0:NS], ones_bf[:], sqs_[:, c, :], c == 0, c == 7, [bconst, bsqs_], [bps[0]])
                        ts('dve', rs_[:], ps[0][:, 0:NS], 1.0 / D, EPS, ALU.mult, ALU.add, [bps[0]], [brs_])
                        rsqrt_inplace(rs_[:], brs_)
                        for c in range(8):
                            stt('dve', w1[:, c, :], xs[:, c, :], fgT[:, c:c + 1], rs_[:], ALU.mult, ALU.mult,
                                [bxs, bpar, brs_], [bw1])
                        dma(ysT_o, w1[:], reads=[bw1], sem_buf=bmisc, q='pool')
                    fw.barrier()

            with ExitStack() as big:
                KT = sb(big, "KT", [128, 4, T], BF16); bKT = Buf("KT")
                Vst = sb(big, "Vst", [128, NKB, 512], BF16); bVst = Buf("Vst")
                win = sb(big, "win", [128, 8, INW], BF16); bwin = Buf("win")
                wout = sb(big, "wout", [128, 8, D], BF16); bwout = Buf("wout")
                with ExitStack() as st:
                    stg = [sb(st, "stgB%d" % i, [128, INW]) for i in range(4)]
                    NSTG = 4
                    engs = ['dve', 'pool']
                    n = 0
                    for kc in range(8):
                        s_ = n % NSTG
                        dma(stg[s_][:, 0:INW], w_in_d[l, kc * 128:(kc + 1) * 128, :], writes=[bstg[s_]])
                        cp(engs[n % 2], win[:, kc, :], stg[s_][:, 0:INW], [bstg[s_]], [bwin])
                        n += 1
                    for kc in range(8):
                        s_ = n % NSTG
                        dma(stg[s_][:, 0:D], w_out_d[l, kc * 128:(kc + 1) * 128, :], writes=[bstg[s_]])
                        ts(engs[n % 2], wout[:, kc, :], stg[s_][:, 0:D], mixgT[:, kc:kc + 1], None, ALU.mult, None,
                           [bstg[s_], bpar], [bwout])
                        n += 1
                    fw.barrier()
                    stop(2)
                sample_phase1(win, bwin, wout, bwout)

                with ExitStack() as sc:
                    hT = sb(sc, "hT", [128, 8, 512], BF16); bhT = Buf("hT")
                    qTp = sb(sc, "qTp", [128, 2, 4, 512], BF16); bqT = Buf("qTp")
                    yT = sb(sc, "yT", [128, 8, 512], BF16)
                    byA, byB, byC = Buf("yA"), Buf("yB"), Buf("yC")
                    glu = sb(sc, "glu", [128, 2, 542]); bglu = Buf("glu")
                    cacc = sb(sc, "cacc", [128, 2, 512]); bcacc = Buf("cacc")
                    za = sb(sc, "za", [128, 512]); bza = Buf("za")
                    tA = sb(sc, "tA", [128, 256]); btA = Buf("tA")
                    tA2 = sb(sc, "tA2", [128, 256]); btA2 = Buf("tA2")
                    vln = sb(sc, "vln", [128, 256], BF16); bvln = Buf("vln")
                    yan = sb(sc, "yan", [128, 256], BF16); byan = Buf("yan")
                    PT = [sb(sc, "PT%d" % i, [128, 512], BF16) for i in range(2)]
                    bPT = [Buf("PT%d" % i) for i in range(2)]
                    sq = [sb(sc, "sq%d" % i, [128, 512], BF16) for i in range(2)]
                    bsq = [Buf("sq%d" % i) for i in range(2)]
                    st0 = sb(sc, "st0", [128, 512]); bst0 = Buf("st0")
                    st1 = sb(sc, "st1", [128, 512]); bst1 = Buf("st1")
                    st2 = sb(sc, "st2", [128, 512]); bst2 = Buf("st2")
                    kst = sb(sc, "kst", [128, 512]); bkst = Buf("kst")
                    vst = sb(sc, "vst", [128, 512]); bvst = Buf("vst")
                    lft = sb(sc, "lft", [128, 8]); blft = Buf("lft")
                    lfs = sb(sc, "lfs", [128, 8]); blfs = Buf("lfs")
                    sml = sb(sc, "sml", [128, 16]); bsml = Buf("sml")
                    biasT = sb(sc, "biasT", [128, NKB, 8]); bbias = Buf("biasT")
                    alng = sb(sc, "alng", [128, 256]); alnb = sb(sc, "alnb", [128, 256])
                    dma(alng[:], alng_d[l], writes=[bpar])
                    dma(alnb[:], alnb_d[l], writes=[bpar])
                    fw.barrier()

                    op('pool', lambda e: e.memset(gtot[:], 0.0), [], [bgtot])
                    op('pool', lambda e: e.memset(glu[:, :, 0:30], 0.0), [], [bglu])
                    op('pool', lambda e: e.memset(qTp[:], 0.0), [], [bqT])

                    for i in range(NT):
                        t0 = i * 512
                        dma(x[:], x_srcv[:, :, t0:t0 + 512], writes=[bx])
                        rms_rstd(lambda c: x[:, c, :], bx, st0[:], bst0, sq, bsq, 8, 512, D, ps[0], bps[0])
                        for c in range(8):
                            tmp, btmp = (st1, bst1) if c % 2 == 0 else (st2, bst2)
                            stt('dve', tmp[:], x[:, c, :], a1[:, c, 0:1], st0[:], ALU.mult, ALU.mult,
                                [bx, bmod, bst0], [btmp])
                            act(hT[:, c, :], tmp[:], AF.Identity, [btmp, bmod], [bhT], bias=shift1[:, c, 0:1], scale=1.0)

                        stop(31)
                        pi = [0]

                        def proj_fm(col0):
                            k_ = 1 + (pi[0] % 2)
                            pi[0] += 1
                            for kc in range(8):
                                mm(ps[k_][:], win[:, kc, col0:col0 + 128], hT[:, kc, :], kc == 0, kc == 7,
                                   [bwin, bhT], [bps[k_]], lazy=True)
                            return ps[k_], bps[k_]

                        for cc in range(2):
                            pg_, bpg_ = proj_fm(512 + 256 + cc * 128)
                            act(st1[:], pg_[:], AF.Sigmoid, [bpg_], [bst1])
                            stop(311)
                            pa_, bpa_ = proj_fm(512 + cc * 128)
                            tt('dve', glu[:, cc, 30:542], pa_[:], st1[:], ALU.mult, [bpa_, bst1], [bglu])
                            stop(312)
                        for j in range(4):
                            pq_, bpq_ = proj_fm(1024 + j * 128)
                            stop(313)
                            cp('act', qTp[0:64, 0, j, :], pq_[0:64, :], [bpq_], [bqT])
                            cp('act', qTp[64:128, 1, j, :], pq_[64:128, :], [bpq_], [bqT])
                            stop(314)
                            pk_, bpk_ = proj_fm(1536 + j * 128)
                            stop(315)
                            cp('dve', KT[:, j, t0:t0 + 512], pk_[:], [bpk_], [bKT])
                            stop(316)
                            stop(320 + j)

                        stop(32)
                        for s in range(4):
                            kb = 4 * i + s
                            tok = slice(s * 128, (s + 1) * 128)

                            def proj_tm(col0, ncol, k_):
                                for kc in range(8):
                                    mm(ps[k_][:, 0:ncol], hT[:, kc, tok], win[:, kc, col0:col0 + ncol], kc == 0, kc == 7,
                                       [bwin, bhT], [bps[k_]], lazy=True)
                                return ps[k_], bps[k_]

                            pk_, bpk_ = proj_tm(1536, 512, 3)
                            stop(3301)
                            cp('act', kst[:], pk_[:], [bpk_], [bkst])
                            stop(3302)
                            dma(k_o[l, t0 + s * 128:t0 + (s + 1) * 128, :], kst[:], reads=[bkst])
                            stop(331)
                            pv_, bpv_ = proj_tm(2048, 512, 4)
                            cp('act', vst[:], pv_[:], [bpv_], [bvst])
                            cp('dve', Vst[:, kb, :], vst[:], [bvst], [bVst])
                            dma(v_o[l, t0 + s * 128:t0 + (s + 1) * 128, :], vst[:], reads=[bvst])
                            stop(332)
                            pf_, bpf_ = proj_tm(2560, 8, 5)
                            tt('dve', lft[:], pf_[:, 0:8], bfb[:], ALU.add, [bpf_, bpar], [blft])
                            act(lft[:], lft[:], AF.Exp, [blft], [blft], scale=-1.0)
                            act(lft[:], lft[:], AF.Ln, [blft, bconst], [blft], bias=onecol[:, 0:1], scale=1.0)
                            ts('dve', lfs[:], lft[:], -1.0, None, ALU.mult, None, [blft], [blfs])
                            dma(lf_o[l, t0 + s * 128:t0 + (s + 1) * 128, :], lfs[:], reads=[blfs], q='pool')
                            stop(333)
                            mm(ps[5][:, 16:24], tri_f[:], lfs[:], True, True, [bconst, blfs], [bps[5]])
                            mm(ps[5][:, 32:40], ones_f[:], lfs[:], True, True, [bconst, blfs], [bps[5]])
                            tt('dve', cum[:, kb, :], ps[5][:, 16:24], gtot[:], ALU.add, [bps[5], bgtot], [bcum])
                            tt('dve', gtot[:], ps[5][:, 32:40], gtot[:], ALU.add, [bps[5], bgtot], [bgtot])
                            stop(334)

                            pa_, bpa_ = proj_tm(0, 512, 6)
                            act(za[:], pa_[:], AF.Gelu, [bpa_], [bza])
                            stop(335)
                            op('dve', lambda e: e.bn_stats(sml[:, 0:6], za[:, 256:512]), [bza], [bsml])
                            op('dve', lambda e: e.bn_aggr(sml[:, 8:10], sml[:, 0:6]), [bsml], [bsml])
                            ts('dve', sml[:, 9:10], sml[:, 9:10], EPS, None, ALU.add, None, [bsml], [bsml])
                            rsqrt_inplace(sml[:, 9:10], bsml)
                            stop(336)
                            ts('dve', tA[:], za[:, 256:512], sml[:, 8:9], sml[:, 9:10], ALU.subtract, ALU.mult,
                               [bza, bsml], [btA])
                            tt('dve', tA[:], tA[:], alng[:], ALU.mult, [btA, bpar], [btA])
                            tt('dve', vln[:], tA[:], alnb[:], ALU.add, [btA, bpar], [bvln])
                            stop(337)
                            for h in range(4):
                                mm(ps[7][:, h * 64:(h + 1) * 64], wsT_bf[:, h, :], vln[:, h * 64:(h + 1) * 64], True, True,
                                   [bpar, bvln], [bps[7]])
                            for h in range(4):
                                stt('dve', tA2[:, h * 64:(h + 1) * 64], ps[7][:, h * 64:(h + 1) * 64], bsT[:, h:h + 1],
                                    za[:, h * 64:(h + 1) * 64], ALU.add, ALU.mult, [bps[7], bpar, bza], [btA2])
                            tt('dve', tA[:], tA2[:], tA2[:], ALU.mult, [btA2], [btA])
                            op('dve', lambda e: e.reduce_sum(sml[:, 12:13], tA[:], mybir.AxisListType.X), [btA], [bsml])
                            ts('dve', sml[:, 12:13], sml[:, 12:13], 1.0 / 256, EPS, ALU.mult, ALU.add, [bsml], [bsml])
                            rsqrt_inplace(sml[:, 12:13], bsml)
                            stop(338)
                            ts('dve', yan[:], tA2[:], sml[:, 12:13], None, ALU.mult, None, [btA2, bsml], [byan])
                            stop(339)
                            for cA in range(2):
                                mm(ps[7][:, 256 + cA * 128:256 + (cA + 1) * 128], yan[:, cA * 128:(cA + 1) * 128], ident_bf[:],
                                   True, True, [byan, bconst], [bps[7]])
                                cp('act', yT[:, cA, tok], ps[7][:, 256 + cA * 128:256 + (cA + 1) * 128], [bps[7]], [byA])

                        stop(33)
                        for cc in range(2):
                            ts('dve', cacc[:, cc, :], glu[:, cc, 0:512], cwT[:, cc, 0:1], None, ALU.mult, None,
                               [bglu, bpar], [bcacc])
                            for j in range(1, 31):
                                stt('dve', cacc[:, cc, :], glu[:, cc, j:j + 512], cwT[:, cc, j:j + 1], cacc[:, cc, :],
                                    ALU.mult, ALU.add, [bglu, bpar, bcacc], [bcacc])
                            act(cacc[:, cc, :], cacc[:, cc, :], AF.Identity, [bcacc, bpar], [bcacc],
                                bias=cbT[:, cc:cc + 1], scale=1.0)
                        if i == NT - 1:
                            dma(convp_o[l].rearrange("(c p) j -> p c j", p=128), glu[:, :, 512:542], reads=[bglu], sem_buf=bmisc, q='pool')
                        for cc in range(2):
                            mm(ps[1][:], ones_f[:], cacc[:, cc, :], cc == 0, cc == 1, [bconst, bcacc], [bps[1]])
                        for cc in range(2):
                            act(st1[:], cacc[:, cc, :], AF.Square, [bcacc], [bst1])
                            mm(ps[2][:], ones_f[:], st1[:], cc == 0, cc == 1, [bconst, bst1], [bps[2]])
                        ts('dve', st0[:], ps[1][:], 1.0 / 256, None, ALU.mult, None, [bps[1]], [bst0])
                        tt('dve', st2[:], st0[:], st0[:], ALU.mult, [bst0], [bst2])
                        stt('dve', st2[:], ps[2][:], 1.0 / 256, st2[:], ALU.mult, ALU.subtract, [bps[2], bst2], [bst2])
                        ts('dve', st2[:], st2[:], EPS, None, ALU.add, None, [bst2], [bst2])
                        rsqrt_inplace(st2[:], bst2)
                        for cc in range(2):
                            tt('dve', cacc[:, cc, :], cacc[:, cc, :], st0[:], ALU.subtract, [bcacc, bst0], [bcacc])
                            tt('dve', cacc[:, cc, :], cacc[:, cc, :], st2[:], ALU.mult, [bcacc, bst2], [bcacc])
                            act(cacc[:, cc, :], cacc[:, cc, :], AF.Silu, [bcacc, bpar], [bcacc],
                                scale=clgT[:, cc:cc + 1], bias=clbT[:, cc:cc + 1])
                        rms_rstd(lambda c: cacc[:, c, :], bcacc, st1[:], bst1, sq, bsq, 2, 512, 256, ps[1], bps[1])
                        for cc in range(2):
                            tt('dve', yT[:, 2 + cc, :], cacc[:, cc, :], st1[:], ALU.mult, [bcacc, bst1], [byB])
                        for cc in range(2):
                            cp('pool', glu[:, cc, 0:30], glu[:, cc, 512:542], [bglu], [bglu])

                        stop(34)
                        nkb = 4 * i + 4
                        for kb in range(nkb):
                            tt('dve', biasT[:, kb, :], gtot[:], cum[:, kb, :], ALU.subtract, [bgtot, bcum], [bbias])
                        for h in range(8):
                            c, pb = h // 2, 64 * (h % 2)
                            if h % 2 == 0:
                                pO, bpO, pD, bpD = ps[5], bps[5], ps[6], bps[6]
                            else:
                                pO, bpO, pD, bpD = ps[7], bps[7], ps[0], bps[0]

                            def issue_S(kb):
                                j = kb - 4 * i
                                col0 = 0 if j <= 0 else 128 * j
                                ncols = 512 - col0
                                sl_ = kb % 2
                                pS, bpS = ps[3 + sl_], bps[3 + sl_]
                                P_, bP_ = PT[sl_], bPT[sl_]
                                mm(pS[:, 0:ncols], KT[:, c, kb * 128:(kb + 1) * 128], qTp[:, h % 2, c, col0:512],
                                   True, True, [bKT, bqT], [bpS])
                                act(P_[:, 0:ncols], pS[:, 0:ncols], AF.Exp, [bpS, bbias], [bP_],
                                    scale=0.125, bias=biasT[:, kb, h:h + 1])
                                if j >= 0:
                                    tt('pool', P_[:, 0:128], P_[:, 0:128], mask_bf[:], ALU.mult, [bP_, bconst], [bP_])
                                return P_, bP_, col0, ncols

                            cur = issue_S(0)
                            for kb in range(nkb):
                                nxt = issue_S(kb + 1) if kb + 1 < nkb else None
                                P_, bP_, col0, ncols = cur
                                mm(pO[:, col0:512], Vst[:, kb, c * 128:(c + 1) * 128], P_[:, 0:ncols], kb == 0, kb == nkb - 1,
                                   [bVst, bP_], [bpO])
                                mm(pD[:, col0:512], ones_bf[:], P_[:, 0:ncols], kb == 0, kb == nkb - 1,
                                   [bconst, bP_], [bpD])
                                cur = nxt
                            op('dve', lambda e: e.reciprocal(st0[pb:pb + 64, :], pD[pb:pb + 64, :]), [bpD], [bst0])
                            tt('dve', yT[pb:pb + 64, 4 + c, :], pO[pb:pb + 64, :], st0[pb:pb + 64, :], ALU.mult,
                               [bpO, bst0], [byC])
                        rms_rstd(lambda c: yT[:, 4 + c, :], byC, st1[:], bst1, sq, bsq, 4, 512, 512, ps[0], bps[0])
                        for c in range(4):
                            tt('dve', yT[:, 4 + c, :], yT[:, 4 + c, :], st1[:], ALU.mult, [byC, bst1], [byC])

                        stop(35)
                        for dc in range(8):
                            k_ = 1 + (dc % 2)
                            for kc in range(8):
                                mm(ps[k_][:], wout[:, kc, dc * 128:(dc + 1) * 128], yT[:, kc, :], kc == 0, kc == 7,
                                   [bwout, byA, byB, byC], [bps[k_]], lazy=True)
                            stt('dve', x[:, dc, :], ps[k_][:], gate1[:, dc, 0:1], x[:, dc, :], ALU.mult, ALU.add,
                                [bps[k_], bmod, bx], [bx])
                        dma(xsv[:, :, t0:t0 + 512], x[:], reads=[bx])
                        stop(3)
                    fw.barrier()

            with ExitStack() as big:
                wup = sb(big, "wup", [128, 8, 2 * DFF], BF16); bwup = Buf("wup")
                wdn = sb(big, "wdn", [128, 22, D], BF16); bwdn = Buf("wdn")
                with ExitStack() as st:
                    stg = [sb(st, "stgC%d" % i, [128, DFF]) for i in range(3)]
                    NSTG = 3
                    engs = ['dve', 'pool']
                    n = 0
                    for kc in range(8):
                        for hf in range(2):
                            s_ = n % NSTG
                            dma(stg[s_][:, 0:DFF], w_up_d[l, kc * 128:(kc + 1) * 128, hf * DFF:(hf + 1) * DFF],
                                writes=[bstg[s_]])
                            cp(engs[n % 2], wup[:, kc, hf * DFF:(hf + 1) * DFF], stg[s_][:, 0:DFF], [bstg[s_]], [bwup])
                            n += 1
                    for j in range(22):
                        s_ = n % NSTG
                        dma(stg[s_][:, 0:D], w_down_d[l, j * 128:(j + 1) * 128, :], writes=[bstg[s_]])
                        cp(engs[n % 2], wdn[:, j, :], stg[s_][:, 0:D], [bstg[s_]], [bwdn])
                        n += 1
                    fw.barrier()
                    stop(4)
                sample_phase2(wup, bwup, wdn, bwdn)

                with ExitStack() as sc:
                    hT = sb(sc, "hT2", [128, 8, 512], BF16); bhT = Buf("hT2")
                    hmid = sb(sc, "hmid", [128, 22, 512], BF16); bhmid = Buf("hmid")
                    raw = [sb(sc, "raw%d" % i, [128, 514]) for i in range(4)]
                    braw = [Buf("raw%d" % i) for i in range(4)]
                    acc = [sb(sc, "acc%d" % i, [128, 512]) for i in range(4)]
                    bacc = [Buf("acc%d" % i) for i in range(4)]
                    sq = [sb(sc, "sq2%d" % i, [128, 512], BF16) for i in range(2)]
                    bsq = [Buf("sq2%d" % i) for i in range(2)]
                    st0 = sb(sc, "st02", [128, 512]); bst0 = Buf("st02")
                    st1, bst1, st2, bst2 = acc[0], bacc[0], acc[1], bacc[1]

                    op('pool', lambda e: e.memset(halo[:], 0.0), [], [bhalo])
                    for i in range(NT):
                        t0 = i * 512
                        dma(x[:], xsv[:, :, t0:t0 + 512], writes=[bx])
                        rms_rstd(lambda c: x[:, c, :], bx, st0[:], bst0, sq, bsq, 8, 512, D, ps[0], bps[0])
                        for c in range(8):
                            tmp, btmp = (st1, bst1) if c % 2 == 0 else (st2, bst2)
                            stt('dve', tmp[:], x[:, c, :], a2[:, c, 0:1], st0[:], ALU.mult, ALU.mult,
                                [bx, bmod, bst0], [btmp])
                            act(hT[:, c, :], tmp[:], AF.Identity, [btmp, bmod], [bhT], bias=shift2[:, c, 0:1], scale=1.0)
                        for j in range(22):
                            for hf in range(2):
                                ch = hf * 22 + j
                                k_ = 1 + hf + 2 * (j % 2)
                                for kc in range(8):
                                    mm(ps[k_][:], wup[:, kc, ch * 128:(ch + 1) * 128], hT[:, kc, :], kc == 0, kc == 7,
                                       [bwup, bhT], [bps[k_]], lazy=True)
                                r_, br_ = raw[hf + 2 * (j % 2)], braw[hf + 2 * (j % 2)]
                                a_, ba_ = acc[hf + 2 * (j % 2)], bacc[hf + 2 * (j % 2)]
                                cp('act', r_[:, 2:514], ps[k_][:], [bps[k_]], [br_])
                                cp('pool', r_[:, 0:2], halo[:, ch, :], [bhalo], [br_])
                                act(a_[:], ps[k_][:], AF.Identity, [bps[k_], bpar], [ba_],
                                    scale=fwT[:, ch, 2:3], bias=fbT[:, ch:ch + 1])
                                stt('dve', a_[:], r_[:, 0:512], fwT[:, ch, 0:1], a_[:], ALU.mult, ALU.add,
                                    [br_, bpar, ba_], [ba_])
                                stt('dve', a_[:], r_[:, 1:513], fwT[:, ch, 1:2], a_[:], ALU.mult, ALU.add,
                                    [br_, bpar, ba_], [ba_])
                                cp('pool', halo[:, ch, :], r_[:, 512:514], [br_], [bhalo])
                            ag_, bag_ = acc[2 * (j % 2)], bacc[2 * (j % 2)]
                            au_, bau_ = acc[1 + 2 * (j % 2)], bacc[1 + 2 * (j % 2)]
                            act(ag_[:], ag_[:], AF.Silu, [bag_], [bag_])
                            tt('dve', hmid[:, j, :], ag_[:], au_[:], ALU.mult, [bag_, bau_], [bhmid])
                        if i == NT - 1:
                            dma(ffnp_o[l].rearrange("(c p) k -> p c k", p=128), halo[:], reads=[bhalo], sem_buf=bmisc, q='pool')
                        for dc in range(8):
                            k_ = 5 + (dc % 2)
                            for j in range(22):
                                mm(ps[k_][:], wdn[:, j, dc * 128:(dc + 1) * 128], hmid[:, j, :], j == 0, j == 21,
                                   [bwdn, bhmid], [bps[k_]], lazy=True)
                            stt('dve', x[:, dc, :], ps[k_][:], gate2[:, dc, 0:1], x[:, dc, :], ALU.mult, ALU.add,
                                [bps[k_], bmod, bx], [bx])
                        if not last:
                            dma(xsv[:, :, t0:t0 + 512], x[:], reads=[bx])
                        else:
                            rms_rstd(lambda c: x[:, c, :], bx, st0[:], bst0, sq, bsq, 8, 512, D, ps[0], bps[0])
                            for c in range(8):
                                stt('dve', x[:, c, :], x[:, c, :], fgT[:, c:c + 1], st0[:], ALU.mult, ALU.mult,
                                    [bx, bpar, bst0], [bx])
                            dma(yT_o.rearrange("(c p) t -> p c t", p=128)[:, :, t0:t0 + 512], x[:], reads=[bx])
                    fw.barrier()
        fw.barrier()
    fw.barrier()
    fw.close()
    return nc


_NC_CACHE = {}


def _host_inputs(inp):
    f = np.float32
    A = lambda a: np.ascontiguousarray(np.asarray(a), dtype=f)
    shared = {}
    shared["w_ada"] = A(inp["w_ada"])
    shared["badaT"] = A(np.asarray(inp["b_ada"]).reshape(NL, 48, 128).transpose(0, 2, 1))
    rep = lambda g: A(np.broadcast_to(np.asarray(g).reshape(NL, 8, 128).transpose(0, 2, 1)[..., None], (NL, 128, 8, 1 + NS)))
    shared["g1r"] = rep(inp["norm1_g"])
    shared["g2r"] = rep(inp["norm2_g"])
    shared["mixgT"] = A(np.asarray(inp["mix_g"]).reshape(NL, 8, 128).transpose(0, 2, 1))
    shared["fgT"] = A(np.asarray(inp["final_g"]).reshape(8, 128).T)
    for k in ("w_in", "w_out", "w_up", "w_down"):
        shared[k] = A(inp[k])
    shared["bfb"] = A(np.broadcast_to(np.asarray(inp["b_forget"])[:, None, :], (NL, 128, 8)))
    shared["alng"] = A(np.broadcast_to(np.asarray(inp["a_ln_g"])[:, None, :], (NL, 128, 256)))
    shared["alnb"] = A(np.broadcast_to(np.asarray(inp["a_ln_b"])[:, None, :], (NL, 128, 256)))
    shared["wsT"] = A(np.asarray(inp["w_s"]).transpose(0, 3, 1, 2))
    shared["bsT"] = A(np.asarray(inp["b_s"]).transpose(0, 2, 1))
    shared["cwT"] = A(np.asarray(inp["conv_w"]).transpose(0, 2, 1).reshape(NL, 2, 128, 31).transpose(0, 2, 1, 3))
    cm = lambda a: A(np.asarray(a).reshape(NL, 2, 128).transpose(0, 2, 1))
    shared["cbT"] = cm(inp["conv_b"]); shared["clgT"] = cm(inp["conv_ln_g"]); shared["clbT"] = cm(inp["conv_ln_b"])
    shared["fwT"] = A(np.asarray(inp["ffn_conv_w"]).transpose(0, 2, 1).reshape(NL, 44, 128, 3).transpose(0, 2, 1, 3))
    shared["fbT"] = A(np.asarray(inp["ffn_conv_b"]).reshape(NL, 44, 128).transpose(0, 2, 1))
    shared["tri"] = np.triu(np.ones((128, 128), f))
    shared["ident"] = np.eye(128, dtype=f)
    shared["alngT"] = cm(inp["a_ln_g"]); shared["alnbT"] = cm(inp["a_ln_b"])
    rep64 = lambda a: A(np.repeat(np.asarray(a), 64, axis=1).reshape(NL, 2, 128).transpose(0, 2, 1))
    shared["ws00T"] = rep64(np.asarray(inp["w_s"])[:, :, 0, 0])
    shared["bs0T"] = rep64(np.asarray(inp["b_s"])[:, :, 0])
    sel = np.zeros((NS, NS, 128), f)
    for n in range(NS):
        sel[n, n, :] = 1.0
    shared["sel"] = sel
    bigm = np.full((128, 8), -30000.0, f); bigm[0, :] = 0.0
    shared["bigm"] = bigm
    blk = np.zeros((8, 512), f)
    for h in range(8):
        blk[h, h * 64:(h + 1) * 64] = 1.0
    shared["blkm"] = blk
    shared["iot"] = np.ascontiguousarray(np.broadcast_to(np.arange(128, dtype=np.int32)[:, None], (128, NS * NPG)))
    shared["cache_k"] = A(inp["cache_k"]).reshape(NL * NPHYS * 128, 512)
    shared["cache_v"] = A(inp["cache_v"]).reshape(NL * NPHYS * 128, 512)
    shared["cache_f"] = A(inp["cache_logf"]).reshape(NL * NPHYS * 128, 8)
    xsm = np.asarray(inp["x_sample"]); stc = np.asarray(inp["state_conv"]); stf = np.asarray(inp["state_ffn_conv"])
    ptab = np.asarray(inp["page_table"]).astype(np.int32)
    xp = np.asarray(inp["x_prompt"]); cp_ = np.asarray(inp["c_prompt"]); cs = np.asarray(inp["c_sample"])
    maps = []
    for c in range(NCORES):
        b = c // 2
        m = dict(shared)
        m["xT"] = A(xp[b].T)
        m["cT"] = A(np.concatenate([cp_[b:b + 1], cs[NS * c:NS * (c + 1)]], 0).T)
        sl = slice(NS * c, NS * (c + 1))
        m["xsT0"] = A(xsm[sl, 0, :].T.reshape(8, 128, NS).transpose(1, 0, 2))
        m["ptb"] = np.ascontiguousarray(np.broadcast_to(ptab[sl].reshape(1, NS * NPG), (128, NS * NPG)))
        m["stconvT"] = A(stc[:, sl].transpose(0, 3, 1, 2).reshape(NL, 2, 128, NS, 30).transpose(0, 2, 1, 3, 4))
        m["sffnT"] = A(stf[:, sl].transpose(0, 3, 1, 2).reshape(NL, 44, 128, NS, 2).transpose(0, 2, 1, 3, 4))
        maps.append(m)
    return maps


def kernel(**inp):
    if "nc" not in _NC_CACHE:
        _NC_CACHE["nc"] = build_program()
    nc = _NC_CACHE["nc"]
    maps = _host_inputs(inp)
    res = run_bass_kernel_spmd(nc, maps, core_ids=list(range(NCORES))).results
    f = np.float32
    B = 4
    y_prompt = np.stack([res[2 * b]["yT"].T for b in range(B)]).astype(f)
    k_p = np.stack([res[2 * b]["k_o"] for b in range(B)], 1).reshape(NL, B, T, 8, 64).astype(f)
    v_p = np.stack([res[2 * b]["v_o"] for b in range(B)], 1).reshape(NL, B, T, 8, 64).astype(f)
    lf_p = np.stack([res[2 * b]["lf_o"] for b in range(B)], 1).astype(f)
    conv_p = np.stack([res[2 * b]["convp"].transpose(0, 2, 1) for b in range(B)], 1).astype(f)
    ffn_p = np.stack([res[2 * b]["ffnp"].transpose(0, 2, 1) for b in range(B)], 1).astype(f)
    cat = lambda fn, ax: np.concatenate([fn(res[c]) for c in range(NCORES)], ax).astype(f)
    y_s = cat(lambda r: r["ysT"].transpose(2, 1, 0).reshape(NS, 1, D), 0)
    k_s = cat(lambda r: r["ks_o"].reshape(NL, NS, 1, 8, 64), 1)
    v_s = cat(lambda r: r["vs_o"].reshape(NL, NS, 1, 8, 64), 1)
    lf_s = cat(lambda r: r["lfs_o"].reshape(NL, NS, 1, 8), 1)
    conv_s = cat(lambda r: r["convs"].transpose(0, 3, 4, 2, 1).reshape(NL, NS, 30, 256), 1)
    ffn_s = cat(lambda r: r["ffns"].transpose(0, 3, 4, 2, 1).reshape(NL, NS, 2, 2 * DFF), 1)
    chv_s = cat(lambda r: r["chv"].transpose(0, 3, 2, 1).reshape(NL, NS, 1, 256), 1)
    return (y_prompt, y_s, k_p, v_p, lf_p, conv_p, ffn_p, k_s, v_s, lf_s, conv_s, ffn_s, chv_s)
```

```python
import numpy as np
import concourse.bass as bass
import concourse.mybir as mybir
from concourse.bass_utils import run_bass_kernel_spmd

F32 = mybir.dt.float32
BF16 = mybir.dt.bfloat16
I32 = mybir.dt.int32
ALU = mybir.AluOpType
AF = mybir.ActivationFunctionType

D = 1024
NL = 2
T = 4096
NS = 4
NPG = 64
DFF = 2816
INW = 2568
EPS = 1e-6
NCORES = 8
NPHYS = 2560
STOP_AT = None


class _Stop(Exception):
    pass


class Buf:
    def __init__(self, name, excl=False):
        self.name = name
        self.w = None
        self.r = {}
        self.excl = excl


class FW:
    def __init__(self, nc):
        self.nc = nc
        self.eng = {'pe': nc.tensor, 'act': nc.scalar, 'dve': nc.vector, 'pool': nc.gpsimd, 'sp': nc.sync}
        self.sems, self.cnt = {}, {}
        self.seen = {k: {} for k in self.eng}
        self._cms = []
        for k in ('pe', 'act', 'dve', 'pool'):
            self._mksem(k)

    def _mksem(self, key):
        cm = self.nc.semaphore("s%d" % len(self.sems))
        self.sems[key] = cm.__enter__()
        self._cms.append(cm)
        self.cnt[key] = 0

    def close(self):
        for cm in reversed(self._cms):
            cm.__exit__(None, None, None)

    def _wait(self, e, key, val):
        if val <= 0 or (e == 'pe' and key == 'pe'):
            return
        if self.seen[e].get(key, 0) >= val:
            return
        self.eng[e].wait_ge(self.sems[key], val)
        self.seen[e][key] = val

    def _deps(self, e, reads, writes, skip=None):
        for b in reads:
            if b.w is not None:
                self._wait(e, *b.w)
            if b.excl:
                for k, v in b.r.items():
                    if k != e:
                        self._wait(e, k, v)
        for b in writes:
            if b.w is not None and b.w[0] != skip:
                self._wait(e, *b.w)
            for k, v in b.r.items():
                self._wait(e, k, v)

    def _mark(self, key, val, reads, writes):
        for b in reads:
            b.r[key] = val
        for b in writes:
            b.w = (key, val)
            b.r = {}

    def op(self, e, fn, reads=(), writes=(), inc=True):
        self._deps(e, reads, writes)
        ins = fn(self.eng[e])
        if inc:
            self.cnt[e] += 1
            ins.then_inc(self.sems[e], 1)
            self._mark(e, self.cnt[e], reads, writes)
        else:
            self._mark(e, self.cnt[e] + 1, reads, writes)

    def dma(self, out, in_, reads=(), writes=(), sem_buf=None, q='sp'):
        b0 = sem_buf if sem_buf is not None else (writes[0] if writes else reads[0])
        key = ('dma', id(b0))
        if key not in self.sems:
            self._mksem(key)
        self._deps(q, reads, writes, skip=key)
        ins = self.eng[q].dma_start(out=out, in_=in_)
        self.cnt[key] += 16
        ins.then_inc(self.sems[key], 16)
        self._mark(key, self.cnt[key], reads, writes)

    def barrier(self):
        for e in self.eng:
            for key in self.sems:
                self._wait(e, key, self.cnt[key])


def build_program():
    nc = bass.Bass("TRN2", target_bir_lowering=False)
    NT = T // 512
    NKB = T // 128

    def din(name, shape, dt=F32):
        return nc.dram_tensor(name, list(shape), dt, kind="ExternalInput").ap()

    def dout(name, shape):
        return nc.dram_tensor(name, list(shape), F32, kind="ExternalOutput").ap()

    xT_d = din("xT", [D, T])
    cT_d = din("cT", [D, 1 + NS])
    w_ada_d = din("w_ada", [NL, D, 6 * D])
    badaT_d = din("badaT", [NL, 128, 48])
    g1r_d = din("g1r", [NL, 128, 8, 1 + NS])
    g2r_d = din("g2r", [NL, 128, 8, 1 + NS])
    mixgT_d = din("mixgT", [NL, 128, 8])
    fgT_d = din("fgT", [128, 8])
    w_in_d = din("w_in", [NL, D, INW])
    w_out_d = din("w_out", [NL, D, D])
    w_up_d = din("w_up", [NL, D, 2 * DFF])
    w_down_d = din("w_down", [NL, DFF, D])
    bfb_d = din("bfb", [NL, 128, 8])
    alng_d = din("alng", [NL, 128, 256])
    alnb_d = din("alnb", [NL, 128, 256])
    wsT_d = din("wsT", [NL, 128, 4, 128])
    bsT_d = din("bsT", [NL, 128, 4])
    cwT_d = din("cwT", [NL, 128, 2, 31])
    cbT_d = din("cbT", [NL, 128, 2])
    clgT_d = din("clgT", [NL, 128, 2])
    clbT_d = din("clbT", [NL, 128, 2])
    fwT_d = din("fwT", [NL, 128, 44, 3])
    fbT_d = din("fbT", [NL, 128, 44])
    tri_d = din("tri", [128, 128])
    ident_d = din("ident", [128, 128])

    xsT0_d = din("xsT0", [128, 8, NS])
    ptb_d = din("ptb", [128, NS * NPG], I32)
    iot_d = din("iot", [128, NS * NPG], I32)
    sel_d = din("sel", [NS, NS, 128])
    bigm_d = din("bigm", [128, 8])
    blkm_d = din("blkm", [8, 512])
    ckv_d = [din("cache_kv%d" % l_, [NPHYS * 128, 1024]) for l_ in range(NL)]
    cf_d = din("cache_f", [NL * NPHYS * 128, 8])
    stconv_d = din("stconvT", [NL, 128, 2, NS, 30])
    sffn_d = din("sffnT", [NL, 128, 44, NS, 2])
    alngT_d = din("alngT", [NL, 128, 2])
    alnbT_d = din("alnbT", [NL, 128, 2])
    ws00T_d = din("ws00T", [NL, 128, 2])
    bs0T_d = din("bs0T", [NL, 128, 2])
    ysT_o = dout("ysT", [128, 8, NS])
    ks_o = dout("ks_o", [NL, NS, 512])
    vs_o = dout("vs_o", [NL, NS, 512])
    lfs_o = dout("lfs_o", [NL, NS, 8])
    convs_o = dout("convs", [NL, 128, 2, NS, 30])
    ffns_o = dout("ffns", [NL, 128, 44, NS, 2])
    chv_o = dout("chv", [NL, 128, 2, NS])

    yT_o = dout("yT", [D, T])
    k_o = dout("k_o", [NL, T, 512])
    v_o = dout("v_o", [NL, T, 512])
    lf_o = dout("lf_o", [NL, T, 8])
    convp_o = dout("convp", [NL, 256, 30])
    ffnp_o = dout("ffnp", [NL, 2 * DFF, 2])

    xs_d = nc.dram_tensor("xs_scr", [D, T], F32).ap()

    fw = FW(nc)
    op, dma = fw.op, fw.dma

    def act(out, in_, func, reads, writes, **kw):
        op('act', lambda e: e.activation(out=out, in_=in_, func=func, **kw), reads, writes)

    def mm(out, lhsT, rhs, start, stop, reads, writes, lazy=False):
        op('pe', lambda e: e.matmul(out, lhsT, rhs, start=start, stop=stop), reads, writes,
           inc=(bool(stop) or not lazy))

    def tt(eng, out, a, b, o, reads, writes):
        op(eng, lambda e: e.tensor_tensor(out, a, b, o), reads, writes)

    def ts(eng, out, a, s1, s2, o0, o1, reads, writes):
        if o1 is None:
            op(eng, lambda e: e.tensor_scalar(out, a, s1, None, o0), reads, writes)
        else:
            op(eng, lambda e: e.tensor_scalar(out, a, s1, s2, o0, o1), reads, writes)

    def stt(eng, out, in0, scalar, in1, o0, o1, reads, writes):
        op(eng, lambda e: e.scalar_tensor_tensor(out, in0, scalar, in1, o0, o1), reads, writes)

    def cp(eng, out, in_, reads, writes):
        if eng == 'act':
            act(out, in_, AF.Copy, reads, writes)
        else:
            op(eng, lambda e: e.tensor_copy(out, in_), reads, writes)

    def rsqrt_inplace(tile_ap, buf):
        act(tile_ap, tile_ap, AF.Sqrt, [buf], [buf])
        op('dve', lambda e: e.reciprocal(tile_ap, tile_ap), [buf], [buf])

    from contextlib import ExitStack
    import contextlib
    with ExitStack() as top:
        top.enter_context(contextlib.suppress(_Stop))
        top.enter_context(nc.allow_non_contiguous_dma(reason="small strided parameter loads"))

        uid = [0]

        def sb(stack, name, shape, dt=F32):
            uid[0] += 1
            return stack.enter_context(nc.sbuf_tensor("sb%d_%s" % (uid[0], name), list(shape), dt))

        ps = [top.enter_context(nc.psum_tensor("ps%d" % i, [128, 512], F32)) for i in range(8)]
        bps = [Buf("ps%d" % i, excl=True) for i in range(8)]

        x = sb(top, "x", [128, 8, 512]); bx = Buf("x")
        ones_bf = sb(top, "ones_bf", [128, 128], BF16)
        ones_f = sb(top, "ones_f", [128, 128])
        tri_f = sb(top, "tri_f", [128, 128])
        mask_bf = sb(top, "mask_bf", [128, 128], BF16)
        ident_bf = sb(top, "ident_bf", [128, 128], BF16)
        onecol = sb(top, "onecol", [128, 1])
        bconst = Buf("const")
        silu_c = sb(top, "silu_c", [128, 8, 1 + NS]); bsc = Buf("silu_c")
        mod = sb(top, "mod", [128, 48, 1 + NS]); bmod = Buf("mod")
        a1 = sb(top, "a1", [128, 8, 1 + NS]); a2 = sb(top, "a2", [128, 8, 1 + NS])
        g1r = sb(top, "g1r", [128, 8, 1 + NS]); g2r = sb(top, "g2r", [128, 8, 1 + NS])
        badaT = sb(top, "badaT", [128, 48])
        mixgT = sb(top, "mixgT", [128, 8]); fgT = sb(top, "fgT", [128, 8])
        bfb = sb(top, "bfb", [128, 8])
        wsT_bf = sb(top, "wsT_bf", [128, 4, 128], BF16)
        bsT = sb(top, "bsT", [128, 4])
        cwT = sb(top, "cwT", [128, 2, 31]); cbT = sb(top, "cbT", [128, 2])
        clgT = sb(top, "clgT", [128, 2]); clbT = sb(top, "clbT", [128, 2])
        fwT = sb(top, "fwT", [128, 44, 3]); fbT = sb(top, "fbT", [128, 44])
        bpar = Buf("params")
        bstg = [Buf("stg%d" % i) for i in range(4)]
        bmisc = Buf("misc")
        cum = sb(top, "cum", [128, NKB, 8]); bcum = Buf("cum")
        gtot = sb(top, "gtot", [128, 8]); bgtot = Buf("gtot")
        halo = sb(top, "halo", [128, 44, 2]); bhalo = Buf("halo")

        xs = sb(top, "xs", [128, 8, NS]); bxs = Buf("xs")
        bigm = sb(top, "bigm", [128, 8])
        idx0 = sb(top, "idx0", [128, NS * NPG])
        idxl = sb(top, "idxl", [128, NS * NPG], I32); bidx = Buf("idx")
        alngT = sb(top, "alngT", [128, 2]); alnbT = sb(top, "alnbT", [128, 2])
        ws00T = sb(top, "ws00T", [128, 2]); bs0T = sb(top, "bs0T", [128, 2])
        bpool = [Buf("pq%d" % i) for i in range(5)]
        with ExitStack() as tmpsc:
            ptb = sb(tmpsc, "ptb", [128, NS * NPG], I32)
            iot = sb(tmpsc, "iot", [128, NS * NPG], I32)
            for dst_, src_ in ((xs, xsT0_d), (bigm, bigm_d), (ptb, ptb_d), (iot, iot_d), (tri_f, tri_d),
                               (ones_f, ident_d), (fgT, fgT_d)):
                dma(dst_[:], src_, writes=[bconst], sem_buf=bpar)
            dma(silu_c[:], cT_d.rearrange("(c p) n -> p c n", p=128), writes=[bsc], sem_buf=bpar)
            fw.barrier()
            iof = sb(tmpsc, "iof", [128, NS * NPG])
            cp('dve', idx0[:], ptb[:], [bconst], [bconst])
            cp('dve', iof[:], iot[:], [bconst], [bconst])
            stt('dve', idx0[:], idx0[:], 128.0, iof[:], ALU.mult, ALU.add, [bconst], [bconst])
            op('dve', lambda e: e.tensor_copy(mask_bf[:], tri_f[:]), [bconst], [bconst])
            op('dve', lambda e: e.tensor_copy(ident_bf[:], ones_f[:]), [bconst], [bconst])
            op('pool', lambda e: e.memset(ones_f[:], 1.0), [], [bconst])
            op('pool', lambda e: e.memset(ones_bf[:], 1.0), [], [bconst])
            op('pool', lambda e: e.memset(onecol[:], 1.0), [], [bconst])
            act(silu_c[:], silu_c[:], AF.Silu, [bsc], [bsc])
            fw.barrier()

        def stop(k):
            if STOP_AT == k:
                raise _Stop()

        for l in range(NL):
            last = (l == NL - 1)
            stop(0)
            for dst, src in ((badaT, badaT_d), (g1r, g1r_d), (g2r, g2r_d), (mixgT, mixgT_d), (bfb, bfb_d),
                             (bsT, bsT_d), (cwT, cwT_d),
                             (cbT, cbT_d), (clgT, clgT_d), (clbT, clbT_d), (fwT, fwT_d), (fbT, fbT_d),
                             (alngT, alngT_d), (alnbT, alnbT_d), (ws00T, ws00T_d), (bs0T, bs0T_d)):
                dma(dst[:], src[l], writes=[bpar])
            with ExitStack() as tmpsc:
                wsT_f = sb(tmpsc, "wsT_f", [128, 4, 128])
                dma(wsT_f[:], wsT_d[l], writes=[bpar])
                fw.barrier()
                for h in range(4):
                    tt('dve', wsT_bf[:, h, :], wsT_f[:, h, :], tri_f[:], ALU.mult, [bpar, bconst], [bpar])
                fw.barrier()

            with ExitStack() as st:
                stg = [sb(st, "stgA%d" % i, [128, 8, 512]) for i in range(4)]
                wv = w_ada_d[l].rearrange("(c p) n -> p c n", p=128)
                for g in range(12):
                    s_ = g % 4
                    dma(stg[s_][:], wv[:, :, g * 512:(g + 1) * 512], writes=[bstg[s_]])
                    for j in range(4):
                        m = g * 4 + j
                        pb_ = bps[m % 2]
                        pt_ = ps[m % 2]
                        for kc in range(8):
                            mm(pt_[:, 0:1 + NS], stg[s_][:, kc, j * 128:(j + 1) * 128], silu_c[:, kc, :],
                               kc == 0, kc == 7, [bstg[s_], bsc], [pb_], lazy=True)
                        act(mod[:, m, :], pt_[:, 0:1 + NS], AF.Identity, [pb_, bpar], [bmod],
                            bias=badaT[:, m:m + 1], scale=1.0)
                ts('dve', a1[:], mod[:, 8:16, :], 1.0, None, ALU.add, None, [bmod], [bmod])
                tt('dve', a1[:], a1[:], g1r[:], ALU.mult, [bmod, bpar], [bmod])
                ts('dve', a2[:], mod[:, 32:40, :], 1.0, None, ALU.add, None, [bmod], [bmod])
                tt('dve', a2[:], a2[:], g2r[:], ALU.mult, [bmod, bpar], [bmod])
                fw.barrier()
                stop(1)
            shift1 = mod[:, 0:8, :]; gate1 = mod[:, 16:24, :]
            shift2 = mod[:, 24:32, :]; gate2 = mod[:, 40:48, :]

            x_src = xT_d if l == 0 else xs_d
            xsv = xs_d.rearrange("(c p) t -> p c t", p=128)
            x_srcv = x_src.rearrange("(c p) t -> p c t", p=128)

            def rms_rstd(xt, bxt, rstd, brstd, sqs, bsqs, nfeat_chunks, ncols, denom, pst, bpst):
                for c in range(nfeat_chunks):
                    act(sqs[c % 2][:, 0:ncols], xt(c), AF.Square, [bxt], [bsqs[c % 2]])
                    mm(pst[:, 0:ncols], ones_bf[:], sqs[c % 2][:, 0:ncols], c == 0, c == nfeat_chunks - 1,
                       [bconst, bsqs[c % 2]], [bpst])
                ts('dve', rstd, pst[:, 0:ncols], 1.0 / denom, EPS, ALU.mult, ALU.add, [bpst], [brstd])
                rsqrt_inplace(rstd, brstd)

            def idma(out, in_, idx_ap, reads, writes, sem_buf):
                key = ('dma', id(sem_buf))
                if key not in fw.sems:
                    fw._mksem(key)
                fw._deps('pool', reads, writes, skip=key)
                ins = nc.gpsimd.indirect_dma_start(out=out, out_offset=None, in_=in_,
                                                   in_offset=bass.IndirectOffsetOnAxis(ap=idx_ap, axis=0))
                fw.cnt[key] += 16
                ins.then_inc(fw.sems[key], 16)
                fw._mark(key, fw.cnt[key], reads, writes)

            sc1 = slice(1, 1 + NS)

            def chan_stats(chunks, bsrc, w2, bw2, m_t, r_t, bmr, denom, want_mean):
                nch = len(chunks)
                if want_mean:
                    for i_, c_ in enumerate(chunks):
                        mm(ps[6][:, 0:NS], ones_f[:], c_, i_ == 0, i_ == nch - 1, [bconst, bsrc], [bps[6]])
                    ts('dve', m_t, ps[6][:, 0:NS], 1.0 / denom, None, ALU.mult, None, [bps[6]], [bmr])
                for i_, c_ in enumerate(chunks):
                    tt('dve', w2[:, i_, :], c_, c_, ALU.mult, [bsrc], [bw2])
                for i_ in range(nch):
                    mm(ps[6][:, 8:8 + NS], ones_f[:], w2[:, i_, :], i_ == 0, i_ == nch - 1, [bconst, bw2], [bps[6]])
                ts('dve', r_t, ps[6][:, 8:8 + NS], 1.0 / denom, EPS, ALU.mult, ALU.add, [bps[6]], [bmr])
                if want_mean:
                    tt('dve', w2[:, 0, :], m_t, m_t, ALU.mult, [bmr], [bw2])
                    tt('dve', r_t, r_t, w2[:, 0, :], ALU.subtract, [bmr, bw2], [bmr])
                rsqrt_inplace(r_t, bmr)

            def sample_norm(h_bf, bh, a_t, sh_t, ss):
                sqs_ = sb(ss, "s_sq", [128, 8, NS], BF16); bsqs_ = Buf("s_sq")
                rs_ = sb(ss, "s_rs", [128, NS]); brs_ = Buf("s_rs")
                hf_ = sb(ss, "s_hf", [128, 8, NS]); bhf_ = Buf("s_hf")
                for c in range(8):
                    act(sqs_[:, c, :], xs[:, c, :], AF.Square, [bxs], [bsqs_])
                for c in range(8):
                    mm(ps[0][:, 0:NS], ones_bf[:], sqs_[:, c, :], c == 0, c == 7, [bconst, bsqs_], [bps[0]])
                ts('dve', rs_[:], ps[0][:, 0:NS], 1.0 / D, EPS, ALU.mult, ALU.add, [bps[0]], [brs_])
                rsqrt_inplace(rs_[:], brs_)
                for c in range(8):
                    tt('dve', hf_[:, c, :], xs[:, c, :], rs_[:], ALU.mult, [bxs, brs_], [bhf_])
                tt('dve', hf_[:], hf_[:], a_t, ALU.mult, [bhf_, bmod], [bhf_])
                tt('dve', hf_[:], hf_[:], sh_t, ALU.add, [bhf_, bmod], [bhf_])
                cp('dve', h_bf[:], hf_[:], [bhf_], [bh])

            def sample_phase1(win, bwin, wout, bwout):
                with ExitStack() as ss:
                    hs = sb(ss, "s_hs", [128, 8, NS], BF16); bhs = Buf("s_hs")
                    zs = sb(ss, "s_zs", [128, 8, NS]); bzs = Buf("s_zs")
                    w1 = sb(ss, "s_w1", [128, 8, NS]); bw1 = Buf("s_w1")
                    w2 = sb(ss, "s_w2", [128, 8, NS]); bw2 = Buf("s_w2")
                    w3 = sb(ss, "s_w3", [128, 8, NS]); bw3 = Buf("s_w3")
                    mt = sb(ss, "s_mt", [128, NS]); rt = sb(ss, "s_rt", [128, NS]); bmr = Buf("s_mr")
                    tm = sb(ss, "s_tm", [NS, 3, 512]); btm = Buf("s_tm")
                    lftm = sb(ss, "s_lftm", [NS, 8]); blftm = Buf("s_lftm")
                    ysT = sb(ss, "s_ysT", [128, 8, NS], BF16); bys = Buf("s_ysT")
                    ycs = sb(ss, "s_ycs", [128, 4, NS]); bycs = Buf("s_ycs")
                    cst = sb(ss, "s_cst", [128, 2, NS, 31]); bcst = Buf("s_cst")
                    p31 = sb(ss, "s_p31", [128, 2, NS, 31]); bp31 = Buf("s_p31")
                    KVp = [sb(ss, "s_KVp%d" % i, [128, 1024]) for i in range(4)]
                    prod = sb(ss, "s_prod", [128, 512]); bprod = Buf("s_prod")
                    qrep = sb(ss, "s_qrep", [128, 512]); bqrep = Buf("s_qrep")
                    krep = sb(ss, "s_krep", [128, 512]); bkrep = Buf("s_krep")
                    vrep = sb(ss, "s_vrep", [128, 512]); bvrep = Buf("s_vrep")
                    LF = sb(ss, "s_LF", [128, NPG, 8])
                    cumi = sb(ss, "s_cumi", [128, NPG + 1, 8]); bcumi = Buf("s_cumi")
                    S_ = sb(ss, "s_S", [128, NPG + 1, 8]); bS = Buf("s_S")
                    Pm = sb(ss, "s_P", [128, NPG + 1, 8]); bPm = Buf("s_P")
                    gs = sb(ss, "s_gs", [128, 8]); bgs = Buf("s_gs")
                    lfrep = sb(ss, "s_lfrep", [128, 8]); blfrep = Buf("s_lfrep")
                    Om = sb(ss, "s_Om", [8, 512]); bOm = Buf("s_Om")
                    den = sb(ss, "s_den", [128, 2]); bden = Buf("s_den")
                    bKV, bLF = bpool[0:4], bpool[4]
                    selt = sb(ss, "selt", [NS, NS, 128])
                    blkm = sb(ss, "blkm", [8, 512])
                    bsel = Buf("selblk")
                    dma(selt[:], sel_d, writes=[bsel], sem_buf=bpar)
                    dma(blkm[:], blkm_d, writes=[bsel], sem_buf=bpar)
                    dma(cst[:, :, :, 0:30], stconv_d[l], writes=[bcst], sem_buf=bpar)
                    fw.barrier()

                    idxkv = sb(ss, "s_idxkv", [128, NS * NPG], I32)
                    ts('dve', idxl[:], idx0[:], float(l * NPHYS * 128), None, ALU.add, None, [bconst], [bidx])
                    ts('dve', idxkv[:], idx0[:], 0.0, None, ALU.add, None, [bconst], [bidx])
                    sample_norm(hs, bhs, a1[:, :, sc1], shift1[:, :, sc1], ss)
                    for ch in range(8):
                        for kc in range(8):
                            mm(ps[1][:, ch * NS:(ch + 1) * NS], win[:, kc, ch * 128:(ch + 1) * 128], hs[:, kc, :],
                               kc == 0, kc == 7, [bwin, bhs], [bps[1]])
                    cp('dve', zs[:], ps[1][:, 0:8 * NS].rearrange("p (c n) -> p c n", n=NS), [bps[1]], [bzs])
                    for g_, c0 in enumerate((1024, 1536, 2048)):
                        for kc in range(8):
                            mm(ps[2 + g_][0:NS, :], hs[:, kc, :], win[:, kc, c0:c0 + 512], kc == 0, kc == 7,
                               [bwin, bhs], [bps[2 + g_]])
                        cp('act', tm[:, g_, :], ps[2 + g_][0:NS, :], [bps[2 + g_]], [btm])
                    for kc in range(8):
                        mm(ps[5][0:NS, 0:8], hs[:, kc, :], win[:, kc, 2560:2568], kc == 0, kc == 7, [bwin, bhs], [bps[5]])
                    tt('dve', lftm[:], ps[5][0:NS, 0:8], bfb[0:NS, :], ALU.add, [bps[5], bpar], [blftm])
                    act(lftm[:], lftm[:], AF.Exp, [blftm], [blftm], scale=-1.0)
                    act(lftm[:], lftm[:], AF.Ln, [blftm, bconst], [blftm], bias=onecol[0:NS, 0:1], scale=1.0)
                    ts('dve', lftm[:], lftm[:], -1.0, None, ALU.mult, None, [blftm], [blftm])
                    dma(ks_o[l], tm[:, 1, :], reads=[btm], sem_buf=bmisc, q='pool')
                    dma(vs_o[l], tm[:, 2, :], reads=[btm], sem_buf=bmisc, q='pool')
                    dma(lfs_o[l], lftm[:], reads=[blftm], sem_buf=bmisc, q='pool')

                    act(w1[:, 0:4, :], zs[:, 0:4, :], AF.Gelu, [bzs], [bw1])
                    chan_stats([w1[:, 2, :], w1[:, 3, :]], bw1, w2, bw2, mt[:], rt[:], bmr, 256, True)
                    for cc in range(2):
                        tt('dve', w3[:, cc, :], w1[:, 2 + cc, :], mt[:], ALU.subtract, [bw1, bmr], [bw3])
                        tt('dve', w3[:, cc, :], w3[:, cc, :], rt[:], ALU.mult, [bw3, bmr], [bw3])
                        ts('dve', w3[:, cc, :], w3[:, cc, :], alngT[:, cc:cc + 1], alnbT[:, cc:cc + 1], ALU.mult, ALU.add,
                           [bw3, bpar], [bw3])
                    dma(chv_o[l], w3[:, 0:2, :], reads=[bw3], sem_buf=bmisc, q='pool')
                    for cc in range(2):
                        ts('dve', w3[:, 2 + cc, :], w3[:, cc, :], ws00T[:, cc:cc + 1], bs0T[:, cc:cc + 1], ALU.mult, ALU.add,
                           [bw3, bpar], [bw3])
                        tt('dve', w3[:, 2 + cc, :], w3[:, 2 + cc, :], w1[:, cc, :], ALU.mult, [bw3, bw1], [bw3])
                    chan_stats([w3[:, 2, :], w3[:, 3, :]], bw3, w2, bw2, mt[:], rt[:], bmr, 256, False)
                    for cc in range(2):
                        tt('dve', ysT[:, cc, :], w3[:, 2 + cc, :], rt[:], ALU.mult, [bw3, bmr], [bys])

                    act(w1[:, 4:6, :], zs[:, 6:8, :], AF.Sigmoid, [bzs], [bw1])
                    for cc in range(2):
                        tt('dve', cst[:, cc, :, 30], zs[:, 4 + cc, :], w1[:, 4 + cc, :], ALU.mult, [bzs, bw1], [bcst])
                    dma(convs_o[l], cst[:, :, :, 1:31], reads=[bcst], sem_buf=bmisc, q='pool')
                    for cc in range(2):
                        for n in range(NS):
                            tt('dve', p31[:, cc, n, :], cst[:, cc, n, :], cwT[:, cc, :], ALU.mult, [bcst, bpar], [bp31])
                        op('dve', lambda e: e.reduce_sum(w3[:, 4 + cc, :], p31[:, cc, :, :], mybir.AxisListType.X),
                           [bp31], [bw3])
                        ts('dve', w3[:, 4 + cc, :], w3[:, 4 + cc, :], cbT[:, cc:cc + 1], None, ALU.add, None, [bw3, bpar], [bw3])
                    chan_stats([w3[:, 4, :], w3[:, 5, :]], bw3, w2, bw2, mt[:], rt[:], bmr, 256, True)
                    for cc in range(2):
                        tt('dve', w3[:, 4 + cc, :], w3[:, 4 + cc, :], mt[:], ALU.subtract, [bw3, bmr], [bw3])
                        tt('dve', w3[:, 4 + cc, :], w3[:, 4 + cc, :], rt[:], ALU.mult, [bw3, bmr], [bw3])
                        act(w3[:, 4 + cc, :], w3[:, 4 + cc, :], AF.Silu, [bw3, bpar], [bw3],
                            scale=clgT[:, cc:cc + 1], bias=clbT[:, cc:cc + 1])
                    chan_stats([w3[:, 4, :], w3[:, 5, :]], bw3, w2, bw2, mt[:], rt[:], bmr, 256, False)
                    for cc in range(2):
                        tt('dve', ysT[:, 2 + cc, :], w3[:, 4 + cc, :], rt[:], ALU.mult, [bw3, bmr], [bys])

                    it = 0
                    for n in range(NS):
                        col = n * NPG
                        mm(ps[7][:, 0:8], selt[:, n, :], lftm[:], True, True, [bsel, blftm], [bps[7]])
                        cp('dve', lfrep[:], ps[7][:, 0:8], [bps[7]], [blfrep])
                        mm(ps[2][:], selt[:, n, :], tm[:, 0, :], True, True, [bsel, btm], [bps[2]])
                        act(qrep[:], ps[2][:], AF.Identity, [bps[2]], [bqrep], scale=0.125, bias=0.0)
                        mm(ps[3][:], selt[:, n, :], tm[:, 1, :], True, True, [bsel, btm], [bps[3]])
                        cp('act', krep[:], ps[3][:], [bps[3]], [bkrep])
                        mm(ps[4][:], selt[:, n, :], tm[:, 2, :], True, True, [bsel, btm], [bps[4]])
                        cp('act', vrep[:], ps[4][:], [bps[4]], [bvrep])
                        op('pool', lambda e: e.memset(gs[:], 0.0), [], [bgs])
                        for pg in range(NPG):
                            idma(LF[:, pg, :], cf_d, idxl[:, col + pg:col + pg + 1], [bidx], [bLF], bLF)
                        for pg in range(NPG):
                            mm(ps[7][:, 16:24], tri_f[:], LF[:, pg, :], True, True, [bconst, bLF], [bps[7]])
                            mm(ps[7][:, 32:40], ones_f[:], LF[:, pg, :], True, True, [bconst, bLF], [bps[7]])
                            tt('dve', cumi[:, pg, :], ps[7][:, 16:24], gs[:], ALU.add, [bps[7], bgs], [bcumi])
                            tt('dve', gs[:], ps[7][:, 32:40], gs[:], ALU.add, [bps[7], bgs], [bgs])
                        tt('dve', gs[:], gs[:], lfrep[:], ALU.add, [bgs, blfrep], [bgs])
                        for pg in range(NPG):
                            tt('dve', cumi[:, pg, :], gs[:], cumi[:, pg, :], ALU.subtract, [bgs, bcumi], [bcumi])
                        cp('dve', cumi[:, NPG, :], bigm[:], [bconst], [bcumi])
                        for pg in range(NPG + 1):
                            sl = it % 4
                            it += 1
                            if pg < NPG:
                                idma(KVp[sl][:], ckv_d[l], idxkv[:, col + pg:col + pg + 1], [bidx], [bKV[sl]], bKV[sl])
                                k_ap, bk_, v_ap, bv_ = KVp[sl][:, 0:512], bKV[sl], KVp[sl][:, 512:1024], bKV[sl]
                            else:
                                k_ap, bk_, v_ap, bv_ = krep[:], bkrep, vrep[:], bvrep
                            tt('dve', prod[:], k_ap, qrep[:], ALU.mult, [bk_, bqrep], [bprod])
                            op('dve', lambda e: e.reduce_sum(S_[:, pg, :], prod[:].rearrange("p (h d) -> p h d", d=64),
                                                             mybir.AxisListType.X), [bprod], [bS])
                            tt('dve', S_[:, pg, :], S_[:, pg, :], cumi[:, pg, :], ALU.add, [bS, bcumi], [bS])
                            act(Pm[:, pg, :], S_[:, pg, :], AF.Exp, [bS], [bPm])
                            mm(ps[5][0:8, :], Pm[:, pg, :], v_ap, pg == 0, pg == NPG, [bPm, bv_], [bps[5]])
                        op('dve', lambda e: e.reduce_sum(prod[:, 0:8], Pm[:].rearrange("p g h -> p h g"),
                                                         mybir.AxisListType.X), [bPm], [bprod])
                        mm(ps[7][0:8, 48:49], prod[:, 0:8], ones_f[:, 0:1], True, True, [bprod, bconst], [bps[7]])
                        op('dve', lambda e: e.reciprocal(den[0:8, 0:1], ps[7][0:8, 48:49]), [bps[7]], [bden])
                        stt('dve', Om[:], ps[5][0:8, :], den[0:8, 0:1], blkm[:], ALU.mult, ALU.mult,
                            [bps[5], bden, bsel], [bOm])
                        for c in range(4):
                            mm(ps[6][:, 16 + c:17 + c], Om[:, c * 128:(c + 1) * 128], ones_f[0:8, 0:1], True, True,
                               [bOm, bconst], [bps[6]])
                        cp('dve', ycs[:, :, n], ps[6][:, 16:20], [bps[6]], [bycs])
                    chan_stats([ycs[:, c, :] for c in range(4)], bycs, w2, bw2, mt[:], rt[:], bmr, 512, False)
                    for c in range(4):
                        tt('dve', ysT[:, 4 + c, :], ycs[:, c, :], rt[:], ALU.mult, [bycs, bmr], [bys])

                    for dc in range(8):
                        for kc in range(8):
                            mm(ps[1][:, dc * NS:(dc + 1) * NS], wout[:, kc, dc * 128:(dc + 1) * 128], ysT[:, kc, :],
                               kc == 0, kc == 7, [bwout, bys], [bps[1]])
                    tt('dve', w1[:], ps[1][:, 0:8 * NS].rearrange("p (c n) -> p c n", n=NS), gate1[:, :, sc1], ALU.mult,
                       [bps[1], bmod], [bw1])
                    tt('dve', xs[:], xs[:], w1[:], ALU.add, [bxs, bw1], [bxs])
                    fw.barrier()

            def sample_phase2(wup, bwup, wdn, bwdn):
                with ExitStack() as ss:
                    hs = sb(ss, "t_hs", [128, 8, NS], BF16); bhs = Buf("t_hs")
                    raw_s = sb(ss, "t_raw", [128, 44, NS]); braw_s = Buf("t_raw")
                    sff = sb(ss, "t_sff", [128, 44, NS, 2]); bsff = Buf("t_sff")
                    fo = sb(ss, "t_fo", [128, 44, NS, 2]); bfo = Buf("t_fo")
                    up = sb(ss, "t_up", [128, 44, NS]); bup = Buf("t_up")
                    tmp = sb(ss, "t_tmp", [128, 44, NS]); btmp_ = Buf("t_tmp")
                    hm = sb(ss, "t_hm", [128, 22, NS], BF16); bhm = Buf("t_hm")
                    w1 = sb(ss, "t_w1", [128, 8, NS]); bw1 = Buf("t_w1")
                    dma(sff[:], sffn_d[l], writes=[bsff], sem_buf=bpar)
                    fw.barrier()
                    sample_norm(hs, bhs, a2[:, :, sc1], shift2[:, :, sc1], ss)
                    for ch in range(44):
                        k_ = 1 + (ch // 22)
                        cc_ = ch % 22
                        for kc in range(8):
                            mm(ps[k_][:, cc_ * NS:(cc_ + 1) * NS], wup[:, kc, ch * 128:(ch + 1) * 128], hs[:, kc, :],
                               kc == 0, kc == 7, [bwup, bhs], [bps[k_]])
                    for hf in range(2):
                        cp('dve', raw_s[:, hf * 22:(hf + 1) * 22, :],
                           ps[1 + hf][:, 0:22 * NS].rearrange("p (c n) -> p c n", n=NS), [bps[1 + hf]], [braw_s])
                    for n in range(NS):
                        tt('dve', up[:, :, n], sff[:, :, n, 0], fwT[:, :, 0], ALU.mult, [bsff, bpar], [bup])
                        tt('dve', tmp[:, :, n], sff[:, :, n, 1], fwT[:, :, 1], ALU.mult, [bsff, bpar], [btmp_])
                        tt('dve', up[:, :, n], up[:, :, n], tmp[:, :, n], ALU.add, [bup, btmp_], [bup])
                        tt('dve', tmp[:, :, n], raw_s[:, :, n], fwT[:, :, 2], ALU.mult, [braw_s, bpar], [btmp_])
                        tt('dve', up[:, :, n], up[:, :, n], tmp[:, :, n], ALU.add, [bup, btmp_], [bup])
                        tt('dve', up[:, :, n], up[:, :, n], fbT[:], ALU.add, [bup, bpar], [bup])
                        cp('dve', fo[:, :, n, 0], sff[:, :, n, 1], [bsff], [bfo])
                        cp('dve', fo[:, :, n, 1], raw_s[:, :, n], [braw_s], [bfo])
                    dma(ffns_o[l], fo[:], reads=[bfo], sem_buf=bmisc, q='pool')
                    act(tmp[:, 0:22, :], up[:, 0:22, :], AF.Silu, [bup], [btmp_])
                    tt('dve', hm[:], tmp[:, 0:22, :], up[:, 22:44, :], ALU.mult, [btmp_, bup], [bhm])
                    for dc in range(8):
                        for j in range(22):
                            mm(ps[3][:, dc * NS:(dc + 1) * NS], wdn[:, j, dc * 128:(dc + 1) * 128], hm[:, j, :],
                               j == 0, j == 21, [bwdn, bhm], [bps[3]])
                    tt('dve', w1[:], ps[3][:, 0:8 * NS].rearrange("p (c n) -> p c n", n=NS), gate2[:, :, sc1], ALU.mult,
                       [bps[3], bmod], [bw1])
                    tt('dve', xs[:], xs[:], w1[:], ALU.add, [bxs, bw1], [bxs])
                    if last:
                        sqs_ = sb(ss, "t_sq", [128, 8, NS], BF16); bsqs_ = Buf("t_sq")
                        rs_ = sb(ss, "t_rs", [128, NS]); brs_ = Buf("t_rs")
                        for c in range(8):
                            act(sqs_[:, c, :], xs[:, c, :], AF.Square, [bxs], [bsqs_])
                        for c in range(8):
                            mm(ps[0][:, 0:NS], ones_bf[:], sqs_[:, c, :], c == 0, c == 7, [bconst, bsqs_], [bps[0]])
                        ts('dve', rs_[:], ps[0][:, 0:NS], 1.0 / D, EPS, ALU.mult, ALU.add, [bps[0]], [brs_])
                        rsqrt_inplace(rs_[:], brs_)
                        for c in range(8):
                            stt('dve', w1[:, c, :], xs[:, c, :], fgT[:, c:c + 1], rs_[:], ALU.mult, ALU.mult,
                                [bxs, bpar, brs_], [bw1])
                        dma(ysT_o, w1[:], reads=[bw1], sem_buf=bmisc, q='pool')
                    fw.barrier()

            with ExitStack() as big:
                KT = sb(big, "KT", [128, 4, T], BF16); bKT = Buf("KT")
                Vst = sb(big, "Vst", [128, NKB, 512], BF16); bVst = Buf("Vst")
                win = sb(big, "win", [128, 8, INW], BF16); bwin = Buf("win")
                wout = sb(big, "wout", [128, 8, D], BF16); bwout = Buf("wout")
                with ExitStack() as st:
                    stg = [sb(st, "stgB%d" % i, [128, INW]) for i in range(4)]
                    NSTG = 4
                    engs = ['dve', 'pool']
                    n = 0
                    for kc in range(8):
                        s_ = n % NSTG
                        dma(stg[s_][:, 0:INW], w_in_d[l, kc * 128:(kc + 1) * 128, :], writes=[bstg[s_]])
                        cp(engs[n % 2], win[:, kc, :], stg[s_][:, 0:INW], [bstg[s_]], [bwin])
                        n += 1
                    for kc in range(8):
                        s_ = n % NSTG
                        dma(stg[s_][:, 0:D], w_out_d[l, kc * 128:(kc + 1) * 128, :], writes=[bstg[s_]])
                        ts(engs[n % 2], wout[:, kc, :], stg[s_][:, 0:D], mixgT[:, kc:kc + 1], None, ALU.mult, None,
                           [bstg[s_], bpar], [bwout])
                        n += 1
                    fw.barrier()
                    stop(2)
                sample_phase1(win, bwin, wout, bwout)

                with ExitStack() as sc:
                    hT = sb(sc, "hT", [128, 8, 512], BF16); bhT = Buf("hT")
                    qTp = sb(sc, "qTp", [128, 2, 4, 512], BF16); bqT = Buf("qTp")
                    yT = sb(sc, "yT", [128, 8, 512], BF16)
                    byA, byB, byC = Buf("yA"), Buf("yB"), Buf("yC")
                    glu = sb(sc, "glu", [128, 2, 542]); bglu = Buf("glu")
                    cacc = sb(sc, "cacc", [128, 2, 512]); bcacc = Buf("cacc")
                    za = sb(sc, "za", [128, 512]); bza = Buf("za")
                    tA = sb(sc, "tA", [128, 256]); btA = Buf("tA")
                    tA2 = sb(sc, "tA2", [128, 256]); btA2 = Buf("tA2")
                    vln = sb(sc, "vln", [128, 256], BF16); bvln = Buf("vln")
                    yan = sb(sc, "yan", [128, 256], BF16); byan = Buf("yan")
                    PT = [sb(sc, "PT%d" % i, [128, 512], BF16) for i in range(2)]
                    bPT = [Buf("PT%d" % i) for i in range(2)]
                    sq = [sb(sc, "sq%d" % i, [128, 512], BF16) for i in range(2)]
                    bsq = [Buf("sq%d" % i) for i in range(2)]
                    st0 = sb(sc, "st0", [128, 512]); bst0 = Buf("st0")
                    st1 = sb(sc, "st1", [128, 512]); bst1 = Buf("st1")
                    st2 = sb(sc, "st2", [128, 512]); bst2 = Buf("st2")
                    kst = sb(sc, "kst", [128, 512]); bkst = Buf("kst")
                    vst = sb(sc, "vst", [128, 512]); bvst = Buf("vst")
                    lft = sb(sc, "lft", [128, 8]); blft = Buf("lft")
                    lfs = sb(sc, "lfs", [128, 8]); blfs = Buf("lfs")
                    sml = sb(sc, "sml", [128, 16]); bsml = Buf("sml")
                    biasT = sb(sc, "biasT", [128, NKB, 8]); bbias = Buf("biasT")
                    alng = sb(sc, "alng", [128, 256]); alnb = sb(sc, "alnb", [128, 256])
                    dma(alng[:], alng_d[l], writes=[bpar])
                    dma(alnb[:], alnb_d[l], writes=[bpar])
                    fw.barrier()

                    op('pool', lambda e: e.memset(gtot[:], 0.0), [], [bgtot])
                    op('pool', lambda e: e.memset(glu[:, :, 0:30], 0.0), [], [bglu])
                    op('pool', lambda e: e.memset(qTp[:], 0.0), [], [bqT])

                    for i in range(NT):
                        t0 = i * 512
                        dma(x[:], x_srcv[:, :, t0:t0 + 512], writes=[bx])
                        rms_rstd(lambda c: x[:, c, :], bx, st0[:], bst0, sq, bsq, 8, 512, D, ps[0], bps[0])
                        for c in range(8):
                            tmp, btmp = (st1, bst1) if c % 2 == 0 else (st2, bst2)
                            stt('dve', tmp[:], x[:, c, :], a1[:, c, 0:1], st0[:], ALU.mult, ALU.mult,
                                [bx, bmod, bst0], [btmp])
                            act(hT[:, c, :], tmp[:], AF.Identity, [btmp, bmod], [bhT], bias=shift1[:, c, 0:1], scale=1.0)

                        stop(31)
                        pi = [0]

                        def proj_fm(col0):
                            k_ = 1 + (pi[0] % 2)
                            pi[0] += 1
                            for kc in range(8):
                                mm(ps[k_][:], win[:, kc, col0:col0 + 128], hT[:, kc, :], kc == 0, kc == 7,
                                   [bwin, bhT], [bps[k_]], lazy=True)
                            return ps[k_], bps[k_]

                        for cc in range(2):
                            pg_, bpg_ = proj_fm(512 + 256 + cc * 128)
                            act(st1[:], pg_[:], AF.Sigmoid, [bpg_], [bst1])
                            stop(311)
                            pa_, bpa_ = proj_fm(512 + cc * 128)
                            tt('dve', glu[:, cc, 30:542], pa_[:], st1[:], ALU.mult, [bpa_, bst1], [bglu])
                            stop(312)
                        for j in range(4):
                            pq_, bpq_ = proj_fm(1024 + j * 128)
                            stop(313)
                            cp('act', qTp[0:64, 0, j, :], pq_[0:64, :], [bpq_], [bqT])
                            cp('act', qTp[64:128, 1, j, :], pq_[64:128, :], [bpq_], [bqT])
                            stop(314)
                            pk_, bpk_ = proj_fm(1536 + j * 128)
                            stop(315)
                            cp('dve', KT[:, j, t0:t0 + 512], pk_[:], [bpk_], [bKT])
                            stop(316)
                            stop(320 + j)

                        stop(32)
                        for s in range(4):
                            kb = 4 * i + s
                            tok = slice(s * 128, (s + 1) * 128)

                            def proj_tm(col0, ncol, k_):
                                for kc in range(8):
                                    mm(ps[k_][:, 0:ncol], hT[:, kc, tok], win[:, kc, col0:col0 + ncol], kc == 0, kc == 7,
                                       [bwin, bhT], [bps[k_]], lazy=True)
                                return ps[k_], bps[k_]

                            pk_, bpk_ = proj_tm(1536, 512, 3)
                            stop(3301)
                            cp('act', kst[:], pk_[:], [bpk_], [bkst])
                            stop(3302)
                            dma(k_o[l, t0 + s * 128:t0 + (s + 1) * 128, :], kst[:], reads=[bkst])
                            stop(331)
                            pv_, bpv_ = proj_tm(2048, 512, 4)
                            cp('act', vst[:], pv_[:], [bpv_], [bvst])
                            cp('dve', Vst[:, kb, :], vst[:], [bvst], [bVst])
                            dma(v_o[l, t0 + s * 128:t0 + (s + 1) * 128, :], vst[:], reads=[bvst])
                            stop(332)
                            pf_, bpf_ = proj_tm(2560, 8, 5)
                            tt('dve', lft[:], pf_[:, 0:8], bfb[:], ALU.add, [bpf_, bpar], [blft])
                            act(lft[:], lft[:], AF.Exp, [blft], [blft], scale=-1.0)
                            act(lft[:], lft[:], AF.Ln, [blft, bconst], [blft], bias=onecol[:, 0:1], scale=1.0)
                            ts('dve', lfs[:], lft[:], -1.0, None, ALU.mult, None, [blft], [blfs])
                            dma(lf_o[l, t0 + s * 128:t0 + (s + 1) * 128, :], lfs[:], reads=[blfs], q='pool')
                            stop(333)
                            mm(ps[5][:, 16:24], tri_f[:], lfs[:], True, True, [bconst, blfs], [bps[5]])
                            mm(ps[5][:, 32:40], ones_f[:], lfs[:], True, True, [bconst, blfs], [bps[5]])
                            tt('dve', cum[:, kb, :], ps[5][:, 16:24], gtot[:], ALU.add, [bps[5], bgtot], [bcum])
                            tt('dve', gtot[:], ps[5][:, 32:40], gtot[:], ALU.add, [bps[5], bgtot], [bgtot])
                            stop(334)

                            pa_, bpa_ = proj_tm(0, 512, 6)
                            act(za[:], pa_[:], AF.Gelu, [bpa_], [bza])
                            stop(335)
                            op('dve', lambda e: e.bn_stats(sml[:, 0:6], za[:, 256:512]), [bza], [bsml])
                            op('dve', lambda e: e.bn_aggr(sml[:, 8:10], sml[:, 0:6]), [bsml], [bsml])
                            ts('dve', sml[:, 9:10], sml[:, 9:10], EPS, None, ALU.add, None, [bsml], [bsml])
                            rsqrt_inplace(sml[:, 9:10], bsml)
                            stop(336)
                            ts('dve', tA[:], za[:, 256:512], sml[:, 8:9], sml[:, 9:10], ALU.subtract, ALU.mult,
                               [bza, bsml], [btA])
                            tt('dve', tA[:], tA[:], alng[:], ALU.mult, [btA, bpar], [btA])
                            tt('dve', vln[:], tA[:], alnb[:], ALU.add, [btA, bpar], [bvln])
                            stop(337)
                            for h in range(4):
                                mm(ps[7][:, h * 64:(h + 1) * 64], wsT_bf[:, h, :], vln[:, h * 64:(h + 1) * 64], True, True,
                                   [bpar, bvln], [bps[7]])
                            for h in range(4):
                                stt('dve', tA2[:, h * 64:(h + 1) * 64], ps[7][:, h * 64:(h + 1) * 64], bsT[:, h:h + 1],
                                    za[:, h * 64:(h + 1) * 64], ALU.add, ALU.mult, [bps[7], bpar, bza], [btA2])
                            tt('dve', tA[:], tA2[:], tA2[:], ALU.mult, [btA2], [btA])
                            op('dve', lambda e: e.reduce_sum(sml[:, 12:13], tA[:], mybir.AxisListType.X), [btA], [bsml])
                            ts('dve', sml[:, 12:13], sml[:, 12:13], 1.0 / 256, EPS, ALU.mult, ALU.add, [bsml], [bsml])
                            rsqrt_inplace(sml[:, 12:13], bsml)
                            stop(338)
                            ts('dve', yan[:], tA2[:], sml[:, 12:13], None, ALU.mult, None, [btA2, bsml], [byan])
                            stop(339)
                            for cA in range(2):
                                mm(ps[7][:, 256 + cA * 128:256 + (cA + 1) * 128], yan[:, cA * 128:(cA + 1) * 128], ident_bf[:],
                                   True, True, [byan, bconst], [bps[7]])
                                cp('act', yT[:, cA, tok], ps[7][:, 256 + cA * 128:256 + (cA + 1) * 128], [bps[7]], [byA])

                        stop(33)
                        for cc in range(2):
                            ts('dve', cacc[:, cc, :], glu[:, cc, 0:512], cwT[:, cc, 0:1], None, ALU.mult, None,
                               [bglu, bpar], [bcacc])
                            for j in range(1, 31):
                                stt('dve', cacc[:, cc, :], glu[:, cc, j:j + 512], cwT[:, cc, j:j + 1], cacc[:, cc, :],
                                    ALU.mult, ALU.add, [bglu, bpar, bcacc], [bcacc])
                            act(cacc[:, cc, :], cacc[:, cc, :], AF.Identity, [bcacc, bpar], [bcacc],
                                bias=cbT[:, cc:cc + 1], scale=1.0)
                        if i == NT - 1:
                            dma(convp_o[l].rearrange("(c p) j -> p c j", p=128), glu[:, :, 512:542], reads=[bglu], sem_buf=bmisc, q='pool')
                        for cc in range(2):
                            mm(ps[1][:], ones_f[:], cacc[:, cc, :], cc == 0, cc == 1, [bconst, bcacc], [bps[1]])
                        for cc in range(2):
                            act(st1[:], cacc[:, cc, :], AF.Square, [bcacc], [bst1])
                            mm(ps[2][:], ones_f[:], st1[:], cc == 0, cc == 1, [bconst, bst1], [bps[2]])
                        ts('dve', st0[:], ps[1][:], 1.0 / 256, None, ALU.mult, None, [bps[1]], [bst0])
                        tt('dve', st2[:], st0[:], st0[:], ALU.mult, [bst0], [bst2])
                        stt('dve', st2[:], ps[2][:], 1.0 / 256, st2[:], ALU.mult, ALU.subtract, [bps[2], bst2], [bst2])
                        ts('dve', st2[:], st2[:], EPS, None, ALU.add, None, [bst2], [bst2])
                        rsqrt_inplace(st2[:], bst2)
                        for cc in range(2):
                            tt('dve', cacc[:, cc, :], cacc[:, cc, :], st0[:], ALU.subtract, [bcacc, bst0], [bcacc])
                            tt('dve', cacc[:, cc, :], cacc[:, cc, :], st2[:], ALU.mult, [bcacc, bst2], [bcacc])
                            act(cacc[:, cc, :], cacc[:, cc, :], AF.Silu, [bcacc, bpar], [bcacc],
                                scale=clgT[:, cc:cc + 1], bias=clbT[:, cc:cc + 1])
                        rms_rstd(lambda c: cacc[:, c, :], bcacc, st1[:], bst1, sq, bsq, 2, 512, 256, ps[1], bps[1])
                        for cc in range(2):
                            tt('dve', yT[:, 2 + cc, :], cacc[:, cc, :], st1[:], ALU.mult, [bcacc, bst1], [byB])
                        for cc in range(2):
                            cp('pool', glu[:, cc, 0:30], glu[:, cc, 512:542], [bglu], [bglu])

                        stop(34)
                        nkb = 4 * i + 4
                        for kb in range(nkb):
                            tt('dve', biasT[:, kb, :], gtot[:], cum[:, kb, :], ALU.subtract, [bgtot, bcum], [bbias])
                        for h in range(8):
                            c, pb = h // 2, 64 * (h % 2)
                            if h % 2 == 0:
                                pO, bpO, pD, bpD = ps[5], bps[5], ps[6], bps[6]
                            else:
                                pO, bpO, pD, bpD = ps[7], bps[7], ps[0], bps[0]

                            def issue_S(kb):
                                j = kb - 4 * i
                                col0 = 0 if j <= 0 else 128 * j
                                ncols = 512 - col0
                                sl_ = kb % 2
                                pS, bpS = ps[3 + sl_], bps[3 + sl_]
                                P_, bP_ = PT[sl_], bPT[sl_]
                                mm(pS[:, 0:ncols], KT[:, c, kb * 128:(kb + 1) * 128], qTp[:, h % 2, c, col0:512],
                                   True, True, [bKT, bqT], [bpS])
                                act(P_[:, 0:ncols], pS[:, 0:ncols], AF.Exp, [bpS, bbias], [bP_],
                                    scale=0.125, bias=biasT[:, kb, h:h + 1])
                                if j >= 0:
                                    tt('pool', P_[:, 0:128], P_[:, 0:128], mask_bf[:], ALU.mult, [bP_, bconst], [bP_])
                                return P_, bP_, col0, ncols

                            cur = issue_S(0)
                            for kb in range(nkb):
                                nxt = issue_S(kb + 1) if kb + 1 < nkb else None
                                P_, bP_, col0, ncols = cur
                                mm(pO[:, col0:512], Vst[:, kb, c * 128:(c + 1) * 128], P_[:, 0:ncols], kb == 0, kb == nkb - 1,
                                   [bVst, bP_], [bpO])
                                mm(pD[:, col0:512], ones_bf[:], P_[:, 0:ncols], kb == 0, kb == nkb - 1,
                                   [bconst, bP_], [bpD])
                                cur = nxt
                            op('dve', lambda e: e.reciprocal(st0[pb:pb + 64, :], pD[pb:pb + 64, :]), [bpD], [bst0])
                            tt('dve', yT[pb:pb + 64, 4 + c, :], pO[pb:pb + 64, :], st0[pb:pb + 64, :], ALU.mult,
                               [bpO, bst0], [byC])
                        rms_rstd(lambda c: yT[:, 4 + c, :], byC, st1[:], bst1, sq, bsq, 4, 512, 512, ps[0], bps[0])
                        for c in range(4):
                            tt('dve', yT[:, 4 + c, :], yT[:, 4 + c, :], st1[:], ALU.mult, [byC, bst1], [byC])

                        stop(35)
                        for dc in range(8):
                            k_ = 1 + (dc % 2)
                            for kc in range(8):
                                mm(ps[k_][:], wout[:, kc, dc * 128:(dc + 1) * 128], yT[:, kc, :], kc == 0, kc == 7,
                                   [bwout, byA, byB, byC], [bps[k_]], lazy=True)
                            stt('dve', x[:, dc, :], ps[k_][:], gate1[:, dc, 0:1], x[:, dc, :], ALU.mult, ALU.add,
                                [bps[k_], bmod, bx], [bx])
                        dma(xsv[:, :, t0:t0 + 512], x[:], reads=[bx])
                        stop(3)
                    fw.barrier()

            with ExitStack() as big:
                wup = sb(big, "wup", [128, 8, 2 * DFF], BF16); bwup = Buf("wup")
                wdn = sb(big, "wdn", [128, 22, D], BF16); bwdn = Buf("wdn")
                with ExitStack() as st:
                    stg = [sb(st, "stgC%d" % i, [128, DFF]) for i in range(3)]
                    NSTG = 3
                    engs = ['dve', 'pool']
                    n = 0
                    for kc in range(8):
                        for hf in range(2):
                            s_ = n % NSTG
                            dma(stg[s_][:, 0:DFF], w_up_d[l, kc * 128:(kc + 1) * 128, hf * DFF:(hf + 1) * DFF],
                                writes=[bstg[s_]])
                            cp(engs[n % 2], wup[:, kc, hf * DFF:(hf + 1) * DFF], stg[s_][:, 0:DFF], [bstg[s_]], [bwup])
                            n += 1
                    for j in range(22):
                        s_ = n % NSTG
                        dma(stg[s_][:, 0:D], w_down_d[l, j * 128:(j + 1) * 128, :], writes=[bstg[s_]])
                        cp(engs[n % 2], wdn[:, j, :], stg[s_][:, 0:D], [bstg[s_]], [bwdn])
                        n += 1
                    fw.barrier()
                    stop(4)
                sample_phase2(wup, bwup, wdn, bwdn)

                with ExitStack() as sc:
                    hT = sb(sc, "hT2", [128, 8, 512], BF16); bhT = Buf("hT2")
                    hmid = sb(sc, "hmid", [128, 22, 512], BF16); bhmid = Buf("hmid")
                    raw = [sb(sc, "raw%d" % i, [128, 514]) for i in range(4)]
                    braw = [Buf("raw%d" % i) for i in range(4)]
                    acc = [sb(sc, "acc%d" % i, [128, 512]) for i in range(4)]
                    bacc = [Buf("acc%d" % i) for i in range(4)]
                    sq = [sb(sc, "sq2%d" % i, [128, 512], BF16) for i in range(2)]
                    bsq = [Buf("sq2%d" % i) for i in range(2)]
                    st0 = sb(sc, "st02", [128, 512]); bst0 = Buf("st02")
                    st1, bst1, st2, bst2 = acc[0], bacc[0], acc[1], bacc[1]

                    op('pool', lambda e: e.memset(halo[:], 0.0), [], [bhalo])
                    for i in range(NT):
                        t0 = i * 512
                        dma(x[:], xsv[:, :, t0:t0 + 512], writes=[bx])
                        rms_rstd(lambda c: x[:, c, :], bx, st0[:], bst0, sq, bsq, 8, 512, D, ps[0], bps[0])
                        for c in range(8):
                            tmp, btmp = (st1, bst1) if c % 2 == 0 else (st2, bst2)
                            stt('dve', tmp[:], x[:, c, :], a2[:, c, 0:1], st0[:], ALU.mult, ALU.mult,
                                [bx, bmod, bst0], [btmp])
                            act(hT[:, c, :], tmp[:], AF.Identity, [btmp, bmod], [bhT], bias=shift2[:, c, 0:1], scale=1.0)
                        for j in range(22):
                            for hf in range(2):
                                ch = hf * 22 + j
                                k_ = 1 + hf + 2 * (j % 2)
                                for kc in range(8):
                                    mm(ps[k_][:], wup[:, kc, ch * 128:(ch + 1) * 128], hT[:, kc, :], kc == 0, kc == 7,
                                       [bwup, bhT], [bps[k_]], lazy=True)
                                r_, br_ = raw[hf + 2 * (j % 2)], braw[hf + 2 * (j % 2)]
                                a_, ba_ = acc[hf + 2 * (j % 2)], bacc[hf + 2 * (j % 2)]
                                cp('act', r_[:, 2:514], ps[k_][:], [bps[k_]], [br_])
                                cp('pool', r_[:, 0:2], halo[:, ch, :], [bhalo], [br_])
                                act(a_[:], ps[k_][:], AF.Identity, [bps[k_], bpar], [ba_],
                                    scale=fwT[:, ch, 2:3], bias=fbT[:, ch:ch + 1])
                                stt('dve', a_[:], r_[:, 0:512], fwT[:, ch, 0:1], a_[:], ALU.mult, ALU.add,
                                    [br_, bpar, ba_], [ba_])
                                stt('dve', a_[:], r_[:, 1:513], fwT[:, ch, 1:2], a_[:], ALU.mult, ALU.add,
                                    [br_, bpar, ba_], [ba_])
                                cp('pool', halo[:, ch, :], r_[:, 512:514], [br_], [bhalo])
                            ag_, bag_ = acc[2 * (j % 2)], bacc[2 * (j % 2)]
                            au_, bau_ = acc[1 + 2 * (j % 2)], bacc[1 + 2 * (j % 2)]
                            act(ag_[:], ag_[:], AF.Silu, [bag_], [bag_])
                            tt('dve', hmid[:, j, :], ag_[:], au_[:], ALU.mult, [bag_, bau_], [bhmid])
                        if i == NT - 1:
                            dma(ffnp_o[l].rearrange("(c p) k -> p c k", p=128), halo[:], reads=[bhalo], sem_buf=bmisc, q='pool')
                        for dc in range(8):
                            k_ = 5 + (dc % 2)
                            for j in range(22):
                                mm(ps[k_][:], wdn[:, j, dc * 128:(dc + 1) * 128], hmid[:, j, :], j == 0, j == 21,
                                   [bwdn, bhmid], [bps[k_]], lazy=True)
                            stt('dve', x[:, dc, :], ps[k_][:], gate2[:, dc, 0:1], x[:, dc, :], ALU.mult, ALU.add,
                                [bps[k_], bmod, bx], [bx])
                        if not last:
                            dma(xsv[:, :, t0:t0 + 512], x[:], reads=[bx])
                        else:
                            rms_rstd(lambda c: x[:, c, :], bx, st0[:], bst0, sq, bsq, 8, 512, D, ps[0], bps[0])
                            for c in range(8):
                                stt('dve', x[:, c, :], x[:, c, :], fgT[:, c:c + 1], st0[:], ALU.mult, ALU.mult,
                                    [bx, bpar, bst0], [bx])
                            dma(yT_o.rearrange("(c p) t -> p c t", p=128)[:, :, t0:t0 + 512], x[:], reads=[bx])
                    fw.barrier()
        fw.barrier()
    fw.barrier()
    fw.close()
    return nc


_NC_CACHE = {}


def _host_inputs(inp):
    f = np.float32
    A = lambda a: np.ascontiguousarray(np.asarray(a), dtype=f)
    shared = {}
    shared["w_ada"] = A(inp["w_ada"])
    shared["badaT"] = A(np.asarray(inp["b_ada"]).reshape(NL, 48, 128).transpose(0, 2, 1))
    rep = lambda g: A(np.broadcast_to(np.asarray(g).reshape(NL, 8, 128).transpose(0, 2, 1)[..., None], (NL, 128, 8, 1 + NS)))
    shared["g1r"] = rep(inp["norm1_g"])
    shared["g2r"] = rep(inp["norm2_g"])
    shared["mixgT"] = A(np.asarray(inp["mix_g"]).reshape(NL, 8, 128).transpose(0, 2, 1))
    shared["fgT"] = A(np.asarray(inp["final_g"]).reshape(8, 128).T)
    for k in ("w_in", "w_out", "w_up", "w_down"):
        shared[k] = A(inp[k])
    shared["bfb"] = A(np.broadcast_to(np.asarray(inp["b_forget"])[:, None, :], (NL, 128, 8)))
    shared["alng"] = A(np.broadcast_to(np.asarray(inp["a_ln_g"])[:, None, :], (NL, 128, 256)))
    shared["alnb"] = A(np.broadcast_to(np.asarray(inp["a_ln_b"])[:, None, :], (NL, 128, 256)))
    shared["wsT"] = A(np.asarray(inp["w_s"]).transpose(0, 3, 1, 2))
    shared["bsT"] = A(np.asarray(inp["b_s"]).transpose(0, 2, 1))
    shared["cwT"] = A(np.asarray(inp["conv_w"]).transpose(0, 2, 1).reshape(NL, 2, 128, 31).transpose(0, 2, 1, 3))
    cm = lambda a: A(np.asarray(a).reshape(NL, 2, 128).transpose(0, 2, 1))
    shared["cbT"] = cm(inp["conv_b"]); shared["clgT"] = cm(inp["conv_ln_g"]); shared["clbT"] = cm(inp["conv_ln_b"])
    shared["fwT"] = A(np.asarray(inp["ffn_conv_w"]).transpose(0, 2, 1).reshape(NL, 44, 128, 3).transpose(0, 2, 1, 3))
    shared["fbT"] = A(np.asarray(inp["ffn_conv_b"]).reshape(NL, 44, 128).transpose(0, 2, 1))
    shared["tri"] = np.triu(np.ones((128, 128), f))
    shared["ident"] = np.eye(128, dtype=f)
    shared["alngT"] = cm(inp["a_ln_g"]); shared["alnbT"] = cm(inp["a_ln_b"])
    rep64 = lambda a: A(np.repeat(np.asarray(a), 64, axis=1).reshape(NL, 2, 128).transpose(0, 2, 1))
    shared["ws00T"] = rep64(np.asarray(inp["w_s"])[:, :, 0, 0])
    shared["bs0T"] = rep64(np.asarray(inp["b_s"])[:, :, 0])
    sel = np.zeros((NS, NS, 128), f)
    for n in range(NS):
        sel[n, n, :] = 1.0
    shared["sel"] = sel
    bigm = np.full((128, 8), -30000.0, f); bigm[0, :] = 0.0
    shared["bigm"] = bigm
    blk = np.zeros((8, 512), f)
    for h in range(8):
        blk[h, h * 64:(h + 1) * 64] = 1.0
    shared["blkm"] = blk
    shared["iot"] = np.ascontiguousarray(np.broadcast_to(np.arange(128, dtype=np.int32)[:, None], (128, NS * NPG)))
    for l_ in range(NL):
        shared["cache_kv%d" % l_] = np.concatenate([A(inp["cache_k"][l_]).reshape(NPHYS * 128, 512),
                                                    A(inp["cache_v"][l_]).reshape(NPHYS * 128, 512)], axis=1)
    shared["cache_f"] = A(inp["cache_logf"]).reshape(NL * NPHYS * 128, 8)
    xsm = np.asarray(inp["x_sample"]); stc = np.asarray(inp["state_conv"]); stf = np.asarray(inp["state_ffn_conv"])
    ptab = np.asarray(inp["page_table"]).astype(np.int32)
    xp = np.asarray(inp["x_prompt"]); cp_ = np.asarray(inp["c_prompt"]); cs = np.asarray(inp["c_sample"])
    maps = []
    for c in range(NCORES):
        b = c // 2
        m = dict(shared)
        m["xT"] = A(xp[b].T)
        m["cT"] = A(np.concatenate([cp_[b:b + 1], cs[NS * c:NS * (c + 1)]], 0).T)
        sl = slice(NS * c, NS * (c + 1))
        m["xsT0"] = A(xsm[sl, 0, :].T.reshape(8, 128, NS).transpose(1, 0, 2))
        m["ptb"] = np.ascontiguousarray(np.broadcast_to(ptab[sl].reshape(1, NS * NPG), (128, NS * NPG)))
        m["stconvT"] = A(stc[:, sl].transpose(0, 3, 1, 2).reshape(NL, 2, 128, NS, 30).transpose(0, 2, 1, 3, 4))
        m["sffnT"] = A(stf[:, sl].transpose(0, 3, 1, 2).reshape(NL, 44, 128, NS, 2).transpose(0, 2, 1, 3, 4))
        maps.append(m)
    return maps


def kernel(**inp):
    if "nc" not in _NC_CACHE:
        _NC_CACHE["nc"] = build_program()
    nc = _NC_CACHE["nc"]
    maps = _host_inputs(inp)
    res = run_bass_kernel_spmd(nc, maps, core_ids=list(range(NCORES))).results
    f = np.float32
    B = 4
    y_prompt = np.stack([res[2 * b]["yT"].T for b in range(B)]).astype(f)
    k_p = np.stack([res[2 * b]["k_o"] for b in range(B)], 1).reshape(NL, B, T, 8, 64).astype(f)
    v_p = np.stack([res[2 * b]["v_o"] for b in range(B)], 1).reshape(NL, B, T, 8, 64).astype(f)
    lf_p = np.stack([res[2 * b]["lf_o"] for b in range(B)], 1).astype(f)
    conv_p = np.stack([res[2 * b]["convp"].transpose(0, 2, 1) for b in range(B)], 1).astype(f)
    ffn_p = np.stack([res[2 * b]["ffnp"].transpose(0, 2, 1) for b in range(B)], 1).astype(f)
    cat = lambda fn, ax: np.concatenate([fn(res[c]) for c in range(NCORES)], ax).astype(f)
    y_s = cat(lambda r: r["ysT"].transpose(2, 1, 0).reshape(NS, 1, D), 0)
    k_s = cat(lambda r: r["ks_o"].reshape(NL, NS, 1, 8, 64), 1)
    v_s = cat(lambda r: r["vs_o"].reshape(NL, NS, 1, 8, 64), 1)
    lf_s = cat(lambda r: r["lfs_o"].reshape(NL, NS, 1, 8), 1)
    conv_s = cat(lambda r: r["convs"].transpose(0, 3, 4, 2, 1).reshape(NL, NS, 30, 256), 1)
    ffn_s = cat(lambda r: r["ffns"].transpose(0, 3, 4, 2, 1).reshape(NL, NS, 2, 2 * DFF), 1)
    chv_s = cat(lambda r: r["chv"].transpose(0, 3, 2, 1).reshape(NL, NS, 1, 256), 1)
    return (y_prompt, y_s, k_p, v_p, lf_p, conv_p, ffn_p, k_s, v_s, lf_s, conv_s, ffn_s, chv_s)
```
